# Optimizing a Trainium2 kernel written in Bass

```python
import math
import jax, jax.numpy as jnp
from jax import lax
import numpy as np

D_MODEL = 1024
BATCH = 8
SEQ = 2048
DEPTH = 4
DEC_BATCH = 128
DEC_SEQ = 8
PAST_LEN = 8192
PAGE_SIZE = 128

N_MIXERS = 2
N_SSM_LAYERS = (DEPTH + 1) // 2
N_SWA_LAYERS = DEPTH // 2
SSM_GROUP = 16
N_GROUPS = D_MODEL // SSM_GROUP
STATE_DIM = 64
SCAN_CHUNK = 128
HEAD_DIM = 64
N_HEADS = D_MODEL // HEAD_DIM
N_KV_HEADS = 4
KV_REP = N_HEADS // N_KV_HEADS
WINDOW = 128
ATTN_BLOCK = WINDOW
ROT_DIM = HEAD_DIM // 4
ROPE_THETA = 500000.0
ATTN_SCALE = HEAD_DIM ** -0.5
D_FF = ((8 * D_MODEL + 3 * 256 - 1) // (3 * 256)) * 256
NORM_EPS = 1e-6
NEG_INF = -1e30

kernel_name = "hybrid_s5_swa_sink_decoder_step"


def rms_norm(x, g):
    xf = x.astype(jnp.float32)
    y = xf * lax.rsqrt(jnp.mean(xf * xf, axis=-1, keepdims=True) + NORM_EPS)
    return (y * g.astype(jnp.float32)).astype(x.dtype)


def rotary(x, pos):
    half = ROT_DIM // 2
    inv_freq = ROPE_THETA ** (-jnp.arange(half, dtype=jnp.float32) * 2.0 / ROT_DIM)
    ang = pos[:, None] * inv_freq[None, :]
    cos = jnp.cos(ang)[:, None, :]
    sin = jnp.sin(ang)[:, None, :]
    xf = x.astype(jnp.float32)
    x1 = xf[..., :half]
    x2 = xf[..., half:ROT_DIM]
    out = jnp.concatenate([x1 * cos - x2 * sin, x2 * cos + x1 * sin, xf[..., ROT_DIM:]], axis=-1)
    return out.astype(x.dtype)


def cmul(ar, ai, br, bi):
    return ar * br - ai * bi, ar * bi + ai * br


def ssm_discretize(a_re, a_im, log_dt, b_re, b_im):
    f32 = jnp.float32
    a_re = a_re.astype(f32)
    a_im = a_im.astype(f32)
    dt = jnp.exp(log_dt.astype(f32))[:, None]
    mag = jnp.exp(a_re * dt)
    lam_re = mag * jnp.cos(a_im * dt)
    lam_im = mag * jnp.sin(a_im * dt)
    den = a_re * a_re + a_im * a_im
    nr = lam_re - 1.0
    ni = lam_im
    f_re = (nr * a_re + ni * a_im) / den
    f_im = (ni * a_re - nr * a_im) / den
    b_re = b_re.astype(f32)
    b_im = b_im.astype(f32)
    bb_re = f_re[..., None] * b_re - f_im[..., None] * b_im
    bb_im = f_re[..., None] * b_im + f_im[..., None] * b_re
    return lam_re, lam_im, bb_re, bb_im


def ssm_combine(e1, e2):
    a1r, a1i, b1r, b1i = e1
    a2r, a2i, b2r, b2i = e2
    ar, ai = cmul(a1r, a1i, a2r, a2i)
    br, bi = cmul(a2r, a2i, b1r, b1i)
    return ar, ai, br + b2r, bi + b2i


def s5_mixer(x_n, h0_re, h0_im, a_re, a_im, log_dt, b_re, b_im, c_re, c_im, d, w_glu):
    f32 = jnp.float32
    bsz, seq_len, _ = x_n.shape
    lam_re, lam_im, bb_re, bb_im = ssm_discretize(a_re, a_im, log_dt, b_re, b_im)
    c_re = c_re.astype(f32)
    c_im = c_im.astype(f32)
    d_g = d.astype(f32).reshape(N_GROUPS, SSM_GROUP)
    t_blk = SCAN_CHUNK if seq_len % SCAN_CHUNK == 0 else seq_len
    n_blk = seq_len // t_blk
    u = x_n.astype(f32).reshape(bsz, n_blk, t_blk, N_GROUPS, SSM_GROUP).swapaxes(0, 1)

    def block_step(h, u_c):
        hr, hi = h
        br = jnp.einsum('btgc,gpc->btgp', u_c, bb_re)
        bi = jnp.einsum('btgc,gpc->btgp', u_c, bb_im)
        ir, ii = cmul(lam_re, lam_im, hr, hi)
        br = br.at[:, 0].add(ir)
        bi = bi.at[:, 0].add(ii)
        ar = jnp.broadcast_to(lam_re, br.shape)
        ai = jnp.broadcast_to(lam_im, bi.shape)
        _, _, hs_re, hs_im = lax.associative_scan(ssm_combine, (ar, ai, br, bi), axis=1)
        y = (jnp.einsum('btgp,gcp->btgc', hs_re, c_re)
             - jnp.einsum('btgp,gcp->btgc', hs_im, c_im)) + d_g * u_c
        return (hs_re[:, -1], hs_im[:, -1]), y

    (hr, hi), ys = lax.scan(block_step, (h0_re.astype(f32), h0_im.astype(f32)), u)
    y = ys.swapaxes(0, 1).reshape(bsz, seq_len, D_MODEL)
    z = jax.nn.gelu(y).astype(x_n.dtype)
    g = z @ w_glu
    out = g[..., :D_MODEL] * jax.nn.sigmoid(g[..., D_MODEL:])
    return out.astype(x_n.dtype), hr, hi


def qkv_project(x_n, w_qkv, q_gain, k_gain, pos):
    bsz, seq_len, _ = x_n.shape
    qkv = x_n @ w_qkv
    nq = N_HEADS * HEAD_DIM
    nk = N_KV_HEADS * HEAD_DIM
    q = qkv[..., :nq].reshape(bsz, seq_len, N_HEADS, HEAD_DIM)
    k = qkv[..., nq:nq + nk].reshape(bsz, seq_len, N_KV_HEADS, HEAD_DIM)
    v = qkv[..., nq + nk:].reshape(bsz, seq_len, N_KV_HEADS, HEAD_DIM)
    q = rotary(rms_norm(q, q_gain), pos)
    k = rotary(rms_norm(k, k_gain), pos)
    return q, k, v


def sink_softmax(scores, mask, sinks):
    s = jnp.where(mask, scores, NEG_INF)
    sk = sinks.astype(jnp.float32).reshape(N_KV_HEADS, KV_REP, 1, 1)
    m = jnp.maximum(jnp.max(s, axis=-1, keepdims=True), sk)
    p = jnp.exp(s - m)
    denom = jnp.sum(p, axis=-1, keepdims=True) + jnp.exp(sk - m)
    return p / denom


def swa_prompt(x_n, w_qkv, q_gain, k_gain, sinks, w_o):
    f32 = jnp.float32
    bsz, seq_len, _ = x_n.shape
    pos = jnp.arange(seq_len, dtype=f32)
    q, k, v = qkv_project(x_n, w_qkv, q_gain, k_gain, pos)
    nb = seq_len // ATTN_BLOCK
    qb = q.reshape(bsz, nb, ATTN_BLOCK, N_KV_HEADS, KV_REP, HEAD_DIM).astype(f32)
    kb = k.reshape(bsz, nb, ATTN_BLOCK, N_KV_HEADS, HEAD_DIM).astype(f32)
    vb = v.reshape(bsz, nb, ATTN_BLOCK, N_KV_HEADS, HEAD_DIM).astype(f32)
    kc = jnp.concatenate([jnp.concatenate([jnp.zeros_like(kb[:, :1]), kb[:, :-1]], axis=1), kb], axis=2)
    vc = jnp.concatenate([jnp.concatenate([jnp.zeros_like(vb[:, :1]), vb[:, :-1]], axis=1), vb], axis=2)
    scores = jnp.einsum('bnqkrd,bnskd->bnkrqs', qb, kc) * ATTN_SCALE
    iq = jnp.arange(ATTN_BLOCK)[:, None]
    js = jnp.arange(2 * ATTN_BLOCK)[None, :]
    diff = ATTN_BLOCK + iq - js
    blk = jnp.arange(nb)[:, None, None]
    mask = (diff >= 0) & (diff < WINDOW) & ((blk - 1) * ATTN_BLOCK + js >= 0)
    p = sink_softmax(scores, mask[None, :, None, None], sinks)
    o = jnp.einsum('bnkrqs,bnskd->bnqkrd', p, vc).reshape(bsz, seq_len, N_HEADS * HEAD_DIM)
    out = o.astype(x_n.dtype) @ w_o
    buf = min(WINDOW, seq_len)
    return out, k[:, seq_len - buf:], v[:, seq_len - buf:]


def swa_sample(x_n, cache_k, cache_v, w_qkv, q_gain, k_gain, sinks, w_o):
    f32 = jnp.float32
    bsz, s_len, _ = x_n.shape
    w_buf = cache_k.shape[1]
    pos = PAST_LEN + jnp.arange(s_len, dtype=f32)
    q, k, v = qkv_project(x_n, w_qkv, q_gain, k_gain, pos)
    k_all = jnp.concatenate([cache_k.astype(k.dtype), k], axis=1)
    v_all = jnp.concatenate([cache_v.astype(v.dtype), v], axis=1)
    kpos = PAST_LEN - w_buf + jnp.arange(w_buf + s_len)
    qpos = PAST_LEN + jnp.arange(s_len)
    diff = qpos[:, None] - kpos[None, :]
    mask = (diff >= 0) & (diff < WINDOW)
    qg = q.reshape(bsz, s_len, N_KV_HEADS, KV_REP, HEAD_DIM).astype(f32)
    scores = jnp.einsum('bqkrd,bskd->bkrqs', qg, k_all.astype(f32)) * ATTN_SCALE
    p = sink_softmax(scores, mask, sinks)
    o = jnp.einsum('bkrqs,bskd->bqkrd', p, v_all.astype(f32)).reshape(bsz, s_len, N_HEADS * HEAD_DIM)
    out = o.astype(x_n.dtype) @ w_o
    return out, k_all[:, -w_buf:], v_all[:, -w_buf:]


def swiglu(x_n, w_gate_up, w_down):
    h = x_n @ w_gate_up
    return (jax.nn.silu(h[..., :D_FF]) * h[..., D_FF:]) @ w_down


def setup_inputs(seed: int = 0) -> dict:
    key = jax.random.key(seed)
    ks = jax.random.split(key, 24)
    f32 = jnp.float32
    nrm = lambda k, shape, s=1.0: (jax.random.normal(k, shape, f32) * s)
    w_buf = min(WINDOW, PAST_LEN)
    a_im_base = jnp.pi * jnp.arange(STATE_DIM, dtype=f32)
    return {
        "x_prompt": nrm(ks[0], (BATCH, SEQ, D_MODEL)),
        "x_sample": nrm(ks[1], (DEC_BATCH, DEC_SEQ, D_MODEL)),
        "state_ssm_re": nrm(ks[2], (N_SSM_LAYERS, DEC_BATCH, N_GROUPS, STATE_DIM), 0.3),
        "state_ssm_im": nrm(ks[3], (N_SSM_LAYERS, DEC_BATCH, N_GROUPS, STATE_DIM), 0.3),
        "cache_swa_k": nrm(ks[4], (N_SWA_LAYERS, DEC_BATCH, w_buf, N_KV_HEADS, HEAD_DIM)),
        "cache_swa_v": nrm(ks[5], (N_SWA_LAYERS, DEC_BATCH, w_buf, N_KV_HEADS, HEAD_DIM)),
        "norm_mix": 1.0 + nrm(ks[6], (DEPTH, D_MODEL), 0.02),
        "norm_ffn": 1.0 + nrm(ks[7], (DEPTH, D_MODEL), 0.02),
        "ssm_a_re": -0.5 + nrm(ks[8], (N_SSM_LAYERS, N_GROUPS, STATE_DIM), 0.01),
        "ssm_a_im": a_im_base + nrm(ks[9], (N_SSM_LAYERS, N_GROUPS, STATE_DIM), 0.01),
        "ssm_log_dt": jax.random.uniform(ks[10], (N_SSM_LAYERS, N_GROUPS), f32, math.log(1e-3), math.log(1e-1)),
        "ssm_b_re": nrm(ks[11], (N_SSM_LAYERS, N_GROUPS, STATE_DIM, SSM_GROUP), (2 * SSM_GROUP) ** -0.5),
        "ssm_b_im": nrm(ks[12], (N_SSM_LAYERS, N_GROUPS, STATE_DIM, SSM_GROUP), (2 * SSM_GROUP) ** -0.5),
        "ssm_c_re": nrm(ks[13], (N_SSM_LAYERS, N_GROUPS, SSM_GROUP, STATE_DIM), STATE_DIM ** -0.5),
        "ssm_c_im": nrm(ks[14], (N_SSM_LAYERS, N_GROUPS, SSM_GROUP, STATE_DIM), STATE_DIM ** -0.5),
        "ssm_d": nrm(ks[15], (N_SSM_LAYERS, D_MODEL)),
        "ssm_w_glu": nrm(ks[16], (N_SSM_LAYERS, D_MODEL, 2 * D_MODEL), D_MODEL ** -0.5),
        "attn_w_qkv": nrm(ks[17], (N_SWA_LAYERS, D_MODEL, (N_HEADS + 2 * N_KV_HEADS) * HEAD_DIM), D_MODEL ** -0.5),
        "attn_q_norm": 1.0 + nrm(ks[18], (N_SWA_LAYERS, HEAD_DIM), 0.02),
        "attn_k_norm": 1.0 + nrm(ks[19], (N_SWA_LAYERS, HEAD_DIM), 0.02),
        "attn_sinks": nrm(ks[20], (N_SWA_LAYERS, N_HEADS)),
        "attn_w_o": nrm(ks[21], (N_SWA_LAYERS, N_HEADS * HEAD_DIM, D_MODEL), (N_HEADS * HEAD_DIM) ** -0.5),
        "ffn_w_gate_up": nrm(ks[22], (DEPTH, D_MODEL, 2 * D_FF), D_MODEL ** -0.5),
        "ffn_w_down": nrm(ks[23], (DEPTH, D_FF, D_MODEL), D_FF ** -0.5),
    }


def reference(x_prompt, x_sample, state_ssm_re, state_ssm_im, cache_swa_k, cache_swa_v,
              norm_mix, norm_ffn, ssm_a_re, ssm_a_im, ssm_log_dt, ssm_b_re, ssm_b_im,
              ssm_c_re, ssm_c_im, ssm_d, ssm_w_glu, attn_w_qkv, attn_q_norm, attn_k_norm,
              attn_sinks, attn_w_o, ffn_w_gate_up, ffn_w_down):
    yp = x_prompt
    ys = x_sample
    p_re, p_im, p_k, p_v = [], [], [], []
    s_re, s_im, s_k, s_v = [], [], [], []
    h0 = jnp.zeros((x_prompt.shape[0], N_GROUPS, STATE_DIM), jnp.float32)
    for i in range(DEPTH):
        j = i // N_MIXERS
        xp_n = rms_norm(yp, norm_mix[i])
        xs_n = rms_norm(ys, norm_mix[i])
        if i % N_MIXERS == 0:
            ssm_w = (ssm_a_re[j], ssm_a_im[j], ssm_log_dt[j], ssm_b_re[j], ssm_b_im[j],
                     ssm_c_re[j], ssm_c_im[j], ssm_d[j], ssm_w_glu[j])
            op, hr, hi = s5_mixer(xp_n, h0, h0, *ssm_w)
            osm, sr, si = s5_mixer(xs_n, state_ssm_re[j], state_ssm_im[j], *ssm_w)
            p_re.append(hr)
            p_im.append(hi)
            s_re.append(sr)
            s_im.append(si)
        else:
            attn_w = (attn_w_qkv[j], attn_q_norm[j], attn_k_norm[j], attn_sinks[j], attn_w_o[j])
            op, kp, vp = swa_prompt(xp_n, *attn_w)
            osm, ks_, vs_ = swa_sample(xs_n, cache_swa_k[j], cache_swa_v[j], *attn_w)
            p_k.append(kp)
            p_v.append(vp)
            s_k.append(ks_)
            s_v.append(vs_)
        yp = yp + op
        ys = ys + osm
        yp = yp + swiglu(rms_norm(yp, norm_ffn[i]), ffn_w_gate_up[i], ffn_w_down[i])
        ys = ys + swiglu(rms_norm(ys, norm_ffn[i]), ffn_w_gate_up[i], ffn_w_down[i])
    p_state_re = jnp.stack(p_re)
    p_state_im = jnp.stack(p_im)
    p_cache_k = jnp.stack(p_k)
    p_cache_v = jnp.stack(p_v)
    s_state_re = jnp.stack(s_re)
    s_state_im = jnp.stack(s_im)
    s_cache_k = jnp.stack(s_k)
    s_cache_v = jnp.stack(s_v)
    return (yp, ys, p_state_re, p_state_im, p_cache_k, p_cache_v, s_state_re, s_state_im, s_cache_k, s_cache_v)
```

```python
import numpy as np
from contextlib import ExitStack
import concourse.bass as bass
import concourse.mybir as mybir
from concourse.bass_utils import run_bass_kernel_spmd

F32 = mybir.dt.float32
BF16 = mybir.dt.bfloat16
I32 = mybir.dt.int32
ALU = mybir.AluOpType
AF = mybir.ActivationFunctionType
AX = mybir.AxisListType

D = 1024
NTOK = 2176
DFF = 2816
NFT = 22
EPS = 1e-6
DEPTH = 4
PI = float(np.pi)


class _Probe:
    def __getattr__(self, name):
        def f(*a, **k):
            return (name, a, k)
        return f


def _free_size(ap):
    try:
        sh = ap.shape
        n = 1
        for d in sh[1:]:
            n *= int(d)
        return n
    except Exception:
        return 256


class Prog:
    ENGS = ("pe", "dve", "act", "pool", "sp")

    def __init__(self, nc, stack, schedule=True):
        self.nc = nc
        self.stack = stack
        self.schedule = schedule
        self.esem = {e: stack.enter_context(nc.semaphore("es_" + e)) for e in self.ENGS}
        self.dsem = {}
        self.sems = {e: self.esem[e] for e in self.ENGS}
        self.segments = []
        self.seg = []
        self.lastw = {}
        self.readers = {}
        self.nid = 0
        self.handoff = 0.15
        self.cscale = {"pe": 1.0, "dve": 1.0, "act": 1.0, "pool": 1.0, "sp": 1.0}

    def _dma_sem(self, key):
        if key not in self.dsem:
            s = self.stack.enter_context(self.nc.semaphore("ds_%d" % len(self.dsem)))
            self.dsem[key] = [s, 0]
            self.sems[("d", key)] = s
        return self.dsem[key]

    def _cost(self, eng, fn, dma):
        try:
            name, a, k = fn(_Probe())
        except Exception:
            return 0.5
        if dma:
            out = k.get("out", a[0] if a else None)
            n = _free_size(out) if out is not None else 1024
            return 2.5 + n * 128 * 2 / 150e3
        if eng == "pe":
            if name == "transpose":
                return 0.11
            rhs = k.get("rhs", a[2] if len(a) > 2 else None)
            n = _free_size(rhs) if rhs is not None else 128
            return 0.03 + n * 0.00043
        out = k.get("out", a[0] if a else None)
        n = _free_size(out) if out is not None else 256
        if eng == "act":
            return 0.22 + n * 0.00083 + (0.1 if k.get("accum_out") is not None else 0.0)
        if eng == "dve":
            return 0.07 + n * 0.001
        return 0.15 + n * 0.0016

    def op(self, eng, fn, reads=(), writes=(), dma=False, dkey=None):
        preds = set()
        for k in reads:
            w = self.lastw.get(k)
            if w is not None:
                preds.add(w)
        for k in writes:
            w = self.lastw.get(k)
            if w is not None:
                preds.add(w)
            for r in self.readers.get(k, ()):
                preds.add(r)
        node = dict(id=self.nid, eng=eng, fn=fn, dma=dma, preds=preds, reads=tuple(reads), writes=tuple(writes),
                    dk=(dkey if dkey is not None else (writes[0] if writes else reads[0])) if dma else None,
                    cost=self._cost(eng, fn, dma) * self.cscale[eng], ho=self.handoff)
        self.nid += 1
        preds.discard(node["id"])
        for k in reads:
            self.readers.setdefault(k, []).append(node["id"])
        for k in writes:
            self.lastw[k] = node["id"]
            self.readers[k] = []
        self.seg.append(node)
        return node["id"]

    def barrier(self):
        self.segments.append(self.seg)
        self.seg = []
        self.lastw = {}
        self.readers = {}

    def _sched(self, seg, clock):
        import heapq
        if not self.schedule:
            return list(seg)
        byid = {n["id"]: n for n in seg}
        succ = {n["id"]: [] for n in seg}
        indeg = {}
        for n in seg:
            ps = [p for p in n["preds"] if p in byid]
            indeg[n["id"]] = len(ps)
            for p in ps:
                succ[p].append(n["id"])
        fin = {}
        ready = {e: [] for e in self.ENGS}
        dready = {}
        for n in seg:
            if indeg[n["id"]] == 0:
                dready[n["id"]] = 0.0
                heapq.heappush(ready[n["eng"]], (0.0, n["id"]))
        t0 = max(clock.values()) if clock else 0.0
        for e in self.ENGS:
            clock[e] = t0
        order = []
        nleft = len(seg)
        while nleft:
            best = None
            for e in self.ENGS:
                h = ready[e]
                if not h:
                    continue
                dr, i = h[0]
                st = max(clock[e], t0 + dr)
                if best is None or (st, i) < (best[0], best[1]):
                    best = (st, i, e)
            st, i, e = best
            heapq.heappop(ready[e])
            n = byid[i]
            if n["dma"]:
                clock[e] = st + 0.3
                fin[i] = st + n["cost"]
            else:
                clock[e] = st + n["cost"]
                fin[i] = clock[e] + n["ho"]
            order.append(n)
            nleft -= 1
            for s_ in succ[i]:
                dready[s_] = max(dready.get(s_, 0.0), fin[i] - t0)
                indeg[s_] -= 1
                if indeg[s_] == 0:
                    heapq.heappush(ready[byid[s_]["eng"]], (dready[s_], s_))
        return order

    def emit(self):
        nc = self.nc
        if self.seg:
            self.segments.append(self.seg)
            self.seg = []
        lists = {e: [] for e in self.ENGS}
        cnt = {e: 0 for e in self.ENGS}
        waited = {e: {} for e in self.ENGS}
        tok = {}
        clock = {}

        def need(eng, dep, waits, pe_waw):
            semkey, val = dep
            if semkey == eng and eng == "pe" and pe_waw:
                return
            if waited[eng].get(semkey, 0) >= val:
                return
            waited[eng][semkey] = val
            waits.append((self.sems[semkey], val))

        for seg in self.segments:
            for n in self._sched(seg, clock):
                eng = n["eng"]
                waits = []
                wset = set(n["writes"])
                for p in sorted(n["preds"]):
                    t = tok.get(p)
                    if t is None:
                        continue
                    need(eng, t, waits, pe_waw=not n["dma"])
                if n["dma"]:
                    ds = self._dma_sem(n["dk"])
                    ds[1] += 16
                    tok[n["id"]] = (("d", n["dk"]), ds[1])
                    inc = (ds[0], 16)
                else:
                    cnt[eng] += 1
                    tok[n["id"]] = (eng, cnt[eng])
                    inc = (self.esem[eng], 1)
                lists[eng].append((waits, n["fn"], inc))
            for e in self.ENGS:
                waits = []
                for o in self.ENGS:
                    if cnt[o] > waited[e].get(o, 0):
                        waited[e][o] = cnt[o]
                        waits.append((self.esem[o], cnt[o]))
                for k, (s, c) in self.dsem.items():
                    sk = ("d", k)
                    if c > waited[e].get(sk, 0):
                        waited[e][sk] = c
                        waits.append((s, c))
                if waits:
                    lists[e].append((waits, None, None))
            tok = {}
        with nc.Block() as block:
            def run(e, name):
                for waits, fn, inc in lists[name]:
                    for s, v in waits:
                        e.wait_ge(s, v)
                    if fn is not None:
                        fn(e).then_inc(inc[0], inc[1])

            @block.tensor
            def _(e):
                run(e, "pe")

            @block.vector
            def _(e):
                run(e, "dve")

            @block.scalar
            def _(e):
                run(e, "act")

            @block.gpsimd
            def _(e):
                run(e, "pool")

            @block.sync
            def _(e):
                run(e, "sp")


def build(cfg=None):
    cfg = cfg or {}
    do_mix = cfg.get("mix", True)
    do_ffn = cfg.get("ffn", True)
    nlayers = cfg.get("layers", DEPTH)
    nc = bass.Bass("TRN2", target_bir_lowering=False)

    def din(name, shape):
        return nc.dram_tensor(name, list(shape), F32, kind="ExternalInput").ap()

    def dout(name, shape):
        return nc.dram_tensor(name, list(shape), F32, kind="ExternalOutput").ap()

    xp_d = din("xp", [2048, D])
    xs_d = din("xs", [128, D])
    norm_mix_d = din("norm_mix", [4, D])
    norm_ffn_d = din("norm_ffn", [4, D])
    wgu_d = din("ffn_w_gate_up", [4, D, 2 * DFF])
    wdn_d = din("ffn_w_down", [4, DFF, D])
    ident_d = din("ident", [128, 128])
    cst_d = din("cst", [128, 554])
    a_re_d = din("ssm_a_re", [2, 64, 64])
    a_im_d = din("ssm_a_im", [2, 64, 64])
    ldt_d = din("ssm_log_dt", [2, 64])
    b_re_d = din("ssm_b_re", [2, 64, 64, 16])
    b_im_d = din("ssm_b_im", [2, 64, 64, 16])
    c_re_d = din("ssm_c_re", [2, 64, 16, 64])
    c_im_d = din("ssm_c_im", [2, 64, 16, 64])
    ssm_d_d = din("ssm_d", [2, D])
    wglu_d = din("ssm_w_glu", [2, D, 2 * D])
    st_re_d = din("st_re", [2, 16, 64, 64])
    wqkv_d = din("attn_w_qkv", [2, D, 1536])
    qnorm_d = din("attn_q_norm", [2, 64])
    knorm_d = din("attn_k_norm", [2, 64])
    sinks_d = din("attn_sinks", [2, 16])
    wo_d = din("attn_w_o", [2, D, D])
    ck_d = din("ck", [2, 16, 128, 256])
    cv_d = din("cv", [2, 16, 128, 256])
    rope_d = din("rope", [128, 17, 16])
    msk_d = din("msk", [128, 384 + 2048])
    pck_d = dout("pck", [2, 128, 256])
    pcv_d = dout("pcv", [2, 128, 256])
    sck_d = dout("sck", [2, 16, 128, 256])
    scv_d = dout("scv", [2, 16, 128, 256])
    st_im_d = din("st_im", [2, 16, 64, 64])
    pst_re_d = dout("pst_re", [2, 64, 64])
    pst_im_d = dout("pst_im", [2, 64, 64])
    sst_re_d = dout("sst_re", [2, 16, 64, 64])
    sst_im_d = dout("sst_im", [2, 16, 64, 64])
    yp_d = dout("yp", [2048, D])
    ys_d = dout("ys", [128, D])

    with ExitStack() as st:
        P = Prog(nc, st, schedule=cfg.get("sched", True))
        CS_ONE = {"pe": 1.0, "dve": 1.0, "act": 1.0, "pool": 1.0, "sp": 1.0}
        CS_ATT = dict(CS_ONE, act=2.0, dve=2.0, pool=2.0)
        CS_ATT.update(cfg.get("cs_att", {}))
        CS_S5 = dict(CS_ONE)
        CS_S5.update(cfg.get("cs_s5", {}))
        st.enter_context(nc.allow_non_contiguous_dma(reason="small strided parameter loads"))

        NM = {}
        uid = [0]
        used = [0]

        def sb(name, shape, dt, stack=st):
            uid[0] += 1
            full = "%s_u%d" % (name, uid[0])
            NM[full] = name
            t = stack.enter_context(nc.sbuf_tensor(full, list(shape), dt))
            if cfg.get("memdbg"):
                n = 1
                for d_ in shape[1:]:
                    n *= d_
                n *= 2 if dt == BF16 else 4
                used[0] += n
                stack.callback(lambda n=n: used.__setitem__(0, used[0] - n))
                print("SB %-10s %7d B  total %7d" % (name, n, used[0]))
            return t

        XP = sb("XP", [128, 2, 8, D], F32)
        XS = sb("XS", [128, D], F32)
        featT = sb("featT", [128, 8, NTOK], BF16)
        gains = sb("gains", [128, 8, 8], F32)
        identb = sb("identb", [128, 128], BF16)
        identf = sb("identf", [128, 128], F32)
        ssq = sb("ssq", [128, 17], F32)
        cst = sb("cst_sb", [128, 554], F32)
        maskM = cst[:, 281:409]
        rope = sb("rope_sb", [128, 17, 16], F32)
        msk = sb("msk_sb", [128, 384], BF16)
        mskS = sb("mskS", [128, 16, 128], BF16)
        maskrep = sb("maskrep", [128, 128], BF16)
        selb = sb("selb", [128, 16], BF16)
        rstd = sb("rstd", [128, 17], F32)
        PSALL = st.enter_context(nc.psum_tensor("psall", [128, 4096], F32))
        PS = [PSALL[:, i * 512:(i + 1) * 512] for i in range(8)]

        V = {}

        def alloc_scr(stack):
            SCR = sb("SCR", [128, 17408], BF16, stack)
            V["SCR"] = SCR
            V["XSJ"] = SCR[:, 0:16384].rearrange("p (a t d) -> p a t d", a=2, t=8)
            V["XSS"] = SCR[:, 16384:17408]
            V["ACTB"] = SCR[:, :].rearrange("p (f n) -> p f n", f=8)

        def Xrow(rg):
            return XP[:, rg // 8, rg % 8, :] if rg < 16 else XS[:, :]

        def XSrow(rg):
            return V["XSJ"][:, rg // 8, rg % 8, :] if rg < 16 else V["XSS"]

        def feat_cols(t3, kt_or_f, rg):
            if rg < 16:
                jt, t = rg // 8, rg % 8
                return t3[:, kt_or_f, jt * 1024:(jt + 1) * 1024].rearrange("p (j t) -> p t j", t=8)[:, t, :]
            return t3[:, kt_or_f, 2048:2176]

        P.op("sp", lambda e: e.dma_start(out=XP[:].rearrange("p a t d -> p a (t d)"),
                                         in_=xp_d.rearrange("(a j t) d -> j a (t d)", a=2, t=8)),
             writes=["XP"], dma=True)
        P.op("sp", lambda e: e.dma_start(out=XS[:], in_=xs_d), writes=["XS"], dma=True)
        P.op("sp", lambda e: e.dma_start(out=identf[:], in_=ident_d), writes=["identf"], dma=True)
        P.op("pool", lambda e: e.dma_start(out=identb[:], in_=ident_d), writes=["identb"], dma=True)
        P.op("sp", lambda e: e.dma_start(out=cst[:], in_=cst_d), writes=["cst"], dma=True)
        P.op("sp", lambda e: e.dma_start(out=rope[:], in_=rope_d), writes=["rope"], dma=True)
        P.op("pool", lambda e: e.dma_start(out=msk[:], in_=msk_d[:, 0:384]), writes=["masks"], dma=True)
        P.op("pool", lambda e: e.dma_start(out=mskS[:].rearrange("p s q -> p (s q)"), in_=msk_d[:, 384:2432]), writes=["masks"], dma=True)
        P.op("pool", lambda e: e.dma_start(out=maskrep[:], in_=cst_d[:, 410:538]), writes=["maskrep"], dma=True)
        P.op("pool", lambda e: e.dma_start(out=selb[:], in_=cst_d[:, 538:554]), writes=["selb"], dma=True)
        for i in range(4):
            P.op("sp", lambda e, i=i: e.dma_start(out=gains[:, i, :], in_=norm_mix_d[i].rearrange("(k p) -> p k", p=128)),
                 writes=["gains"], dma=True)
            P.op("sp", lambda e, i=i: e.dma_start(out=gains[:, 4 + i, :], in_=norm_ffn_d[i].rearrange("(k p) -> p k", p=128)),
                 writes=["gains"], dma=True)

        bank_rr = [0]

        def next_bank(lo=0, hi=8):
            b = lo + bank_rr[0] % (hi - lo)
            bank_rr[0] += 1
            return b

        def norm(gi, to_feat=True, gmajor=False):
            XSJ, XSS = V["XSJ"], V["XSS"]
            Xr, XSr = Xrow, XSrow
            if gmajor:
                XSG = XSJ.rearrange("p a t (g c) -> p a g t c", c=16)
                XSGm = V["SCR"][:, 0:16384].rearrange("p (a g t c) -> p a g t c", a=2, g=64, t=8)

                def XSr(rg):
                    return XSGm[:, rg // 8, :, rg % 8, :] if rg < 16 else XSS

                def Xr(rg):
                    return XP[:, rg // 8, rg % 8, :].rearrange("p (g c) -> p g c", c=16) if rg < 16 else XS[:, :]
            parts = [list(range(0, 8)), list(range(8, 16)), [16]]
            for pi, part in enumerate(parts):
                r0, r1 = part[0], part[-1] + 1
                sk, rk = ("ssq", pi), ("rstd", pi)
                for rg in part:
                    xk = "XP" if rg < 16 else "XS"
                    P.op("act", lambda e, rg=rg: e.activation(out=XSr(rg), in_=Xr(rg), func=AF.Square, accum_out=ssq[:, rg:rg + 1]),
                         reads=[xk], writes=[("xs", rg), sk])
                P.op("act", lambda e, r0=r0, r1=r1: e.activation(out=rstd[:, r0:r1], in_=ssq[:, r0:r1], func=AF.Sqrt, scale=1.0 / D, bias=EPS),
                     reads=[sk], writes=[rk])
                P.op("dve", lambda e, r0=r0, r1=r1: e.reciprocal(out=rstd[:, r0:r1], in_=rstd[:, r0:r1]), reads=[rk], writes=[rk])
                for rg in part:
                    xk = "XP" if rg < 16 else "XS"
                    if rg % 2 == 0:
                        P.op("dve", lambda e, rg=rg: e.tensor_scalar(out=XSr(rg), in0=Xr(rg), scalar1=rstd[:, rg:rg + 1], scalar2=None, op0=ALU.mult),
                             reads=[xk, rk], writes=[("xs", rg)])
                    else:
                        P.op("act", lambda e, rg=rg: e.activation(out=XSr(rg), in_=Xr(rg), func=AF.Copy, scale=rstd[:, rg:rg + 1]),
                             reads=[xk, rk], writes=[("xs", rg)])
                if not to_feat:
                    continue
                if pi < 2:
                    jt = pi
                    for kt in range(8):
                        b = next_bank(0, 4)
                        psb = PS[b][:].bitcast(BF16)
                        for t in range(8):
                            P.op("pe", lambda e, t=t, psb=psb, jt=jt, kt=kt: e.transpose(
                                out=psb[:, t * 128:(t + 1) * 128], in_=XSJ[:, jt, t, kt * 128:(kt + 1) * 128], identity=identb[:]),
                                 reads=[("xs", jt * 8 + t), "identb"], writes=[("ps", b)])
                        P.op("dve", lambda e, psb=psb, jt=jt, kt=kt: e.tensor_scalar(
                            out=featT[:, kt, jt * 1024:(jt + 1) * 1024].rearrange("p (j t) -> p j t", t=8),
                            in0=psb.rearrange("p (t j) -> p j t", t=8),
                            scalar1=gains[:, gi, kt:kt + 1], scalar2=None, op0=ALU.mult),
                             reads=[("ps", b), "gains"], writes=[("featT", kt), ("featTp", kt, pi)])
                else:
                    b = next_bank(0, 4)
                    psb = PS[b][:].bitcast(BF16)
                    for kt in range(8):
                        P.op("pe", lambda e, kt=kt, psb=psb: e.transpose(
                            out=psb[:, kt * 128:(kt + 1) * 128], in_=XSS[:, kt * 128:(kt + 1) * 128], identity=identb[:]),
                             reads=[("xs", 16), "identb"], writes=[("ps", b)])
                    P.op("dve", lambda e, psb=psb: e.tensor_tensor(
                        out=featT[:, :, 2048:2176], in0=psb.rearrange("p (k n) -> p k n", k=8),
                        in1=gains[:, gi, :].unsqueeze(2).to_broadcast([128, 8, 128]), op=ALU.mult),
                         reads=[("ps", b), "gains"], writes=[("featT", k) for k in range(8)] + [("featTp", k, 2) for k in range(8)])

        def ffn(L):
            P.handoff = 0.5
            P.cscale = dict(CS_ONE)
            with ExitStack() as ph:
                alloc_scr(ph)
                ACTB = V["ACTB"]
                wst = [sb("wst%d" % i, [128, 8, 256], BF16, ph) for i in range(2)]
                wd = sb("wd", [128, 8, D], BF16, ph)
                tmp = [sb("ftmp%d" % i, [128, 512], F32, ph) for i in range(2)]
                norm(4 + L)
                XSK = [("xs", rg_) for rg_ in range(17)]
                groups = [(0, 8), (8, 15), (15, 22)]
                it = 0
                for (f0, f1) in groups:
                    for fl, f in enumerate(range(f0, f1)):
                        P.op("pool", lambda e, fl=fl, f=f: e.dma_start(out=wd[:, fl, :], in_=wdn_d[L, f * 128:(f + 1) * 128, :]),
                             writes=[("wd", fl)], dma=True)
                    for fl, f in enumerate(range(f0, f1)):
                        buf = wst[f % 2]
                        bk = ("wstg", f % 2)
                        bk2 = ("wstu", f % 2)
                        P.op("pool", lambda e, buf=buf, f=f: e.dma_start(
                            out=buf[:, :, 0:128], in_=wgu_d[L, :, f * 128:(f + 1) * 128].rearrange("(k p) c -> p k c", p=128)),
                             writes=[bk], dma=True)
                        P.op("pool", lambda e, buf=buf, f=f: e.dma_start(
                            out=buf[:, :, 128:256], in_=wgu_d[L, :, DFF + f * 128:DFF + (f + 1) * 128].rearrange("(k p) c -> p k c", p=128)),
                             writes=[bk2], dma=True)
                        for c0 in range(0, NTOK, 512):
                            n = min(512, NTOK - c0)
                            bg = 2 * (it % 2)
                            bu = bg + 1
                            tp = tmp[it % 2]
                            tk = ("ftmp", it % 2)
                            it += 1
                            for kt in range(8):
                                P.op("pe", lambda e, kt=kt, bg=bg, buf=buf, c0=c0, n=n: e.matmul(
                                    PS[bg][:, 0:n], lhsT=buf[:, kt, 0:128], rhs=featT[:, kt, c0:c0 + n],
                                    start=(kt == 0), stop=(kt == 7)),
                                     reads=[bk, ("featTp", kt, min(c0 // 1024, 2))], writes=[("ps", bg)])
                            for kt in range(8):
                                P.op("pe", lambda e, kt=kt, bu=bu, buf=buf, c0=c0, n=n: e.matmul(
                                    PS[bu][:, 0:n], lhsT=buf[:, kt, 128:256], rhs=featT[:, kt, c0:c0 + n],
                                    start=(kt == 0), stop=(kt == 7)),
                                     reads=[bk2, ("featTp", kt, min(c0 // 1024, 2))], writes=[("ps", bu)])
                            P.op("act", lambda e, bg=bg, tp=tp, n=n: e.activation(out=tp[:, 0:n], in_=PS[bg][:, 0:n], func=AF.Silu),
                                 reads=[("ps", bg)], writes=[tk])
                            P.op("dve", lambda e, bu=bu, tp=tp, n=n, fl=fl, c0=c0: e.tensor_tensor(
                                out=ACTB[:, fl, c0:c0 + n], in0=tp[:, 0:n], in1=PS[bu][:, 0:n], op=ALU.mult),
                                 reads=[tk, ("ps", bu)], writes=[("actb", fl)] + XSK)
                    nfl = f1 - f0
                    for rg in range(17):
                        for half in range(2):
                            b = 4 + next_bank(0, 4)
                            for fl in range(nfl):
                                P.op("pe", lambda e, fl=fl, b=b, rg=rg, half=half: e.matmul(
                                    PS[b][:, :], lhsT=feat_cols(ACTB, fl, rg), rhs=wd[:, fl, half * 512:(half + 1) * 512],
                                    start=(fl == 0), stop=(fl == nfl - 1)),
                                     reads=[("actb", fl), ("wd", fl)], writes=[("ps", b)])
                            xk = "XP" if rg < 16 else "XS"
                            P.op("dve", lambda e, b=b, rg=rg, half=half: e.tensor_tensor(
                                out=Xrow(rg)[:, half * 512:(half + 1) * 512], in0=Xrow(rg)[:, half * 512:(half + 1) * 512],
                                in1=PS[b][:, :], op=ALU.add),
                                 reads=[("ps", b), xk], writes=[xk])
                P.barrier()

        C1 = 6.28125
        C2 = float(2.0 * np.pi - 6.28125)
        INV2PI = float(1.0 / (2.0 * np.pi))

        def dv(fn, r, w):
            P.op("dve", fn, reads=r, writes=w)

        def ac(fn, r, w):
            P.op("act", fn, reads=r, writes=w)

        def gp(fn, r, w):
            P.op("pool", fn, reads=r, writes=w)

        def pe(fn, r, w):
            P.op("pe", fn, reads=r, writes=w)

        def wrap(x, xk, ki, kf, tag):
            dv(lambda e: e.tensor_scalar(out=ki, in0=x, scalar1=INV2PI, scalar2=None, op0=ALU.mult), [xk], [tag + "ki"])
            dv(lambda e: e.tensor_copy(out=kf, in_=ki), [tag + "ki"], [tag + "kf"])
            dv(lambda e: e.scalar_tensor_tensor(out=x, in0=kf, scalar=-C1, in1=x, op0=ALU.mult, op1=ALU.add), [tag + "kf", xk], [xk])
            dv(lambda e: e.scalar_tensor_tensor(out=x, in0=kf, scalar=-C2, in1=x, op0=ALU.mult, op1=ALU.add), [tag + "kf", xk], [xk])
            dv(lambda e: e.tensor_scalar(out=x, in0=x, scalar1=PI, scalar2=-PI, op0=ALU.min, op1=ALU.max), [xk], [xk])

        def sincos(sn, cs, x, xk, snk, csk):
            ac(lambda e: e.activation(out=sn, in_=x, func=AF.Sin), [xk], [snk])
            ac(lambda e: e.activation(out=cs, in_=x, func=AF.Sin, scale=0.5), [xk], [csk])
            dv(lambda e: e.tensor_tensor(out=cs, in0=cs, in1=cs, op=ALU.mult), [csk], [csk])
            dv(lambda e: e.tensor_scalar(out=cs, in0=cs, scalar1=-2.0, scalar2=1.0, op0=ALU.mult, op1=ALU.add), [csk], [csk])

        def s5_layer(L):
            P.handoff = 0.5
            P.cscale = dict(CS_S5)
            jl = L // 2
            Uv = featT[:].rearrange("p k (g n) -> p k g n", n=272)
            with ExitStack() as ph:
                def t128(name, n, dt=F32):
                    return sb(name, [128, n], dt, ph)
                are, aim, ldt, adt, thr, kf0, Rg, Th8, Lre, Lim, nLim, Th128 = [t128("s5_%d" % i, 32) for i in range(12)]
                lre, lim, fre, fim, den, tq1, tq2 = [t128("s5b_%d" % i, 32) for i in range(7)]
                ki0 = t128("s5ki0", 32, I32)
                bbre = sb("bbre", [128, 32, 16], F32, ph)
                bbim = sb("bbim", [128, 32, 16], F32, ph)
                cre = sb("cre", [128, 32, 16], F32, ph)
                cim = sb("cim", [128, 32, 16], F32, ph)
                gdU = sb("gdU", [128, 64], F32, ph)
                dU = sb("dU", [128, 64], F32, ph)
                Pre = sb("Pre", [128, 32], F32, ph)
                Pim = sb("Pim", [128, 32], F32, ph)
                phb = ExitStack()
                bre = sb("bre", [128, 32, 16], F32, phb)
                bim = sb("bim", [128, 32, 16], F32, phb)
                gainPG = sb("gainPG", [128, 32, 16], F32, phb)
                for g2 in range(2):
                    hs_ = slice(g2 * 64, (g2 + 1) * 64)
                    P.op("sp", lambda e, g2=g2, hs_=hs_: e.dma_start(out=are[hs_, :], in_=a_re_d[jl].rearrange("(r t) p -> t p r", t=2)[g2]), writes=[("are", g2)], dma=True, dkey="are_dk")
                    P.op("sp", lambda e, g2=g2, hs_=hs_: e.dma_start(out=aim[hs_, :], in_=a_im_d[jl].rearrange("(r t) p -> t p r", t=2)[g2]), writes=[("aim", g2)], dma=True, dkey="aim_dk")
                    P.op("sp", lambda e, g2=g2, hs_=hs_: e.dma_start(out=ldt[hs_, :], in_=ldt_d[jl:jl + 1, :].rearrange("o (r t) -> o t r", t=2)[:, g2].to_broadcast([64, 32])),
                         writes=[("ldt", g2)], dma=True, dkey="ldt_dk")
                    P.op("sp", lambda e, g2=g2, hs_=hs_: e.dma_start(out=bre[hs_, :, :], in_=b_re_d[jl].rearrange("(r t) p c -> t p r c", t=2)[g2]), writes=[("bre", g2)], dma=True, dkey="bre_dk")
                    P.op("sp", lambda e, g2=g2, hs_=hs_: e.dma_start(out=bim[hs_, :, :], in_=b_im_d[jl].rearrange("(r t) p c -> t p r c", t=2)[g2]), writes=[("bim", g2)], dma=True, dkey="bim_dk")
                    P.op("sp", lambda e, g2=g2, hs_=hs_: e.dma_start(
                        out=gainPG[hs_, :, :], in_=norm_mix_d[L:L + 1, :].rearrange("o (r t c) -> o t r c", t=2, c=16)[:, g2].to_broadcast([64, 32, 16])),
                         writes=[("gainPG", g2)], dma=True, dkey="gainPG_dk")
                AREK, AIMK, LDTK = [("are", 0), ("are", 1)], [("aim", 0), ("aim", 1)], [("ldt", 0), ("ldt", 1)]
                BREK, BIMK, GPGK = [("bre", 0), ("bre", 1)], [("bim", 0), ("bim", 1)], [("gainPG", 0), ("gainPG", 1)]
                for s_ in range(8):
                    P.op("sp", lambda e, s_=s_: e.dma_start(out=gdU[s_ * 16:(s_ + 1) * 16, :], in_=norm_mix_d[L].rearrange("(g c) -> c g", c=16)),
                         writes=[("gdU", s_)], dma=True, dkey="gdU_dk")
                    P.op("sp", lambda e, s_=s_: e.dma_start(out=dU[s_ * 16:(s_ + 1) * 16, :], in_=ssm_d_d[jl].rearrange("(g c) -> c g", c=16)),
                         writes=[("dU", s_)], dma=True, dkey="dU_dk")
                with ExitStack() as phc:
                    cn_re = sb("cn_re", [64, 8, 128], F32, phc)
                    cn_im = sb("cn_im", [64, 8, 128], F32, phc)
                    for (cn, cnk, src) in ((cn_re, "cn_re", c_re_d), (cn_im, "cn_im", c_im_d)):
                        for pl in range(4):
                            for g2 in range(2):
                                P.op("sp", lambda e, cn=cn, src=src, pl=pl, g2=g2: e.dma_start(
                                    out=cn[pl * 16:(pl + 1) * 16, :, g2 * 64:(g2 + 1) * 64],
                                    in_=src[jl].rearrange("(gb l t) c p -> l t c gb p", l=4, t=2)[pl, g2]), writes=[(cnk, pl, g2)], dma=True, dkey=cnk + "_dk")
                    with ExitStack() as ph0:
                        alloc_scr(ph0)
                        XSJ, XSS = V["XSJ"], V["XSS"]
                        xrep = sb("xrep", [128, 8, 128], BF16, ph0)
                        norm(L, to_feat=False, gmajor=True)
                        XSGm = V["SCR"][:, 0:16384].rearrange("p (a g n) -> p a g n", a=2, g=64)
                        for jt in range(2):
                            for kt in range(8):
                                b = next_bank(0, 4)
                                psb = PS[b][:].bitcast(BF16)
                                for gl in range(8):
                                    g = kt * 8 + gl
                                    P.op("pe", lambda e, psb=psb, gl=gl, g=g, jt=jt: e.transpose(
                                        out=psb[:, gl * 128:(gl + 1) * 128], in_=XSGm[:, jt, g, :], identity=identb[:]),
                                         reads=[("xs", jt * 8 + t) for t in range(8)] + ["identb"], writes=[("ps", b)])
                                P.op("act", lambda e, psb=psb, jt=jt, kt=kt: e.activation(
                                    out=Uv[:, kt, :, jt * 128:(jt + 1) * 128], in_=psb.rearrange("p (g n) -> p g n", g=8), func=AF.Copy),
                                     reads=[("ps", b)], writes=[("featT", kt)])
                        for kt in range(8):
                            dv(lambda e, kt=kt: e.tensor_tensor(
                                out=xrep[:].rearrange("p g (s c) -> p g s c", s=8),
                                in0=XSS[:, kt * 128:(kt + 1) * 128].rearrange("p (g c) -> p g c", c=16).unsqueeze(2).to_broadcast([128, 8, 8, 16]),
                                in1=maskrep[:].rearrange("p (s c) -> p s c", s=8).unsqueeze(1).to_broadcast([128, 8, 8, 16]), op=ALU.mult),
                               [("xs", 16), "maskrep"], ["xrep"])
                            b = next_bank(0, 4)
                            for gl in range(8):
                                pe(lambda e, b=b, gl=gl: e.matmul(PS[b][:, gl * 16:(gl + 1) * 16], lhsT=xrep[:, gl, :], rhs=selb[:],
                                                                   start=True, stop=True), ["xrep", "selb"], [("ps", b)])
                            ac(lambda e, b=b, kt=kt: e.activation(out=Uv[:, kt, :, 256:272], in_=PS[b][:, 0:128].rearrange("p (g n) -> p g n", g=8),
                                                                  func=AF.Copy), [("ps", b)], [("featT", kt)])
                        P.barrier()
                    for (cn, cnk, cc, cck) in ((cn_re, "cn_re", cre, "cre"), (cn_im, "cn_im", cim, "cim")):
                        allk = [(cnk, pl, g2) for pl in range(4) for g2 in range(2)]
                        for gb in range(8):
                            b = next_bank(0, 4)
                            pe(lambda e, b=b, cn=cn, gb=gb: e.transpose(out=PS[b][:, 0:64], in_=cn[:, gb, :], identity=identf[0:64, 0:64]),
                               allk + ["identf"], [("ps", b)])
                            dv(lambda e, b=b, cc=cc, gb=gb: e.tensor_copy(out=cc[:, gb * 4:(gb + 1) * 4, :].rearrange("p g c -> p (g c)"),
                                                                          in_=PS[b][:, 0:64]), [("ps", b)], [cck])
                    P.barrier()
                dv(lambda e: e.tensor_tensor(out=gdU[:], in0=gdU[:], in1=dU[:], op=ALU.mult), [("gdU", s_) for s_ in range(8)] + [("dU", s_) for s_ in range(8)], ["gdU"])
                ac(lambda e: e.activation(out=tq1[:], in_=ldt[:], func=AF.Exp), LDTK, ["tq1"])
                dv(lambda e: e.tensor_tensor(out=adt[:], in0=are[:], in1=tq1[:], op=ALU.mult), AREK + ["tq1"], ["adt"])
                dv(lambda e: e.tensor_tensor(out=thr[:], in0=aim[:], in1=tq1[:], op=ALU.mult), AIMK + ["tq1"], ["thr"])
                wrap(thr[:], "thr", ki0[:], kf0[:], "w0")
                sincos(lim[:], lre[:], thr[:], "thr", "lim", "lre")
                ac(lambda e: e.activation(out=tq2[:], in_=adt[:], func=AF.Exp), ["adt"], ["tq2"])
                dv(lambda e: e.tensor_tensor(out=lre[:], in0=lre[:], in1=tq2[:], op=ALU.mult), ["lre", "tq2"], ["lre"])
                dv(lambda e: e.tensor_tensor(out=lim[:], in0=lim[:], in1=tq2[:], op=ALU.mult), ["lim", "tq2"], ["lim"])
                dv(lambda e: e.tensor_scalar(out=Th8[:], in0=thr[:], scalar1=8.0, scalar2=None, op0=ALU.mult), ["thr"], ["Th8"])
                wrap(Th8[:], "Th8", ki0[:], kf0[:], "w0")
                ac(lambda e: e.activation(out=Rg[:], in_=adt[:], func=AF.Exp, scale=8.0), ["adt"], ["Rg"])
                sincos(Lim[:], Lre[:], Th8[:], "Th8", "Lim", "Lre")
                dv(lambda e: e.tensor_tensor(out=Lre[:], in0=Lre[:], in1=Rg[:], op=ALU.mult), ["Lre", "Rg"], ["Lre"])
                dv(lambda e: e.tensor_tensor(out=Lim[:], in0=Lim[:], in1=Rg[:], op=ALU.mult), ["Lim", "Rg"], ["Lim"])
                dv(lambda e: e.tensor_scalar(out=nLim[:], in0=Lim[:], scalar1=-1.0, scalar2=None, op0=ALU.mult), ["Lim"], ["nLim"])
                dv(lambda e: e.tensor_scalar(out=Th128[:], in0=Th8[:], scalar1=16.0, scalar2=None, op0=ALU.mult), ["Th8"], ["Th128"])
                wrap(Th128[:], "Th128", ki0[:], kf0[:], "w0")
                dv(lambda e: e.tensor_tensor(out=den[:], in0=are[:], in1=are[:], op=ALU.mult), AREK, ["den"])
                dv(lambda e: e.tensor_tensor(out=tq1[:], in0=aim[:], in1=aim[:], op=ALU.mult), AIMK, ["tq1"])
                dv(lambda e: e.tensor_tensor(out=den[:], in0=den[:], in1=tq1[:], op=ALU.add), ["den", "tq1"], ["den"])
                dv(lambda e: e.reciprocal(out=den[:], in_=den[:]), ["den"], ["den"])
                dv(lambda e: e.tensor_scalar(out=tq2[:], in0=lre[:], scalar1=-1.0, scalar2=None, op0=ALU.add), ["lre"], ["tq2"])
                dv(lambda e: e.tensor_tensor(out=fre[:], in0=tq2[:], in1=are[:], op=ALU.mult), ["tq2"] + AREK, ["fre"])
                dv(lambda e: e.tensor_tensor(out=tq1[:], in0=lim[:], in1=aim[:], op=ALU.mult), ["lim"] + AIMK, ["tq1"])
                dv(lambda e: e.tensor_tensor(out=fre[:], in0=fre[:], in1=tq1[:], op=ALU.add), ["fre", "tq1"], ["fre"])
                dv(lambda e: e.tensor_tensor(out=fre[:], in0=fre[:], in1=den[:], op=ALU.mult), ["fre", "den"], ["fre"])
                dv(lambda e: e.tensor_tensor(out=fim[:], in0=lim[:], in1=are[:], op=ALU.mult), ["lim"] + AREK, ["fim"])
                dv(lambda e: e.tensor_tensor(out=tq1[:], in0=tq2[:], in1=aim[:], op=ALU.mult), ["tq2"] + AIMK, ["tq1"])
                dv(lambda e: e.tensor_tensor(out=fim[:], in0=fim[:], in1=tq1[:], op=ALU.subtract), ["fim", "tq1"], ["fim"])
                dv(lambda e: e.tensor_tensor(out=fim[:], in0=fim[:], in1=den[:], op=ALU.mult), ["fim", "den"], ["fim"])
                freb = fre[:].unsqueeze(2).to_broadcast([128, 32, 16])
                fimb = fim[:].unsqueeze(2).to_broadcast([128, 32, 16])
                dv(lambda e: e.tensor_tensor(out=bbre[:], in0=bre[:], in1=freb, op=ALU.mult), BREK + ["fre"], ["bbre"])
                dv(lambda e: e.tensor_tensor(out=bbim[:], in0=bim[:], in1=fimb, op=ALU.mult), BIMK + ["fim"], ["bbim"])
                dv(lambda e: e.tensor_tensor(out=bbre[:], in0=bbre[:], in1=bbim[:], op=ALU.subtract), ["bbre", "bbim"], ["bbre"])
                dv(lambda e: e.tensor_tensor(out=bbim[:], in0=bim[:], in1=freb, op=ALU.mult), BIMK + ["fre"], ["bbim"])
                dv(lambda e: e.tensor_tensor(out=bim[:], in0=bre[:], in1=fimb, op=ALU.mult), BREK + BIMK + ["fim"], BIMK)
                dv(lambda e: e.tensor_tensor(out=bbim[:], in0=bbim[:], in1=bim[:], op=ALU.add), ["bbim"] + BIMK, ["bbim"])
                dv(lambda e: e.tensor_tensor(out=bbre[:], in0=bbre[:], in1=gainPG[:], op=ALU.mult), ["bbre"] + GPGK, ["bbre"])
                dv(lambda e: e.tensor_tensor(out=bbim[:], in0=bbim[:], in1=gainPG[:], op=ALU.mult), ["bbim"] + GPGK, ["bbim"])
                P.barrier()
                phb.close()

                LKre = sb("LKre", [128, 2, 25], F32, ph)
                LKim = sb("LKim", [128, 2, 25], F32, ph)
                lkph = sb("lkph", [128, 2, 25], F32, ph)
                lkkf = sb("lkkf", [128, 2, 25], F32, ph)
                lkki = sb("lkki", [128, 2, 25], I32, ph)
                lkmg = sb("lkmg", [128, 2, 25], F32, ph)
                eph = sb("eph", [128, 2, 32], F32, ph)
                ekf = sb("ekf", [128, 2, 32], F32, ph)
                eki = sb("eki", [128, 2, 32], I32, ph)
                Ere = sb("Ere", [128, 2, 32], F32, ph)
                Eim = sb("Eim", [128, 2, 32], F32, ph)
                WTre = sb("WTre", [128, 2, 128], F32, ph)
                WTim = sb("WTim", [128, 2, 128], F32, ph)
                ATre = sb("ATre", [128, 2, 128], F32, ph)
                ATim = sb("ATim", [128, 2, 128], F32, ph)
                wtmp = sb("wtmp", [128, 2, 144], F32, ph)
                ptmp = sb("ptmp", [128, 2, 128], F32, ph)
                Bfre = sb("Bfre", [128, 2, 144], F32, ph)
                Bfim = sb("Bfim", [128, 2, 144], F32, ph)
                Mtmp = sb("Mtmp", [128, 2, 2, 128], F32, ph)
                Mw2 = [sb("Mw%d" % i, [128, 4, 128], BF16, ph) for i in range(2)]
                W2re2 = [sb("W2re%d" % i, [128, 2, 128], BF16, ph) for i in range(2)]
                W2im2 = [sb("W2im%d" % i, [128, 2, 128], BF16, ph) for i in range(2)]
                W3re2 = [sb("W3re%d" % i, [128, 2, 128], BF16, ph) for i in range(2)]
                W3im2 = [sb("W3im%d" % i, [128, 2, 128], BF16, ph) for i in range(2)]
                Hbre = sb("Hbre", [128, 2, 256], BF16, ph)
                Hbim = sb("Hbim", [128, 2, 256], BF16, ph)
                Sre = sb("Sre", [16, 4, 64], F32, ph)
                Sim = sb("Sim", [16, 4, 64], F32, ph)
                hsre2 = [sb("hsre%d" % i, [128, 2, 16], F32, ph) for i in range(2)]
                hsim2 = [sb("hsim%d" % i, [128, 2, 16], F32, ph) for i in range(2)]
                hbre2 = [sb("hbre%d" % i, [128, 2, 16], BF16, ph) for i in range(2)]
                hbim2 = [sb("hbim%d" % i, [128, 2, 16], BF16, ph) for i in range(2)]
                SNre = sb("SNre", [128, 2, 16], F32, ph)
                SNim = sb("SNim", [128, 2, 16], F32, ph)
                Sore = sb("Sore", [16, 4, 64], F32, ph)
                Soim = sb("Soim", [16, 4, 64], F32, ph)
                zJ = [sb("zJ%d" % i, [128, 8, 128], BF16, ph) for i in range(2)]
                zS = sb("zS", [16, 8, 128], BF16, ph)
                tSs = [sb("tS%d" % i, [128, 256], F32, ph) for i in range(4)]
                tCs = [sb("tC%d" % i, [128, 256], F32, ph) for i in range(4)]
                gtm = [sb("gtm%d" % i, [128, 256], F32, ph) for i in range(2)]
                q1 = sb("q1", [128, 256], F32, ph)
                q2 = sb("q2", [128, 256], F32, ph)
                q3 = sb("q3", [128, 256], F32, ph)
                q4 = sb("q4", [128, 256], F32, ph)
                rre = sb("rre", [128, 256], F32, ph)
                rim = sb("rim", [128, 256], F32, ph)
                Gre = sb("Gre", [128, 256], F32, ph)
                Gim = sb("Gim", [128, 256], F32, ph)
                Po = sb("Po", [32, 256], F32, ph)
                dv(lambda e: e.memset(Hbre[:], 0.0), [], ["Hbre"])
                dv(lambda e: e.memset(Hbim[:], 0.0), [], ["Hbim"])
                kv = cst[:, 0:25]
                jv16 = cst[:, 25:41]

                def prep1(q):
                    r0 = 2 * q
                    dv(lambda e: e.tensor_tensor(out=lkph[:], in0=thr[:, r0:r0 + 2].unsqueeze(2).to_broadcast([128, 2, 25]),
                                                 in1=kv.unsqueeze(1).to_broadcast([128, 2, 25]), op=ALU.mult), ["thr", "cst"], ["lkph"])
                    wrap(lkph[:], "lkph", lkki[:], lkkf[:], "wl")
                    dv(lambda e: e.tensor_tensor(out=lkmg[:], in0=adt[:, r0:r0 + 2].unsqueeze(2).to_broadcast([128, 2, 25]),
                                                 in1=kv.unsqueeze(1).to_broadcast([128, 2, 25]), op=ALU.mult), ["adt", "cst"], ["lkmg"])
                    dv(lambda e: e.tensor_tensor(out=eph[:, :, 0:16], in0=Th8[:, r0:r0 + 2].unsqueeze(2).to_broadcast([128, 2, 16]),
                                                 in1=jv16.unsqueeze(1).to_broadcast([128, 2, 16]), op=ALU.mult), ["Th8", "cst"], ["eph"])
                    dv(lambda e: e.tensor_tensor(out=eph[:, :, 16:32], in0=Th128[:, r0:r0 + 2].unsqueeze(2).to_broadcast([128, 2, 16]),
                                                 in1=jv16.unsqueeze(1).to_broadcast([128, 2, 16]), op=ALU.mult), ["Th128", "cst", "eph"], ["eph"])
                    wrap(eph[:], "eph", eki[:], ekf[:], "we")
                    ac(lambda e: e.activation(out=lkmg[:], in_=lkmg[:], func=AF.Exp), ["lkmg"], ["lkmg"])
                    sincos(LKim[:], LKre[:], lkph[:], "lkph", "LKim", "LKre")
                    sincos(Eim[:], Ere[:], eph[:], "eph", "Eim", "Ere")
                    dv(lambda e: e.tensor_tensor(out=LKre[:], in0=LKre[:], in1=lkmg[:], op=ALU.mult), ["LKre", "lkmg"], ["LKre"])
                    dv(lambda e: e.tensor_tensor(out=LKim[:], in0=LKim[:], in1=lkmg[:], op=ALU.mult), ["LKim", "lkmg"], ["LKim"])

                    def lkb(t, k0, n):
                        return t[:, :, k0:k0 + n].unsqueeze(3).to_broadcast([128, 2, n, 16])

                    def bbb(t, n):
                        return t[:, r0:r0 + 2, :].unsqueeze(2).to_broadcast([128, 2, n, 16])

                    def cplx(opf, tmp, tmpk, ore, oim, k0, n, xre, xim, xk, neg_im=False):
                        o4 = lambda t: t[:, :, 0:n * 16].rearrange("p g (s c) -> p g s c", c=16)
                        Or, Oi, Ot = o4(ore), o4(oim), o4(tmp)
                        lr, li = lkb(LKre, k0, n), lkb(LKim, k0, n)
                        xr, xi = bbb(xre, n), bbb(xim, n)
                        rn, inn = NM[ore.name], NM[oim.name]
                        opf(lambda e: e.tensor_tensor(out=Or, in0=lr, in1=xr, op=ALU.mult), ["LKre"] + xk, [rn])
                        opf(lambda e: e.tensor_tensor(out=Ot, in0=li, in1=xi, op=ALU.mult), ["LKim"] + xk, [tmpk])
                        opf(lambda e: e.tensor_tensor(out=Or, in0=Or, in1=Ot, op=ALU.subtract), [rn, tmpk], [rn])
                        opf(lambda e: e.tensor_tensor(out=Oi, in0=lr, in1=xi, op=ALU.mult), ["LKre"] + xk, [inn])
                        opf(lambda e: e.tensor_tensor(out=Ot, in0=li, in1=xr, op=ALU.mult), ["LKim"] + xk, [tmpk])
                        opf(lambda e: e.tensor_tensor(out=Oi, in0=Oi, in1=Ot, op=ALU.add), [inn, tmpk], [inn])
                        if neg_im:
                            opf(lambda e: e.tensor_scalar(out=Oi, in0=Oi, scalar1=-1.0, scalar2=0.0, op0=ALU.mult, op1=ALU.add), [inn], [inn])

                    cplx(gp, ptmp, "ptmp", WTre, WTim, 0, 8, bbre, bbim, ["bbre", "bbim"])
                    cplx(gp, ptmp, "ptmp", ATre, ATim, 8, 8, bbre, bbim, ["bbre", "bbim"])
                    cplx(gp, wtmp, "wtmp", Bfre, Bfim, 16, 9, cre, cim, ["cre", "cim"], neg_im=True)

                def prep2(q):
                    g0 = q * 4
                    qp = q % 2
                    Mw, W2re, W2im, W3re, W3im = Mw2[qp], W2re2[qp], W2im2[qp], W3re2[qp], W3im2[qp]
                    hsre, hsim, hbre, hbim = hsre2[qp], hsim2[qp], hbre2[qp], hbim2[qp]
                    W = lambda n: (n, qp)
                    for (wt, w2, w2k) in ((WTre, W2re, W("W2re")), (WTim, W2im, W("W2im"))):
                        b = next_bank(0, 4)
                        for pl in range(2):
                            pe(lambda e, b=b, pl=pl, wt=wt: e.transpose(out=PS[b][:, pl * 128:(pl + 1) * 128], in_=wt[:, pl, :], identity=identf[:]),
                               [NM[wt.name], "identf"], [("ps", b)])
                        ac(lambda e, b=b, w2=w2: e.activation(out=w2[:].rearrange("p g n -> p (g n)"), in_=PS[b][:, 0:256], func=AF.Copy),
                           [("ps", b)], [w2k])
                    bm = [next_bank(0, 4), next_bank(0, 4)]
                    for pl in range(2):
                        for g2 in range(2):
                            hs_ = slice(g2 * 64, (g2 + 1) * 64)
                            b = bm[g2]
                            pe(lambda e, b=b, pl=pl, hs_=hs_: e.matmul(PS[b][:, pl * 128:(pl + 1) * 128], lhsT=ATre[hs_, pl, :], rhs=Bfre[hs_, pl, 0:128],
                                                                       start=True, stop=False), ["ATre", "Bfre"], [("ps", b)])
                            pe(lambda e, b=b, pl=pl, hs_=hs_: e.matmul(PS[b][:, pl * 128:(pl + 1) * 128], lhsT=ATim[hs_, pl, :], rhs=Bfim[hs_, pl, 0:128],
                                                                       start=False, stop=True), ["ATim", "Bfim"], [("ps", b)])
                    for g2 in range(2):
                        dv(lambda e, g2=g2: e.tensor_tensor(out=Mtmp[:, g2, :, :], in0=PS[bm[g2]][:, 0:256].rearrange("p (g n) -> p g n", g=2),
                                                            in1=maskM[:].unsqueeze(1).to_broadcast([128, 2, 128]), op=ALU.mult), [("ps", bm[g2]), "maskM"], [("Mtmp", g2)])
                    for pl in range(2):
                        for g2 in range(2):
                            gl = 2 * pl + g2
                            dv(lambda e, gl=gl, pl=pl, g2=g2: e.scalar_tensor_tensor(out=Mw[:, gl, :], in0=identf[:], scalar=gdU[:, g0 + gl:g0 + gl + 1],
                                                                                    in1=Mtmp[:, g2, pl, :], op0=ALU.mult, op1=ALU.add),
                               [("Mtmp", g2), "gdU", "identf"], [W("Mw")])
                    ac(lambda e: e.activation(out=W3re[:], in_=Bfre[:, :, 16:144], func=AF.Copy), ["Bfre"], [W("W3re")])
                    ac(lambda e: e.activation(out=W3im[:], in_=Bfim[:, :, 16:144], func=AF.Copy), ["Bfim"], [W("W3im")])
                    P.op("sp", lambda e: e.dma_start(out=Sre[:], in_=st_re_d[jl, :, g0:g0 + 4, :]), writes=["Sre"], dma=True)
                    P.op("sp", lambda e: e.dma_start(out=Sim[:], in_=st_im_d[jl, :, g0:g0 + 4, :]), writes=["Sim"], dma=True)
                    bs = next_bank(0, 4)
                    for pl in range(2):
                        pe(lambda e, pl=pl, bs=bs: e.transpose(out=PS[bs][:, pl * 16:(pl + 1) * 16], in_=Sre[0:16, 2 * pl:2 * pl + 2, :].rearrange("s g p -> s (g p)"),
                                                               identity=identf[0:16, 0:16]), ["Sre", "identf"], [("ps", bs)])
                        pe(lambda e, pl=pl, bs=bs: e.transpose(out=PS[bs][:, 32 + pl * 16:32 + (pl + 1) * 16], in_=Sim[0:16, 2 * pl:2 * pl + 2, :].rearrange("s g p -> s (g p)"),
                                                               identity=identf[0:16, 0:16]), ["Sim", "identf"], [("ps", bs)])
                    ac(lambda e, bs=bs: e.activation(out=hsre[:].rearrange("p g n -> p (g n)"), in_=PS[bs][:, 0:32], func=AF.Copy), [("ps", bs)], [W("hsre")])
                    ac(lambda e, bs=bs: e.activation(out=hsim[:].rearrange("p g n -> p (g n)"), in_=PS[bs][:, 32:64], func=AF.Copy), [("ps", bs)], [W("hsim")])
                    ac(lambda e: e.activation(out=hbre[:], in_=hsre[:], func=AF.Copy), [W("hsre")], [W("hbre")])
                    ac(lambda e: e.activation(out=hbim[:], in_=hsim[:], func=AF.Copy), [W("hsim")], [W("hbim")])
                    for pl in range(2):
                        ar = Ere[:, pl, 16:32].unsqueeze(2).to_broadcast([128, 16, 16])
                        ai = Eim[:, pl, 16:32].unsqueeze(2).to_broadcast([128, 16, 16])
                        br = Ere[:, pl, 0:16].unsqueeze(1).to_broadcast([128, 16, 16])
                        bi = Eim[:, pl, 0:16].unsqueeze(1).to_broadcast([128, 16, 16])
                        ti = 2 * qp + pl
                        C3 = tCs[ti][:].rearrange("p (a b) -> p a b", b=16)
                        S3 = tSs[ti][:].rearrange("p (a b) -> p a b", b=16)
                        T0 = gtm[0][:].rearrange("p (a b) -> p a b", b=16)
                        T1 = gtm[1][:].rearrange("p (a b) -> p a b", b=16)
                        gp(lambda e, C3=C3, ar=ar, br=br: e.tensor_tensor(out=C3, in0=ar, in1=br, op=ALU.mult), ["Ere"], [("tC", ti)])
                        gp(lambda e, T0=T0, ai=ai, bi=bi: e.tensor_tensor(out=T0, in0=ai, in1=bi, op=ALU.mult), ["Eim"], ["gtm0"])
                        gp(lambda e, C3=C3, T0=T0: e.tensor_tensor(out=C3, in0=C3, in1=T0, op=ALU.subtract), [("tC", ti), "gtm0"], [("tC", ti)])
                        gp(lambda e, S3=S3, ar=ar, bi=bi: e.tensor_tensor(out=S3, in0=ar, in1=bi, op=ALU.mult), ["Ere", "Eim"], [("tS", ti)])
                        gp(lambda e, T1=T1, ai=ai, br=br: e.tensor_tensor(out=T1, in0=ai, in1=br, op=ALU.mult), ["Ere", "Eim"], ["gtm1"])
                        gp(lambda e, S3=S3, T1=T1: e.tensor_tensor(out=S3, in0=S3, in1=T1, op=ALU.add), [("tS", ti), "gtm1"], [("tS", ti)])

                def main(q):
                    g0 = q * 4
                    kt = q // 2
                    qh = q % 2
                    qp = q % 2
                    Mw, W2re, W2im, W3re, W3im = Mw2[qp], W2re2[qp], W2im2[qp], W3re2[qp], W3im2[qp]
                    hsre, hsim, hbre, hbim = hsre2[qp], hsim2[qp], hbre2[qp], hbim2[qp]
                    W = lambda n: (n, qp)
                    for pl in range(2):
                        pr = 2 * q + pl
                        ti = 2 * qp + pl
                        tS, tC = tSs[ti], tCs[ti]
                        tSk, tCk = ("tS", ti), ("tC", ti)
                        bi_re = 4 + next_bank(0, 4)
                        bi_im = 4 + next_bank(0, 4)
                        for g2 in range(2):
                            hs_ = slice(g2 * 64, (g2 + 1) * 64)
                            Ug = Uv[:, kt, qh * 4 + 2 * pl + g2, :]
                            pe(lambda e, pl=pl, Ug=Ug, b=bi_re, hs_=hs_: e.matmul(PS[b][hs_, 0:272], lhsT=W2re[:, pl, hs_], rhs=Ug, start=True, stop=True),
                               [W("W2re"), ("featT", kt)], [("ps", bi_re)])
                            pe(lambda e, pl=pl, Ug=Ug, b=bi_im, hs_=hs_: e.matmul(PS[b][hs_, 0:272], lhsT=W2im[:, pl, hs_], rhs=Ug, start=True, stop=True),
                               [W("W2im"), ("featT", kt)], [("ps", bi_im)])
                        ire = PS[bi_re][:, 0:256]
                        iim = PS[bi_im][:, 0:256]
                        dv(lambda e, ire=ire, tC=tC: e.tensor_tensor(out=q1[:], in0=tC[:], in1=ire, op=ALU.mult), [tCk, ("ps", bi_re)], ["q1"])
                        dv(lambda e, iim=iim, tS=tS: e.tensor_tensor(out=q2[:], in0=tS[:], in1=iim, op=ALU.mult), [tSk, ("ps", bi_im)], ["q2"])
                        dv(lambda e: e.tensor_tensor(out=rre[:], in0=q1[:], in1=q2[:], op=ALU.add), ["q1", "q2"], ["rre"])
                        dv(lambda e, iim=iim, tC=tC: e.tensor_tensor(out=q3[:], in0=tC[:], in1=iim, op=ALU.mult), [tCk, ("ps", bi_im)], ["q3"])
                        dv(lambda e, ire=ire, tS=tS: e.tensor_tensor(out=q4[:], in0=tS[:], in1=ire, op=ALU.mult), [tSk, ("ps", bi_re)], ["q4"])
                        dv(lambda e: e.tensor_tensor(out=rim[:], in0=q3[:], in1=q4[:], op=ALU.subtract), ["q3", "q4"], ["rim"])
                        Rb = Rg[:, pr:pr + 1].to_broadcast([128, 256])
                        dv(lambda e, Rb=Rb: e.tensor_tensor_scan(out=Gre[:], data0=Rb, data1=rre[:], initial=0.0, op0=ALU.mult, op1=ALU.add), ["Rg", "rre"], ["Gre"])
                        dv(lambda e, Rb=Rb: e.tensor_tensor_scan(out=Gim[:], data0=Rb, data1=rim[:], initial=0.0, op0=ALU.mult, op1=ALU.add), ["Rg", "rim"], ["Gim"])
                        dv(lambda e, tC=tC: e.tensor_tensor(out=q1[:], in0=tC[:], in1=Gre[:], op=ALU.mult), [tCk, "Gre"], ["q1"])
                        dv(lambda e, tS=tS: e.tensor_tensor(out=q2[:], in0=tS[:], in1=Gim[:], op=ALU.mult), [tSk, "Gim"], ["q2"])
                        dv(lambda e, pl=pl: e.tensor_tensor(out=Hbre[:, pl, 1:256], in0=q1[:, 0:255], in1=q2[:, 0:255], op=ALU.subtract), ["q1", "q2"], ["Hbre"])
                        dv(lambda e, pr=pr: e.tensor_tensor(out=Pre[:, pr:pr + 1], in0=q1[:, 255:256], in1=q2[:, 255:256], op=ALU.subtract), ["q1", "q2"], ["Pre"])
                        dv(lambda e, tC=tC: e.tensor_tensor(out=q3[:], in0=tC[:], in1=Gim[:], op=ALU.mult), [tCk, "Gim"], ["q3"])
                        dv(lambda e, tS=tS: e.tensor_tensor(out=q4[:], in0=tS[:], in1=Gre[:], op=ALU.mult), [tSk, "Gre"], ["q4"])
                        dv(lambda e, pl=pl: e.tensor_tensor(out=Hbim[:, pl, 1:256], in0=q3[:, 0:255], in1=q4[:, 0:255], op=ALU.add), ["q3", "q4"], ["Hbim"])
                        dv(lambda e, pr=pr: e.tensor_tensor(out=Pim[:, pr:pr + 1], in0=q3[:, 255:256], in1=q4[:, 255:256], op=ALU.add), ["q3", "q4"], ["Pim"])
                        sre = PS[bi_re][:, 256:272]
                        sim = PS[bi_im][:, 256:272]
                        dv(lambda e, pl=pl, pr=pr, sre=sre: e.scalar_tensor_tensor(out=SNre[:, pl, :], in0=hsre[:, pl, :], scalar=Lre[:, pr:pr + 1], in1=sre,
                                                                                   op0=ALU.mult, op1=ALU.add), [W("hsre"), "Lre", ("ps", bi_re)], ["SNre"])
                        dv(lambda e, pl=pl, pr=pr: e.scalar_tensor_tensor(out=SNre[:, pl, :], in0=hsim[:, pl, :], scalar=nLim[:, pr:pr + 1], in1=SNre[:, pl, :],
                                                                          op0=ALU.mult, op1=ALU.add), [W("hsim"), "nLim", "SNre"], ["SNre"])
                        dv(lambda e, pl=pl, pr=pr, sim=sim: e.scalar_tensor_tensor(out=SNim[:, pl, :], in0=hsim[:, pl, :], scalar=Lre[:, pr:pr + 1], in1=sim,
                                                                                   op0=ALU.mult, op1=ALU.add), [W("hsim"), "Lre", ("ps", bi_im)], ["SNim"])
                        dv(lambda e, pl=pl, pr=pr: e.scalar_tensor_tensor(out=SNim[:, pl, :], in0=hsre[:, pl, :], scalar=Lim[:, pr:pr + 1], in1=SNim[:, pl, :],
                                                                          op0=ALU.mult, op1=ALU.add), [W("hsre"), "Lim", "SNim"], ["SNim"])
                    yield None
                    bo = next_bank(0, 4)
                    for pl in range(2):
                        pe(lambda e, pl=pl, bo=bo: e.transpose(out=PS[bo][0:16, pl * 128:(pl + 1) * 128], in_=SNre[:, pl, :], identity=identf[:]),
                           ["SNre", "identf"], [("ps", bo)])
                        pe(lambda e, pl=pl, bo=bo: e.transpose(out=PS[bo][0:16, 256 + pl * 128:256 + (pl + 1) * 128], in_=SNim[:, pl, :], identity=identf[:]),
                           ["SNim", "identf"], [("ps", bo)])
                    ac(lambda e, bo=bo: e.activation(out=Sore[:].rearrange("p g n -> p (g n)"), in_=PS[bo][0:16, 0:256], func=AF.Copy), [("ps", bo)], ["Sore"])
                    ac(lambda e, bo=bo: e.activation(out=Soim[:].rearrange("p g n -> p (g n)"), in_=PS[bo][0:16, 256:512], func=AF.Copy), [("ps", bo)], ["Soim"])
                    P.op("sp", lambda e: e.dma_start(out=sst_re_d[jl, :, g0:g0 + 4, :], in_=Sore[:]), reads=["Sore"], writes=["sst_re_o"], dma=True)
                    P.op("sp", lambda e: e.dma_start(out=sst_im_d[jl, :, g0:g0 + 4, :], in_=Soim[:]), reads=["Soim"], writes=["sst_im_o"], dma=True)
                    for jt in range(2):
                        by = [4 + next_bank(0, 4), 4 + next_bank(0, 4)]
                        for pl in range(2):
                            for g2 in range(2):
                                hs_ = slice(g2 * 64, (g2 + 1) * 64)
                                gl = 2 * pl + g2
                                b = by[g2]
                                o = PS[b][:, pl * 128:(pl + 1) * 128]
                                pe(lambda e, o=o, gl=gl, jt=jt: e.matmul(o, lhsT=Uv[:, kt, qh * 4 + gl, jt * 128:(jt + 1) * 128], rhs=Mw[:, gl, :], start=True, stop=False),
                                   [("featT", kt), W("Mw")], [("ps", b)])
                                pe(lambda e, o=o, pl=pl, jt=jt, hs_=hs_: e.matmul(o, lhsT=Hbre[hs_, pl, jt * 128:(jt + 1) * 128], rhs=W3re[hs_, pl, :], start=False, stop=False),
                                   ["Hbre", W("W3re")], [("ps", b)])
                                pe(lambda e, o=o, pl=pl, jt=jt, hs_=hs_: e.matmul(o, lhsT=Hbim[hs_, pl, jt * 128:(jt + 1) * 128], rhs=W3im[hs_, pl, :], start=False, stop=True),
                                   ["Hbim", W("W3im")], [("ps", b)])
                        for g2 in range(2):
                            ac(lambda e, b=by[g2], jt=jt, g2=g2: e.activation(
                                out=zJ[jt][:, :, qh * 64:(qh + 1) * 64].rearrange("p t (l g c) -> p g l t c", g=2, c=16)[:, g2],
                                in_=PS[b][:, 0:256].rearrange("p (l t c) -> p l t c", l=2, t=8), func=AF.Gelu), [("ps", by[g2])], [("zJ", jt)])
                    bys = [4 + next_bank(0, 4), 4 + next_bank(0, 4)]
                    for pl in range(2):
                        for g2 in range(2):
                            hs_ = slice(g2 * 64, (g2 + 1) * 64)
                            gl = 2 * pl + g2
                            b = bys[g2]
                            o = PS[b][0:16, pl * 128:(pl + 1) * 128]
                            pe(lambda e, o=o, gl=gl: e.matmul(o, lhsT=Uv[:, kt, qh * 4 + gl, 256:272], rhs=Mw[:, gl, :], start=True, stop=False),
                               [("featT", kt), W("Mw")], [("ps", b)])
                            pe(lambda e, o=o, pl=pl, hs_=hs_: e.matmul(o, lhsT=hbre[hs_, pl, :], rhs=W3re[hs_, pl, :], start=False, stop=False), [W("hbre"), W("W3re")], [("ps", b)])
                            pe(lambda e, o=o, pl=pl, hs_=hs_: e.matmul(o, lhsT=hbim[hs_, pl, :], rhs=W3im[hs_, pl, :], start=False, stop=True), [W("hbim"), W("W3im")], [("ps", b)])
                    for g2 in range(2):
                        ac(lambda e, b=bys[g2], g2=g2: e.activation(
                            out=zS[:, :, qh * 64:(qh + 1) * 64].rearrange("p t (l g c) -> p g l t c", g=2, c=16)[:, g2],
                            in_=PS[b][0:16, 0:256].rearrange("p (l t c) -> p l t c", l=2, t=8), func=AF.Gelu), [("ps", bys[g2])], ["zS"])

                    def finish():
                        for jt in range(2):
                            b = next_bank(0, 4)
                            psb = PS[b][:].bitcast(BF16)
                            for t in range(8):
                                pe(lambda e, psb=psb, t=t, jt=jt: e.transpose(out=psb[:, t * 128:(t + 1) * 128], in_=zJ[jt][:, t, :], identity=identb[:]),
                                   [("zJ", jt), "identb"], [("ps", b)])
                            dv(lambda e, psb=psb, jt=jt: e.tensor_copy(
                                out=featT[:, kt, jt * 1024:(jt + 1) * 1024].rearrange("p (j t) -> p j t", t=8),
                                in_=psb.rearrange("p (t j) -> p j t", t=8)), [("ps", b)], [("featT", kt)])
                        b = next_bank(0, 4)
                        psb = PS[b][:].bitcast(BF16)
                        for t in range(8):
                            pe(lambda e, psb=psb, t=t: e.transpose(out=psb[:, t * 16:(t + 1) * 16], in_=zS[0:16, t, :], identity=identb[0:16, 0:16]),
                               ["zS", "identb"], [("ps", b)])
                        dv(lambda e, psb=psb: e.tensor_copy(out=featT[:, kt, 2048:2176].rearrange("p (s t) -> p s t", t=8),
                                                            in_=psb[:, 0:128].rearrange("p (t s) -> p s t", t=8)), [("ps", b)], [("featT", kt)])

                    yield (finish if qh == 1 else None)

                prep1(0)
                prep2(0)
                prep1(1)
                pending = None
                for q in range(16):
                    if pending is not None:
                        pending()
                        pending = None
                    mg = main(q)
                    next(mg)
                    if q + 1 < 16:
                        prep2(q + 1)
                    if q + 2 < 16:
                        prep1(q + 2)
                    fin = next(mg)
                    if fin is not None:
                        pending = fin
                if pending is not None:
                    pending()
                b = next_bank(0, 4)
                pe(lambda e, b=b: e.transpose(out=PS[b][0:32, 0:128], in_=Pre[:], identity=identf[:]), ["Pre", "identf"], [("ps", b)])
                pe(lambda e, b=b: e.transpose(out=PS[b][0:32, 128:256], in_=Pim[:], identity=identf[:]), ["Pim", "identf"], [("ps", b)])
                dv(lambda e, b=b: e.tensor_copy(out=Po[:], in_=PS[b][0:32, 0:256]), [("ps", b)], ["Po"])
                P.op("sp", lambda e: e.dma_start(out=pst_re_d[jl].rearrange("(r t) p -> r (t p)", t=2), in_=Po[:, 0:128]), reads=["Po"], writes=["pst_re_o"], dma=True)
                P.op("sp", lambda e: e.dma_start(out=pst_im_d[jl].rearrange("(r t) p -> r (t p)", t=2), in_=Po[:, 128:256]), reads=["Po"], writes=["pst_im_o"], dma=True)
                P.barrier()
            with ExitStack() as ph:
                wv = sb("wv", [128, 8, 512], BF16, ph)
                wgt = sb("wgt", [128, 8, 512], BF16, ph)
                gt = [sb("gt%d" % i, [128, 512], F32, ph) for i in range(2)]
                it = 0
                for half in range(2):
                    P.op("pool", lambda e, half=half: e.dma_start(
                        out=wv[:], in_=wglu_d[jl, :, half * 512:(half + 1) * 512].rearrange("(k p) c -> p k c", p=128)), writes=["wv"], dma=True)
                    P.op("pool", lambda e, half=half: e.dma_start(
                        out=wgt[:], in_=wglu_d[jl, :, 1024 + half * 512:1024 + (half + 1) * 512].rearrange("(k p) c -> p k c", p=128)), writes=["wgt"], dma=True)
                    for rg in range(17):
                        bv = 2 * (it % 4)
                        bg = bv + 1
                        tp = gt[it % 2]
                        tk = ("gt", it % 2)
                        it += 1
                        for k in range(8):
                            pe(lambda e, k=k, bv=bv, rg=rg: e.matmul(PS[bv][:, :], lhsT=feat_cols(featT, k, rg), rhs=wv[:, k, :], start=(k == 0), stop=(k == 7)),
                               [("featT", k), "wv"], [("ps", bv)])
                        for k in range(8):
                            pe(lambda e, k=k, bg=bg, rg=rg: e.matmul(PS[bg][:, :], lhsT=feat_cols(featT, k, rg), rhs=wgt[:, k, :], start=(k == 0), stop=(k == 7)),
                               [("featT", k), "wgt"], [("ps", bg)])
                        ac(lambda e, bg=bg, tp=tp: e.activation(out=tp[:], in_=PS[bg][:, :], func=AF.Sigmoid), [("ps", bg)], [tk])
                        dv(lambda e, bv=bv, tp=tp: e.tensor_tensor(out=tp[:], in0=tp[:], in1=PS[bv][:, :], op=ALU.mult), [tk, ("ps", bv)], [tk])
                        xk = "XP" if rg < 16 else "XS"
                        xr = Xrow(rg)[:, half * 512:(half + 1) * 512]
                        dv(lambda e, xr=xr, tp=tp: e.tensor_tensor(out=xr, in0=xr, in1=tp[:], op=ALU.add), [tk, xk], [xk])
                P.barrier()

        adbg = cfg.get("adbg", {})

        def attn_layer(L):
            P.handoff = 0.0
            P.cscale = dict(CS_ATT)
            jl = L // 2
            with ExitStack() as ph0:
                alloc_scr(ph0)
                norm(L)
                P.barrier()
            with ExitStack() as ph:
                wqkv = sb("wqkv", [128, 8, 1536], BF16, ph)
                for c in range(3):
                    P.op("pool", lambda e, c=c: e.dma_start(out=wqkv[:, :, c * 512:(c + 1) * 512],
                                                            in_=wqkv_d[jl, :, c * 512:(c + 1) * 512].rearrange("(k p) c -> p k c", p=128)),
                         writes=[("wqkv", c)], dma=True)
                qkvf2 = [sb("qkvf%d" % i, [128, 1536], F32, ph) for i in range(2)]
                gq = sb("gq", [128, 2, 64], F32, ph)
                sq = sb("sq", [128, 20, 64], F32, ph)
                rs2 = [sb("rs%d" % i, [128, 20], F32, ph) for i in range(2)]
                rt = [sb("rt%d" % i, [128, 20, 8], F32, ph) for i in range(4)]
                qkb2 = [sb("qkb%d" % i, [128, 20, 64], BF16, ph) for i in range(2)]
                kdup2 = [sb("kdup%d" % i, [128, 4, 2, 64], BF16, ph) for i in range(2)]
                kdups = [sb("kdups%d" % i, [128, 4, 2, 64], BF16, ph) for i in range(2)]
                qT3 = [sb("qT%d" % i, [128, 8, 128], BF16, ph) for i in range(3)]
                kTd = [sb("kTd%d" % i, [128, 4, 128], BF16, ph) for i in range(3)]
                vaug = [sb("vaug%d" % i, [128, 4, 65], BF16, ph) for i in range(3)]
                kTs = [sb("kTs%d" % i, [128, 4, 128], BF16, ph) for i in range(2)]
                vas = [sb("vas%d" % i, [128, 4, 65], BF16, ph) for i in range(2)]
                ckb = [sb("ckb%d" % i, [128, 4, 64], BF16, ph) for i in range(2)]
                pT = [sb("pT%d" % i, [128, 16, 128], BF16, ph) for i in range(2)]
                SC = PSALL[:, 0:2048]
                PVr = PSALL[:, 2048:3584]
                oacc3 = [sb("oacc%d" % i, [128, 16, 65], F32, ph) for i in range(3)]
                ob = sb("ob", [128, 16, 64], BF16, ph)
                esink = sb("esink", [128, 16], F32, ph)
                den = sb("den", [128, 16], F32, ph)
                P.op("sp", lambda e: e.dma_start(out=gq[:, 0, :], in_=qnorm_d[jl:jl + 1, :].to_broadcast([128, 64])), writes=[("gq", 0)], dma=True)
                P.op("sp", lambda e: e.dma_start(out=gq[:, 1, :], in_=knorm_d[jl:jl + 1, :].to_broadcast([128, 64])), writes=[("gq", 1)], dma=True)
                gqk = [("gq", 0), ("gq", 1)]
                P.op("sp", lambda e: e.dma_start(out=esink[:], in_=sinks_d[jl:jl + 1, :].to_broadcast([128, 16])), writes=["esink"], dma=True)
                ac(lambda e: e.activation(out=esink[:], in_=esink[:], func=AF.Exp), ["esink"], ["esink"])
                for i in range(3):
                    dv(lambda e, i=i: e.memset(vaug[i][:], 1.0), [], [("vaug", i)])
                for i in range(2):
                    dv(lambda e, i=i: e.memset(vas[i][:], 1.0), [], [("vas", i)])
                P.op("sp", lambda e: e.dma_start(out=sck_d[jl, :, 0:120, :], in_=ck_d[jl, :, 8:128, :]), writes=["sck_a"], dma=True)
                P.op("sp", lambda e: e.dma_start(out=scv_d[jl, :, 0:120, :], in_=cv_d[jl, :, 8:128, :]), writes=["scv_a"], dma=True)
                itc = [0]
                sck = [("ps", 0), ("ps", 1), ("ps", 2), ("ps", 3)]
                pvk = [("ps", 4), ("ps", 5), ("ps", 6)]

                def key_block(st_, kT, kTk, va, vak, mask):
                    qT, oacc, qTk, oak = st_["qT"], st_["oacc"], st_["qTk"], st_["oak"]
                    i = itc[0] % 2
                    itc[0] += 1
                    for kv in range(4):
                        for h2 in range(2):
                            off = (h2 * 2 + kv // 2) * 512 + (kv % 2) * 256
                            pe(lambda e, off=off, h2=h2, kv=kv: e.matmul(SC[:, off:off + 256], lhsT=kT[h2 * 64:(h2 + 1) * 64, kv, :],
                                                                         rhs=qT[h2 * 64:(h2 + 1) * 64, 2 * kv:2 * kv + 2, :], start=True, stop=True),
                               [kTk, qTk], sck)
                    ac(lambda e, i=i: e.activation(out=pT[i][:].rearrange("p h q -> p (h q)"), in_=SC, func=AF.Exp, scale=0.125), sck, [("pT", i)])
                    dv(lambda e, i=i: e.tensor_tensor(out=pT[i][:], in0=pT[i][:], in1=mask.unsqueeze(1).to_broadcast([128, 16, 128]), op=ALU.mult),
                       [("pT", i), "masks"], [("pT", i)])
                    for h in range(16):
                        kv, r = h // 4, h % 4
                        slot = (r % 2) * 8 + kv * 2 + r // 2
                        off = (h // 7) * 512 + (h % 7) * 65
                        pe(lambda e, off=off, slot=slot, kv=kv, i=i: e.matmul(PVr[:, off:off + 65], lhsT=pT[i][:, slot, :], rhs=va[:, kv, :], start=True, stop=True),
                           [("pT", i), vak], pvk)
                    pv14 = PVr[:, 0:1024].rearrange("p (b n) -> p b n", b=2)[:, :, 0:455].rearrange("p b (h d) -> p b h d", d=65)
                    pv2 = PVr[:, 1024:1154].rearrange("p (h d) -> p h d", d=65)
                    o14 = oacc[:, 0:14, :].rearrange("p (b h) d -> p b h d", b=2)
                    o2 = oacc[:, 14:16, :]
                    if st_["first"]:
                        st_["first"] = False
                        dv(lambda e: e.tensor_copy(out=o14, in_=pv14), pvk, [oak])
                        dv(lambda e: e.tensor_copy(out=o2, in_=pv2), pvk, [oak])
                    else:
                        dv(lambda e: e.tensor_tensor(out=o14, in0=o14, in1=pv14, op=ALU.add), pvk + [oak], [oak])
                        dv(lambda e: e.tensor_tensor(out=o2, in0=o2, in1=pv2, op=ALU.add), pvk + [oak], [oak])

                def block(blk):
                    par = blk % 2
                    p3 = blk % 3
                    q3 = (blk - 1) % 3
                    cols = slice(blk * 128, (blk + 1) * 128)
                    qkvf, rs, qkb, kdup = qkvf2[par], rs2[par], qkb2[par], kdup2[par]
                    qT, oacc = qT3[p3], oacc3[p3]
                    K_ = lambda n: (n, par)
                    st_ = dict(qT=qT, oacc=oacc, qTk=("qT", p3), oak=("oacc", p3), first=True)
                    fbk = ("fb", blk)
                    qf = [("qkvf", c, par) for c in range(3)]
                    for c in range(3):
                        for k in range(8):
                            pe(lambda e, c=c, k=k: e.matmul(PS[7][:, :], lhsT=featT[:, k, cols], rhs=wqkv[:, k, c * 512:(c + 1) * 512], start=(k == 0), stop=(k == 7)),
                               [fbk, ("wqkv", c)], [("ps", 7)])
                        ac(lambda e, c=c: e.activation(out=qkvf[:, c * 512:(c + 1) * 512], in_=PS[7][:, :], func=AF.Copy), [("ps", 7)], [("qkvf", c, par)])
                    yield
                    qk3 = qkvf[:, 0:1280].rearrange("p (h d) -> p h d", d=64)
                    ac(lambda e: e.activation(out=sq[:], in_=qk3, func=AF.Square), qf, ["sq"])
                    dv(lambda e: e.tensor_reduce(out=rs[:], in_=sq[:], axis=AX.X, op=ALU.add), ["sq"], [K_("rs")])
                    dv(lambda e: e.tensor_scalar(out=rs[:], in0=rs[:], scalar1=1.0 / 64, scalar2=EPS, op0=ALU.mult, op1=ALU.add), [K_("rs")], [K_("rs")])
                    ac(lambda e: e.activation(out=rs[:], in_=rs[:], func=AF.Ln), [K_("rs")], [K_("rs")])
                    ac(lambda e: e.activation(out=rs[:], in_=rs[:], func=AF.Exp, scale=-0.5), [K_("rs")], [K_("rs")])
                    dv(lambda e: e.tensor_tensor(out=qk3, in0=qk3, in1=rs[:].unsqueeze(2).to_broadcast([128, 20, 64]), op=ALU.mult), qf + [K_("rs")], qf)
                    dv(lambda e: e.tensor_tensor(out=qk3[:, 0:16, :], in0=qk3[:, 0:16, :], in1=gq[:, 0:1, :].to_broadcast([128, 16, 64]), op=ALU.mult),
                       qf + gqk, qf)
                    dv(lambda e: e.tensor_tensor(out=qk3[:, 16:20, :], in0=qk3[:, 16:20, :], in1=gq[:, 1:2, :].to_broadcast([128, 4, 64]), op=ALU.mult),
                       qf + gqk, qf)
                    cb = rope[:, blk, 0:8].unsqueeze(1).to_broadcast([128, 20, 8])
                    sbb = rope[:, blk, 8:16].unsqueeze(1).to_broadcast([128, 20, 8])
                    x1 = qk3[:, :, 0:8]
                    x2 = qk3[:, :, 8:16]
                    gp(lambda e: e.tensor_tensor(out=rt[0][:], in0=x1, in1=cb, op=ALU.mult), qf + ["rope"], ["rt0"])
                    gp(lambda e: e.tensor_tensor(out=rt[1][:], in0=x2, in1=sbb, op=ALU.mult), qf + ["rope"], ["rt1"])
                    gp(lambda e: e.tensor_tensor(out=rt[2][:], in0=x2, in1=cb, op=ALU.mult), qf + ["rope"], ["rt2"])
                    gp(lambda e: e.tensor_tensor(out=rt[3][:], in0=x1, in1=sbb, op=ALU.mult), qf + ["rope"], ["rt3"])
                    gp(lambda e: e.tensor_tensor(out=x1, in0=rt[0][:], in1=rt[1][:], op=ALU.subtract), ["rt0", "rt1"] + qf, qf)
                    gp(lambda e: e.tensor_tensor(out=x2, in0=rt[2][:], in1=rt[3][:], op=ALU.add), ["rt2", "rt3"] + qf, qf)
                    ac(lambda e: e.activation(out=qkb[:], in_=qk3, func=AF.Copy), qf, [K_("qkb")])
                    gp(lambda e: e.tensor_copy(out=kdup[:], in_=qkb[:, 16:20, :].unsqueeze(2).to_broadcast([128, 4, 2, 64])), [K_("qkb")], [K_("kdup")])
                    if blk == 15:
                        P.op("sp", lambda e: e.dma_start(out=pck_d[jl], in_=qkvf[:, 1024:1280]), reads=qf, writes=["pck_o"], dma=True)
                        P.op("sp", lambda e: e.dma_start(out=pcv_d[jl], in_=qkvf[:, 1280:1536]), reads=qf, writes=["pcv_o"], dma=True)
                    if blk == 16:
                        for s_ in range(16):
                            P.op("sp", lambda e, s_=s_: e.dma_start(out=sck_d[jl, s_, 120:128, :], in_=qkvf[s_ * 8:(s_ + 1) * 8, 1024:1280]),
                                 reads=qf, writes=[("sck_b", s_ % 4)], dma=True, dkey="sck_b_dk")
                            P.op("sp", lambda e, s_=s_: e.dma_start(out=scv_d[jl, s_, 120:128, :], in_=qkvf[s_ * 8:(s_ + 1) * 8, 1280:1536]),
                                 reads=qf, writes=[("scv_b", s_ % 4)], dma=True, dkey="scv_b_dk")
                    yield
                    dv(lambda e: e.tensor_copy(out=vaug[p3][:, :, 0:64], in_=qkvf[:, 1280:1536].rearrange("p (h d) -> p h d", d=64)), qf, [("vaug", p3)])
                    psb = PS[7][:].bitcast(BF16)
                    for hp in range(8):
                        pe(lambda e, hp=hp: e.transpose(out=psb[:, hp * 128:(hp + 1) * 128], in_=qkb[:, 2 * hp:2 * hp + 2, :].rearrange("p h d -> p (h d)"),
                                                        identity=identb[:]), [K_("qkb"), "identb"], [("ps", 7)])
                    ac(lambda e: e.activation(out=qT[:].rearrange("p h q -> p (h q)"), in_=psb, func=AF.Copy), [("ps", 7)], [("qT", p3)])
                    for kv in range(4):
                        pe(lambda e, kv=kv: e.transpose(out=psb[:, kv * 128:(kv + 1) * 128], in_=kdup[:, kv, :, :].rearrange("p a d -> p (a d)"),
                                                        identity=identb[:]), [K_("kdup"), "identb"], [("ps", 7)])
                    ac(lambda e: e.activation(out=kTd[p3][:].rearrange("p h q -> p (h q)"), in_=psb[:, 0:512], func=AF.Copy), [("ps", 7)], [("kTd", p3)])
                    yield
                    if blk < 16:
                        if blk > 0:
                            key_block(st_, kTd[q3], ("kTd", q3), vaug[q3], ("vaug", q3), msk[:, 128:256])
                    else:
                        for s_ in range(16):
                            i = s_ % 2
                            P.op("pool", lambda e, s_=s_, i=i: e.dma_start(out=ckb[i][:].rearrange("p h d -> p (h d)"), in_=ck_d[jl, s_]), writes=[("ckb", i)], dma=True)
                            P.op("pool", lambda e, s_=s_, i=i: e.dma_start(out=vas[i][:, :, 0:64], in_=cv_d[jl, s_].rearrange("p (h d) -> p h d", d=64)),
                                 writes=[("vas", i)], dma=True)
                            gp(lambda e, i=i: e.tensor_copy(out=kdups[i][:], in_=ckb[i][:].unsqueeze(2).to_broadcast([128, 4, 2, 64])), [("ckb", i)], [("kdups", i)])
                            for kv in range(4):
                                pe(lambda e, kv=kv, i=i: e.transpose(out=psb[:, kv * 128:(kv + 1) * 128], in_=kdups[i][:, kv, :, :].rearrange("p a d -> p (a d)"),
                                                                     identity=identb[:]), [("kdups", i), "identb"], [("ps", 7)])
                            ac(lambda e, i=i: e.activation(out=kTs[i][:].rearrange("p h q -> p (h q)"), in_=psb[:, 0:512], func=AF.Copy),
                               [("ps", 7)], [("kTs", i)])
                            key_block(st_, kTs[i], ("kTs", i), vas[i], ("vas", i), mskS[:, s_, :])
                    yield
                    key_block(st_, kTd[p3], ("kTd", p3), vaug[p3], ("vaug", p3), msk[:, 0:128] if blk < 16 else msk[:, 256:384])
                    yield
                    oak = ("oacc", p3)
                    dv(lambda e: e.tensor_tensor(out=den[:], in0=oacc[:, :, 64], in1=esink[:], op=ALU.add), [oak, "esink"], ["den"])
                    dv(lambda e: e.reciprocal(out=den[:], in_=den[:]), ["den"], ["den"])
                    dv(lambda e: e.tensor_tensor(out=ob[:], in0=oacc[:, :, 0:64], in1=den[:].unsqueeze(2).to_broadcast([128, 16, 64]), op=ALU.mult),
                       [oak, "den"], ["ob"])
                    for k in range(8):
                        pe(lambda e, k=k: e.transpose(out=psb[:, k * 128:(k + 1) * 128], in_=ob[:, 2 * k:2 * k + 2, :].rearrange("p h d -> p (h d)"),
                                                      identity=identb[:]), ["ob", "identb"], [("ps", 7)])
                    dv(lambda e: e.tensor_copy(out=featT[:, :, cols], in_=psb.rearrange("p (k n) -> p k n", k=8)), [("ps", 7)], [fbk])

                nb = 17
                gens = [block(b_) for b_ in range(nb)]
                NST = 6
                for it_ in range(nb + NST - 1):
                    for s_ in range(NST - 1, -1, -1):
                        b_ = it_ - s_
                        if 0 <= b_ < nb:
                            try:
                                next(gens[b_])
                            except StopIteration:
                                pass
                P.barrier()
            with ExitStack() as ph:
                wo = sb("wo", [128, 8, D], BF16, ph)
                for half in range(2):
                    P.op("pool", lambda e, half=half: e.dma_start(out=wo[:, :, half * 512:(half + 1) * 512],
                                                                  in_=wo_d[jl, :, half * 512:(half + 1) * 512].rearrange("(k p) c -> p k c", p=128)),
                         writes=[("wo", half)], dma=True)
                for rg in range(17):
                    for half in range(2):
                        b = 4 + next_bank(0, 4)
                        for k in range(8):
                            pe(lambda e, k=k, b=b, rg=rg, half=half: e.matmul(PS[b][:, :], lhsT=feat_cols(featT, k, rg), rhs=wo[:, k, half * 512:(half + 1) * 512],
                                                                              start=(k == 0), stop=(k == 7)), [("featT", k), ("wo", half)], [("ps", b)])
                        xk = "XP" if rg < 16 else "XS"
                        xr = Xrow(rg)[:, half * 512:(half + 1) * 512]
                        dv(lambda e, xr=xr, b=b: e.tensor_tensor(out=xr, in0=xr, in1=PS[b][:, :], op=ALU.add), [("ps", b), xk], [xk])
                P.barrier()

        mix_layers = cfg.get("mix_layers", list(range(nlayers)) if do_mix else [])
        ffn_layers = cfg.get("ffn_layers", list(range(nlayers)) if do_ffn else [])
        for L in range(DEPTH):
            if L in mix_layers and L % 2 == 0:
                s5_layer(L)
            if L in mix_layers and L % 2 == 1:
                attn_layer(L)
            if L in ffn_layers:
                ffn(L)

        P.op("sp", lambda e: e.dma_start(out=yp_d.rearrange("(a j t) d -> j a (t d)", a=2, t=8),
                                         in_=XP[:].rearrange("p a t d -> p a (t d)")),
             reads=["XP"], writes=["yp_out"], dma=True)
        P.op("sp", lambda e: e.dma_start(out=ys_d, in_=XS[:]), reads=["XS"], writes=["ys_out"], dma=True)
        P.barrier()
        P.emit()
    return nc


def make_consts():
    c = np.zeros((128, 554), np.float32)
    kvec = np.array([7 - s for s in range(8)] + [-s for s in range(8)] + list(range(9)), np.float32)
    c[:, 0:25] = kvec[None, :]
    c[:, 25:281] = np.arange(256, dtype=np.float32)[None, :]
    p = np.arange(128)
    c[:, 281:409] = ((p[None, :] // 16) >= (p[:, None] // 16)).astype(np.float32)
    c[:, 409] = 1.0
    c[:, 410:538] = ((p[:, None] % 8) == (p[None, :] // 16)).astype(np.float32)
    c[:, 538:554] = ((p[:, None] // 8) == np.arange(16)[None, :]).astype(np.float32)
    return c


SHARED = ["attn_w_qkv", "attn_q_norm", "attn_k_norm", "attn_sinks", "attn_w_o", "norm_mix", "norm_ffn", "ffn_w_gate_up", "ffn_w_down", "ssm_a_re", "ssm_a_im", "ssm_log_dt", "ssm_b_re", "ssm_b_im",
          "ssm_c_re", "ssm_c_im", "ssm_d", "ssm_w_glu"]
_CONST = {}


def core_inputs(inputs, c, shared=None):
    if shared is None:
        shared = {k: np.ascontiguousarray(np.asarray(inputs[k], dtype=np.float32)) for k in SHARED}
    if "ident" not in _CONST:
        _CONST["ident"] = np.eye(128, dtype=np.float32)
        _CONST["cst"] = make_consts()
        p = np.arange(128)
        pos = np.concatenate([np.arange(2048, dtype=np.float32).reshape(16, 128), (8192 + (p % 8)).astype(np.float32)[None, :]], 0)
        inv = (np.float32(500000.0) ** (-np.arange(8, dtype=np.float32) * np.float32(2.0) / np.float32(16))).astype(np.float32)
        ang = (pos[:, :, None] * inv[None, None, :]).astype(np.float32)
        rp = np.concatenate([np.cos(ang), np.sin(ang)], -1).astype(np.float32)
        _CONST["rope"] = np.ascontiguousarray(rp.transpose(1, 0, 2))
        mk = np.zeros((128, 384 + 2048), np.float32)
        mk[:, 0:128] = (p[:, None] <= p[None, :])
        mk[:, 128:256] = (p[:, None] > p[None, :])
        mk[:, 256:384] = ((p[:, None] // 8) == (p[None, :] // 8)) & ((p[:, None] % 8) <= (p[None, :] % 8))
        for s_ in range(16):
            mk[:, 384 + s_ * 128:384 + (s_ + 1) * 128] = ((p[None, :] // 8) == s_) & (p[:, None] > (p[None, :] % 8))
        _CONST["msk"] = mk
    m = dict(shared)
    m.update(_CONST)
    m["xp"] = np.ascontiguousarray(np.asarray(inputs["x_prompt"][c], dtype=np.float32))
    m["xs"] = np.ascontiguousarray(np.asarray(inputs["x_sample"][16 * c:16 * (c + 1)], dtype=np.float32).reshape(128, D))
    m["st_re"] = np.ascontiguousarray(np.asarray(inputs["state_ssm_re"][:, 16 * c:16 * (c + 1)], dtype=np.float32))
    m["st_im"] = np.ascontiguousarray(np.asarray(inputs["state_ssm_im"][:, 16 * c:16 * (c + 1)], dtype=np.float32))
    m["ck"] = np.ascontiguousarray(np.asarray(inputs["cache_swa_k"][:, 16 * c:16 * (c + 1)], dtype=np.float32).reshape(2, 16, 128, 256))
    m["cv"] = np.ascontiguousarray(np.asarray(inputs["cache_swa_v"][:, 16 * c:16 * (c + 1)], dtype=np.float32).reshape(2, 16, 128, 256))
    return m


def kernel(**inputs):
    ncores = 8
    nc = build()
    shared = {k: np.ascontiguousarray(np.asarray(inputs[k], dtype=np.float32)) for k in SHARED}
    in_maps = [core_inputs(inputs, c, shared) for c in range(ncores)]
    res = run_bass_kernel_spmd(nc, in_maps, core_ids=list(range(ncores)))
    R = res.results
    yp = np.stack([R[c]["yp"] for c in range(ncores)], 0)
    ys = np.concatenate([R[c]["ys"].reshape(16, 8, D) for c in range(ncores)], 0)
    p_re = np.stack([R[c]["pst_re"] for c in range(ncores)], 1)
    p_im = np.stack([R[c]["pst_im"] for c in range(ncores)], 1)
    s_re = np.concatenate([R[c]["sst_re"] for c in range(ncores)], 1)
    s_im = np.concatenate([R[c]["sst_im"] for c in range(ncores)], 1)
    p_k = np.stack([R[c]["pck"].reshape(2, 128, 4, 64) for c in range(ncores)], 1)
    p_v = np.stack([R[c]["pcv"].reshape(2, 128, 4, 64) for c in range(ncores)], 1)
    s_k = np.concatenate([R[c]["sck"].reshape(2, 16, 128, 4, 64) for c in range(ncores)], 1)
    s_v = np.concatenate([R[c]["scv"].reshape(2, 16, 128, 4, 64) for c in range(ncores)], 1)
    return yp, ys, p_re, p_im, p_k, p_v, s_re, s_im, s_k, s_v
```

```python
import numpy as np
from contextlib import ExitStack
import concourse.bass as bass
import concourse.mybir as mybir
from concourse.bass_utils import run_bass_kernel_spmd

F32 = mybir.dt.float32
BF16 = mybir.dt.bfloat16
I32 = mybir.dt.int32
ALU = mybir.AluOpType
AF = mybir.ActivationFunctionType
AX = mybir.AxisListType

D = 1024
NTOK = 2176
DFF = 2816
NFT = 22
EPS = 1e-6
DEPTH = 4
PI = float(np.pi)


class _Probe:
    def __getattr__(self, name):
        def f(*a, **k):
            return (name, a, k)
        return f


def _free_size(ap):
    try:
        sh = ap.shape
        n = 1
        for d in sh[1:]:
            n *= int(d)
        return n
    except Exception:
        return 256


class Prog:
    ENGS = ("pe", "dve", "act", "pool", "sp")

    def __init__(self, nc, stack, schedule=True):
        self.nc = nc
        self.stack = stack
        self.schedule = schedule
        self.esem = {e: stack.enter_context(nc.semaphore("es_" + e)) for e in self.ENGS}
        self.dsem = {}
        self.sems = {e: self.esem[e] for e in self.ENGS}
        self.segments = []
        self.seg = []
        self.lastw = {}
        self.readers = {}
        self.nid = 0
        self.handoff = 0.15
        self.cscale = {"pe": 1.0, "dve": 1.0, "act": 1.0, "pool": 1.0, "sp": 1.0}

    def _dma_sem(self, key):
        if key not in self.dsem:
            s = self.stack.enter_context(self.nc.semaphore("ds_%d" % len(self.dsem)))
            self.dsem[key] = [s, 0]
            self.sems[("d", key)] = s
        return self.dsem[key]

    def _cost(self, eng, fn, dma):
        try:
            name, a, k = fn(_Probe())
        except Exception:
            return 0.5
        if dma:
            out = k.get("out", a[0] if a else None)
            n = _free_size(out) if out is not None else 1024
            return 2.5 + n * 128 * 2 / 150e3
        if eng == "pe":
            if name == "transpose":
                return 0.11
            rhs = k.get("rhs", a[2] if len(a) > 2 else None)
            n = _free_size(rhs) if rhs is not None else 128
            return 0.03 + n * 0.00043
        out = k.get("out", a[0] if a else None)
        n = _free_size(out) if out is not None else 256
        if eng == "act":
            return 0.22 + n * 0.00083 + (0.1 if k.get("accum_out") is not None else 0.0)
        if eng == "dve":
            return 0.07 + n * 0.001
        return 0.15 + n * 0.0016

    def op(self, eng, fn, reads=(), writes=(), dma=False, dkey=None):
        preds = set()
        for k in reads:
            w = self.lastw.get(k)
            if w is not None:
                preds.add(w)
        for k in writes:
            w = self.lastw.get(k)
            if w is not None:
                preds.add(w)
            for r in self.readers.get(k, ()):
                preds.add(r)
        node = dict(id=self.nid, eng=eng, fn=fn, dma=dma, preds=preds, reads=tuple(reads), writes=tuple(writes),
                    dk=(dkey if dkey is not None else (writes[0] if writes else reads[0])) if dma else None,
                    cost=self._cost(eng, fn, dma) * self.cscale[eng], ho=self.handoff)
        self.nid += 1
        preds.discard(node["id"])
        for k in reads:
            self.readers.setdefault(k, []).append(node["id"])
        for k in writes:
            self.lastw[k] = node["id"]
            self.readers[k] = []
        self.seg.append(node)
        return node["id"]

    def barrier(self):
        self.segments.append(self.seg)
        self.seg = []
        self.lastw = {}
        self.readers = {}

    def _sched(self, seg, clock):
        import heapq
        if not self.schedule:
            return list(seg)
        byid = {n["id"]: n for n in seg}
        succ = {n["id"]: [] for n in seg}
        indeg = {}
        for n in seg:
            ps = [p for p in n["preds"] if p in byid]
            indeg[n["id"]] = len(ps)
            for p in ps:
                succ[p].append(n["id"])
        fin = {}
        ready = {e: [] for e in self.ENGS}
        dready = {}
        for n in seg:
            if indeg[n["id"]] == 0:
                dready[n["id"]] = 0.0
                heapq.heappush(ready[n["eng"]], (0.0, n["id"]))
        t0 = max(clock.values()) if clock else 0.0
        for e in self.ENGS:
            clock[e] = t0
        order = []
        nleft = len(seg)
        while nleft:
            best = None
            for e in self.ENGS:
                h = ready[e]
                if not h:
                    continue
                dr, i = h[0]
                st = max(clock[e], t0 + dr)
                if best is None or (st, i) < (best[0], best[1]):
                    best = (st, i, e)
            st, i, e = best
            heapq.heappop(ready[e])
            n = byid[i]
            if n["dma"]:
                clock[e] = st + 0.3
                fin[i] = st + n["cost"]
            else:
                clock[e] = st + n["cost"]
                fin[i] = clock[e] + n["ho"]
            order.append(n)
            nleft -= 1
            for s_ in succ[i]:
                dready[s_] = max(dready.get(s_, 0.0), fin[i] - t0)
                indeg[s_] -= 1
                if indeg[s_] == 0:
                    heapq.heappush(ready[byid[s_]["eng"]], (dready[s_], s_))
        return order

    def emit(self):
        nc = self.nc
        if self.seg:
            self.segments.append(self.seg)
            self.seg = []
        lists = {e: [] for e in self.ENGS}
        cnt = {e: 0 for e in self.ENGS}
        waited = {e: {} for e in self.ENGS}
        tok = {}
        clock = {}

        def need(eng, dep, waits, pe_waw):
            semkey, val = dep
            if semkey == eng and eng == "pe" and pe_waw:
                return
            if waited[eng].get(semkey, 0) >= val:
                return
            waited[eng][semkey] = val
            waits.append((self.sems[semkey], val))

        for seg in self.segments:
            for n in self._sched(seg, clock):
                eng = n["eng"]
                waits = []
                wset = set(n["writes"])
                for p in sorted(n["preds"]):
                    t = tok.get(p)
                    if t is None:
                        continue
                    need(eng, t, waits, pe_waw=not n["dma"])
                if n["dma"]:
                    ds = self._dma_sem(n["dk"])
                    ds[1] += 16
                    tok[n["id"]] = (("d", n["dk"]), ds[1])
                    inc = (ds[0], 16)
                else:
                    cnt[eng] += 1
                    tok[n["id"]] = (eng, cnt[eng])
                    inc = (self.esem[eng], 1)
                lists[eng].append((waits, n["fn"], inc))
            for e in self.ENGS:
                waits = []
                for o in self.ENGS:
                    if cnt[o] > waited[e].get(o, 0):
                        waited[e][o] = cnt[o]
                        waits.append((self.esem[o], cnt[o]))
                for k, (s, c) in self.dsem.items():
                    sk = ("d", k)
                    if c > waited[e].get(sk, 0):
                        waited[e][sk] = c
                        waits.append((s, c))
                if waits:
                    lists[e].append((waits, None, None))
            tok = {}
        with nc.Block() as block:
            def run(e, name):
                for waits, fn, inc in lists[name]:
                    for s, v in waits:
                        e.wait_ge(s, v)
                    if fn is not None:
                        fn(e).then_inc(inc[0], inc[1])

            @block.tensor
            def _(e):
                run(e, "pe")

            @block.vector
            def _(e):
                run(e, "dve")

            @block.scalar
            def _(e):
                run(e, "act")

            @block.gpsimd
            def _(e):
                run(e, "pool")

            @block.sync
            def _(e):
                run(e, "sp")


def build(cfg=None):
    cfg = cfg or {}
    do_mix = cfg.get("mix", True)
    do_ffn = cfg.get("ffn", True)
    nlayers = cfg.get("layers", DEPTH)
    nc = bass.Bass("TRN2", target_bir_lowering=False)

    def din(name, shape):
        return nc.dram_tensor(name, list(shape), F32, kind="ExternalInput").ap()

    def dout(name, shape):
        return nc.dram_tensor(name, list(shape), F32, kind="ExternalOutput").ap()

    xp_d = din("xp", [2048, D])
    xs_d = din("xs", [128, D])
    norm_mix_d = din("norm_mix", [4, D])
    norm_ffn_d = din("norm_ffn", [4, D])
    wgu_d = din("ffn_w_gate_up", [4, D, 2 * DFF])
    wdn_d = din("ffn_w_down", [4, DFF, D])
    ident_d = din("ident", [128, 128])
    cst_d = din("cst", [128, 554])
    a_re_d = din("ssm_a_re", [2, 64, 64])
    a_im_d = din("ssm_a_im", [2, 64, 64])
    ldt_d = din("ssm_log_dt", [2, 64])
    b_re_d = din("ssm_b_re", [2, 64, 64, 16])
    b_im_d = din("ssm_b_im", [2, 64, 64, 16])
    c_re_d = din("ssm_c_re", [2, 64, 16, 64])
    c_im_d = din("ssm_c_im", [2, 64, 16, 64])
    ssm_d_d = din("ssm_d", [2, D])
    wglu_d = din("ssm_w_glu", [2, D, 2 * D])
    st_re_d = din("st_re", [2, 16, 64, 64])
    wqkv_d = din("attn_w_qkv", [2, D, 1536])
    qnorm_d = din("attn_q_norm", [2, 64])
    knorm_d = din("attn_k_norm", [2, 64])
    sinks_d = din("attn_sinks", [2, 16])
    wo_d = din("attn_w_o", [2, D, D])
    ck_d = din("ck", [2, 16, 128, 256])
    cv_d = din("cv", [2, 16, 128, 256])
    rope_d = din("rope", [128, 17, 16])
    msk_d = din("msk", [128, 384 + 2048])
    pck_d = dout("pck", [2, 128, 256])
    pcv_d = dout("pcv", [2, 128, 256])
    sck_d = dout("sck", [2, 16, 128, 256])
    scv_d = dout("scv", [2, 16, 128, 256])
    st_im_d = din("st_im", [2, 16, 64, 64])
    pst_re_d = dout("pst_re", [2, 64, 64])
    pst_im_d = dout("pst_im", [2, 64, 64])
    sst_re_d = dout("sst_re", [2, 16, 64, 64])
    sst_im_d = dout("sst_im", [2, 16, 64, 64])
    yp_d = dout("yp", [2048, D])
    ys_d = dout("ys", [128, D])

    with ExitStack() as st:
        P = Prog(nc, st, schedule=cfg.get("sched", True))
        CS_ONE = {"pe": 1.0, "dve": 1.0, "act": 1.0, "pool": 1.0, "sp": 1.0}
        CS_ATT = dict(CS_ONE, act=2.0, dve=2.0, pool=2.0)
        CS_ATT.update(cfg.get("cs_att", {}))
        CS_FFN = dict(CS_ONE, act=3.0, dve=3.0)
        CS_FFN.update(cfg.get("cs_ffn", {}))
        CS_S5 = dict(CS_ONE)
        CS_S5.update(cfg.get("cs_s5", {}))
        st.enter_context(nc.allow_non_contiguous_dma(reason="small strided parameter loads"))

        NM = {}
        uid = [0]
        used = [0]

        def sb(name, shape, dt, stack=st):
            uid[0] += 1
            full = "%s_u%d" % (name, uid[0])
            NM[full] = name
            t = stack.enter_context(nc.sbuf_tensor(full, list(shape), dt))
            if cfg.get("memdbg"):
                n = 1
                for d_ in shape[1:]:
                    n *= d_
                n *= 2 if dt == BF16 else 4
                used[0] += n
                stack.callback(lambda n=n: used.__setitem__(0, used[0] - n))
                print("SB %-10s %7d B  total %7d" % (name, n, used[0]))
            return t

        XP = sb("XP", [128, 2, 8, D], F32)
        XS = sb("XS", [128, D], F32)
        featT = sb("featT", [128, 8, NTOK], BF16)
        gains = sb("gains", [128, 8, 8], F32)
        identb = sb("identb", [128, 128], BF16)
        identf = sb("identf", [128, 128], F32)
        ssq = sb("ssq", [128, 17], F32)
        cst = sb("cst_sb", [128, 554], F32)
        maskM = cst[:, 281:409]
        rope = sb("rope_sb", [128, 17, 16], F32)
        msk = sb("msk_sb", [128, 384], BF16)
        mskS = sb("mskS", [128, 16, 128], BF16)
        maskrep = sb("maskrep", [128, 128], BF16)
        selb = sb("selb", [128, 16], BF16)
        rstd = sb("rstd", [128, 17], F32)
        PSALL = st.enter_context(nc.psum_tensor("psall", [128, 4096], F32))
        PS = [PSALL[:, i * 512:(i + 1) * 512] for i in range(8)]

        V = {}

        def alloc_scr(stack):
            SCR = sb("SCR", [128, 17408], BF16, stack)
            V["SCR"] = SCR
            V["XSJ"] = SCR[:, 0:16384].rearrange("p (a t d) -> p a t d", a=2, t=8)
            V["XSS"] = SCR[:, 16384:17408]
            V["ACTB"] = SCR[:, :].rearrange("p (f n) -> p f n", f=8)

        def Xrow(rg):
            return XP[:, rg // 8, rg % 8, :] if rg < 16 else XS[:, :]

        def XSrow(rg):
            return V["XSJ"][:, rg // 8, rg % 8, :] if rg < 16 else V["XSS"]

        def feat_cols(t3, kt_or_f, rg):
            if rg < 16:
                jt, t = rg // 8, rg % 8
                return t3[:, kt_or_f, jt * 1024:(jt + 1) * 1024].rearrange("p (j t) -> p t j", t=8)[:, t, :]
            return t3[:, kt_or_f, 2048:2176]

        P.op("sp", lambda e: e.dma_start(out=XP[:].rearrange("p a t d -> p a (t d)"),
                                         in_=xp_d.rearrange("(a j t) d -> j a (t d)", a=2, t=8)),
             writes=["XP"], dma=True)
        P.op("sp", lambda e: e.dma_start(out=XS[:], in_=xs_d), writes=["XS"], dma=True)
        P.op("sp", lambda e: e.dma_start(out=identf[:], in_=ident_d), writes=["identf"], dma=True)
        P.op("pool", lambda e: e.dma_start(out=identb[:], in_=ident_d), writes=["identb"], dma=True)
        P.op("sp", lambda e: e.dma_start(out=cst[:], in_=cst_d), writes=["cst"], dma=True)
        P.op("sp", lambda e: e.dma_start(out=rope[:], in_=rope_d), writes=["rope"], dma=True)
        P.op("pool", lambda e: e.dma_start(out=msk[:], in_=msk_d[:, 0:384]), writes=["masks"], dma=True)
        P.op("pool", lambda e: e.dma_start(out=mskS[:].rearrange("p s q -> p (s q)"), in_=msk_d[:, 384:2432]), writes=["masks"], dma=True)
        P.op("pool", lambda e: e.dma_start(out=maskrep[:], in_=cst_d[:, 410:538]), writes=["maskrep"], dma=True)
        P.op("pool", lambda e: e.dma_start(out=selb[:], in_=cst_d[:, 538:554]), writes=["selb"], dma=True)
        for i in range(4):
            P.op("sp", lambda e, i=i: e.dma_start(out=gains[:, i, :], in_=norm_mix_d[i].rearrange("(k p) -> p k", p=128)),
                 writes=["gains"], dma=True)
            P.op("sp", lambda e, i=i: e.dma_start(out=gains[:, 4 + i, :], in_=norm_ffn_d[i].rearrange("(k p) -> p k", p=128)),
                 writes=["gains"], dma=True)

        bank_rr = [0]

        def next_bank(lo=0, hi=8):
            b = lo + bank_rr[0] % (hi - lo)
            bank_rr[0] += 1
            return b

        def norm(gi, to_feat=True, gmajor=False):
            XSJ, XSS = V["XSJ"], V["XSS"]
            Xr, XSr = Xrow, XSrow
            if gmajor:
                XSG = XSJ.rearrange("p a t (g c) -> p a g t c", c=16)
                XSGm = V["SCR"][:, 0:16384].rearrange("p (a g t c) -> p a g t c", a=2, g=64, t=8)

                def XSr(rg):
                    return XSGm[:, rg // 8, :, rg % 8, :] if rg < 16 else XSS

                def Xr(rg):
                    return XP[:, rg // 8, rg % 8, :].rearrange("p (g c) -> p g c", c=16) if rg < 16 else XS[:, :]
            parts = [list(range(0, 8)), list(range(8, 16)), [16]]
            for pi, part in enumerate(parts):
                r0, r1 = part[0], part[-1] + 1
                sk, rk = ("ssq", pi), ("rstd", pi)
                for rg in part:
                    xk = "XP" if rg < 16 else "XS"
                    P.op("act", lambda e, rg=rg: e.activation(out=XSr(rg), in_=Xr(rg), func=AF.Square, accum_out=ssq[:, rg:rg + 1]),
                         reads=[xk], writes=[("xs", rg), sk])
                P.op("act", lambda e, r0=r0, r1=r1: e.activation(out=rstd[:, r0:r1], in_=ssq[:, r0:r1], func=AF.Sqrt, scale=1.0 / D, bias=EPS),
                     reads=[sk], writes=[rk])
                P.op("dve", lambda e, r0=r0, r1=r1: e.reciprocal(out=rstd[:, r0:r1], in_=rstd[:, r0:r1]), reads=[rk], writes=[rk])
                for rg in part:
                    xk = "XP" if rg < 16 else "XS"
                    if rg % 2 == 0:
                        P.op("dve", lambda e, rg=rg: e.tensor_scalar(out=XSr(rg), in0=Xr(rg), scalar1=rstd[:, rg:rg + 1], scalar2=None, op0=ALU.mult),
                             reads=[xk, rk], writes=[("xs", rg)])
                    else:
                        P.op("act", lambda e, rg=rg: e.activation(out=XSr(rg), in_=Xr(rg), func=AF.Copy, scale=rstd[:, rg:rg + 1]),
                             reads=[xk, rk], writes=[("xs", rg)])
                if not to_feat:
                    continue
                if pi < 2:
                    jt = pi
                    for kt in range(8):
                        b = next_bank(0, 4)
                        psb = PS[b][:].bitcast(BF16)
                        for t in range(8):
                            P.op("pe", lambda e, t=t, psb=psb, jt=jt, kt=kt: e.transpose(
                                out=psb[:, t * 128:(t + 1) * 128], in_=XSJ[:, jt, t, kt * 128:(kt + 1) * 128], identity=identb[:]),
                                 reads=[("xs", jt * 8 + t), "identb"], writes=[("ps", b)])
                        P.op("dve", lambda e, psb=psb, jt=jt, kt=kt: e.tensor_scalar(
                            out=featT[:, kt, jt * 1024:(jt + 1) * 1024].rearrange("p (j t) -> p j t", t=8),
                            in0=psb.rearrange("p (t j) -> p j t", t=8),
                            scalar1=gains[:, gi, kt:kt + 1], scalar2=None, op0=ALU.mult),
                             reads=[("ps", b), "gains"], writes=[("featT", kt), ("featTp", kt, pi)])
                else:
                    b = next_bank(0, 4)
                    psb = PS[b][:].bitcast(BF16)
                    for kt in range(8):
                        P.op("pe", lambda e, kt=kt, psb=psb: e.transpose(
                            out=psb[:, kt * 128:(kt + 1) * 128], in_=XSS[:, kt * 128:(kt + 1) * 128], identity=identb[:]),
                             reads=[("xs", 16), "identb"], writes=[("ps", b)])
                    P.op("dve", lambda e, psb=psb: e.tensor_tensor(
                        out=featT[:, :, 2048:2176], in0=psb.rearrange("p (k n) -> p k n", k=8),
                        in1=gains[:, gi, :].unsqueeze(2).to_broadcast([128, 8, 128]), op=ALU.mult),
                         reads=[("ps", b), "gains"], writes=[("featT", k) for k in range(8)] + [("featTp", k, 2) for k in range(8)])

        def ffn(L):
            P.handoff = 0.5
            P.cscale = dict(CS_FFN)
            with ExitStack() as ph:
                alloc_scr(ph)
                ACTB = V["ACTB"]
                wst = [sb("wst%d" % i, [128, 8, 256], BF16, ph) for i in range(2)]
                wd = sb("wd", [128, 8, D], BF16, ph)
                tmp = [sb("ftmp%d" % i, [128, 512], F32, ph) for i in range(2)]
                norm(4 + L)
                XSK = [("xs", rg_) for rg_ in range(17)]
                groups = [(0, 8), (8, 15), (15, 22)]
                it = 0
                for (f0, f1) in groups:
                    for fl, f in enumerate(range(f0, f1)):
                        P.op("pool", lambda e, fl=fl, f=f: e.dma_start(out=wd[:, fl, :], in_=wdn_d[L, f * 128:(f + 1) * 128, :]),
                             writes=[("wd", fl)], dma=True)
                    for fl, f in enumerate(range(f0, f1)):
                        buf = wst[f % 2]
                        bk = ("wstg", f % 2)
                        bk2 = ("wstu", f % 2)
                        P.op("pool", lambda e, buf=buf, f=f: e.dma_start(
                            out=buf[:, :, 0:128], in_=wgu_d[L, :, f * 128:(f + 1) * 128].rearrange("(k p) c -> p k c", p=128)),
                             writes=[bk], dma=True)
                        P.op("pool", lambda e, buf=buf, f=f: e.dma_start(
                            out=buf[:, :, 128:256], in_=wgu_d[L, :, DFF + f * 128:DFF + (f + 1) * 128].rearrange("(k p) c -> p k c", p=128)),
                             writes=[bk2], dma=True)
                        for c0 in range(0, NTOK, 512):
                            n = min(512, NTOK - c0)
                            bg = 2 * (it % 2)
                            bu = bg + 1
                            tp = tmp[it % 2]
                            tk = ("ftmp", it % 2)
                            it += 1
                            for kt in range(8):
                                P.op("pe", lambda e, kt=kt, bg=bg, buf=buf, c0=c0, n=n: e.matmul(
                                    PS[bg][:, 0:n], lhsT=buf[:, kt, 0:128], rhs=featT[:, kt, c0:c0 + n],
                                    start=(kt == 0), stop=(kt == 7)),
                                     reads=[bk, ("featTp", kt, min(c0 // 1024, 2))], writes=[("ps", bg)])
                            for kt in range(8):
                                P.op("pe", lambda e, kt=kt, bu=bu, buf=buf, c0=c0, n=n: e.matmul(
                                    PS[bu][:, 0:n], lhsT=buf[:, kt, 128:256], rhs=featT[:, kt, c0:c0 + n],
                                    start=(kt == 0), stop=(kt == 7)),
                                     reads=[bk2, ("featTp", kt, min(c0 // 1024, 2))], writes=[("ps", bu)])
                            P.op("act", lambda e, bg=bg, tp=tp, n=n: e.activation(out=tp[:, 0:n], in_=PS[bg][:, 0:n], func=AF.Silu),
                                 reads=[("ps", bg)], writes=[tk])
                            P.op("dve", lambda e, bu=bu, tp=tp, n=n, fl=fl, c0=c0: e.tensor_tensor(
                                out=ACTB[:, fl, c0:c0 + n], in0=tp[:, 0:n], in1=PS[bu][:, 0:n], op=ALU.mult),
                                 reads=[tk, ("ps", bu)], writes=[("actb", fl)] + XSK)
                    nfl = f1 - f0
                    for rg in range(17):
                        for half in range(2):
                            b = 4 + next_bank(0, 4)
                            for fl in range(nfl):
                                P.op("pe", lambda e, fl=fl, b=b, rg=rg, half=half: e.matmul(
                                    PS[b][:, :], lhsT=feat_cols(ACTB, fl, rg), rhs=wd[:, fl, half * 512:(half + 1) * 512],
                                    start=(fl == 0), stop=(fl == nfl - 1)),
                                     reads=[("actb", fl), ("wd", fl)], writes=[("ps", b)])
                            xk = "XP" if rg < 16 else "XS"
                            P.op("dve", lambda e, b=b, rg=rg, half=half: e.tensor_tensor(
                                out=Xrow(rg)[:, half * 512:(half + 1) * 512], in0=Xrow(rg)[:, half * 512:(half + 1) * 512],
                                in1=PS[b][:, :], op=ALU.add),
                                 reads=[("ps", b), xk], writes=[xk])
                P.barrier()

        C1 = 6.28125
        C2 = float(2.0 * np.pi - 6.28125)
        INV2PI = float(1.0 / (2.0 * np.pi))

        def dv(fn, r, w):
            P.op("dve", fn, reads=r, writes=w)

        def ac(fn, r, w):
            P.op("act", fn, reads=r, writes=w)

        def gp(fn, r, w):
            P.op("pool", fn, reads=r, writes=w)

        def pe(fn, r, w):
            P.op("pe", fn, reads=r, writes=w)

        def wrap(x, xk, ki, kf, tag):
            dv(lambda e: e.tensor_scalar(out=ki, in0=x, scalar1=INV2PI, scalar2=None, op0=ALU.mult), [xk], [tag + "ki"])
            dv(lambda e: e.tensor_copy(out=kf, in_=ki), [tag + "ki"], [tag + "kf"])
            dv(lambda e: e.scalar_tensor_tensor(out=x, in0=kf, scalar=-C1, in1=x, op0=ALU.mult, op1=ALU.add), [tag + "kf", xk], [xk])
            dv(lambda e: e.scalar_tensor_tensor(out=x, in0=kf, scalar=-C2, in1=x, op0=ALU.mult, op1=ALU.add), [tag + "kf", xk], [xk])
            dv(lambda e: e.tensor_scalar(out=x, in0=x, scalar1=PI, scalar2=-PI, op0=ALU.min, op1=ALU.max), [xk], [xk])

        def sincos(sn, cs, x, xk, snk, csk):
            ac(lambda e: e.activation(out=sn, in_=x, func=AF.Sin), [xk], [snk])
            ac(lambda e: e.activation(out=cs, in_=x, func=AF.Sin, scale=0.5), [xk], [csk])
            dv(lambda e: e.tensor_tensor(out=cs, in0=cs, in1=cs, op=ALU.mult), [csk], [csk])
            dv(lambda e: e.tensor_scalar(out=cs, in0=cs, scalar1=-2.0, scalar2=1.0, op0=ALU.mult, op1=ALU.add), [csk], [csk])

        def s5_layer(L):
            P.handoff = 0.5
            P.cscale = dict(CS_S5)
            jl = L // 2
            Uv = featT[:].rearrange("p k (g n) -> p k g n", n=272)
            with ExitStack() as ph:
                def t128(name, n, dt=F32):
                    return sb(name, [128, n], dt, ph)
                are, aim, ldt, adt, thr, kf0, Rg, Th8, Lre, Lim, nLim, Th128 = [t128("s5_%d" % i, 32) for i in range(12)]
                lre, lim, fre, fim, den, tq1, tq2 = [t128("s5b_%d" % i, 32) for i in range(7)]
                ki0 = t128("s5ki0", 32, I32)
                bbre = sb("bbre", [128, 32, 16], F32, ph)
                bbim = sb("bbim", [128, 32, 16], F32, ph)
                cre = sb("cre", [128, 32, 16], F32, ph)
                cim = sb("cim", [128, 32, 16], F32, ph)
                gdU = sb("gdU", [128, 64], F32, ph)
                dU = sb("dU", [128, 64], F32, ph)
                Pre = sb("Pre", [128, 32], F32, ph)
                Pim = sb("Pim", [128, 32], F32, ph)
                phb = ExitStack()
                bre = sb("bre", [128, 32, 16], F32, phb)
                bim = sb("bim", [128, 32, 16], F32, phb)
                gainPG = sb("gainPG", [128, 32, 16], F32, phb)
                for g2 in range(2):
                    hs_ = slice(g2 * 64, (g2 + 1) * 64)
                    P.op("sp", lambda e, g2=g2, hs_=hs_: e.dma_start(out=are[hs_, :], in_=a_re_d[jl].rearrange("(r t) p -> t p r", t=2)[g2]), writes=[("are", g2)], dma=True, dkey="are_dk")
                    P.op("sp", lambda e, g2=g2, hs_=hs_: e.dma_start(out=aim[hs_, :], in_=a_im_d[jl].rearrange("(r t) p -> t p r", t=2)[g2]), writes=[("aim", g2)], dma=True, dkey="aim_dk")
                    P.op("sp", lambda e, g2=g2, hs_=hs_: e.dma_start(out=ldt[hs_, :], in_=ldt_d[jl:jl + 1, :].rearrange("o (r t) -> o t r", t=2)[:, g2].to_broadcast([64, 32])),
                         writes=[("ldt", g2)], dma=True, dkey="ldt_dk")
                    P.op("sp", lambda e, g2=g2, hs_=hs_: e.dma_start(out=bre[hs_, :, :], in_=b_re_d[jl].rearrange("(r t) p c -> t p r c", t=2)[g2]), writes=[("bre", g2)], dma=True, dkey="bre_dk")
                    P.op("sp", lambda e, g2=g2, hs_=hs_: e.dma_start(out=bim[hs_, :, :], in_=b_im_d[jl].rearrange("(r t) p c -> t p r c", t=2)[g2]), writes=[("bim", g2)], dma=True, dkey="bim_dk")
                    P.op("sp", lambda e, g2=g2, hs_=hs_: e.dma_start(
                        out=gainPG[hs_, :, :], in_=norm_mix_d[L:L + 1, :].rearrange("o (r t c) -> o t r c", t=2, c=16)[:, g2].to_broadcast([64, 32, 16])),
                         writes=[("gainPG", g2)], dma=True, dkey="gainPG_dk")
                AREK, AIMK, LDTK = [("are", 0), ("are", 1)], [("aim", 0), ("aim", 1)], [("ldt", 0), ("ldt", 1)]
                BREK, BIMK, GPGK = [("bre", 0), ("bre", 1)], [("bim", 0), ("bim", 1)], [("gainPG", 0), ("gainPG", 1)]
                for s_ in range(8):
                    P.op("sp", lambda e, s_=s_: e.dma_start(out=gdU[s_ * 16:(s_ + 1) * 16, :], in_=norm_mix_d[L].rearrange("(g c) -> c g", c=16)),
                         writes=[("gdU", s_)], dma=True, dkey="gdU_dk")
                    P.op("sp", lambda e, s_=s_: e.dma_start(out=dU[s_ * 16:(s_ + 1) * 16, :], in_=ssm_d_d[jl].rearrange("(g c) -> c g", c=16)),
                         writes=[("dU", s_)], dma=True, dkey="dU_dk")
                with ExitStack() as phc:
                    cn_re = sb("cn_re", [64, 8, 128], F32, phc)
                    cn_im = sb("cn_im", [64, 8, 128], F32, phc)
                    for (cn, cnk, src) in ((cn_re, "cn_re", c_re_d), (cn_im, "cn_im", c_im_d)):
                        for pl in range(4):
                            for g2 in range(2):
                                P.op("sp", lambda e, cn=cn, src=src, pl=pl, g2=g2: e.dma_start(
                                    out=cn[pl * 16:(pl + 1) * 16, :, g2 * 64:(g2 + 1) * 64],
                                    in_=src[jl].rearrange("(gb l t) c p -> l t c gb p", l=4, t=2)[pl, g2]), writes=[(cnk, pl, g2)], dma=True, dkey=cnk + "_dk")
                    with ExitStack() as ph0:
                        alloc_scr(ph0)
                        XSJ, XSS = V["XSJ"], V["XSS"]
                        xrep = sb("xrep", [128, 8, 128], BF16, ph0)
                        norm(L, to_feat=False, gmajor=True)
                        XSGm = V["SCR"][:, 0:16384].rearrange("p (a g n) -> p a g n", a=2, g=64)
                        for jt in range(2):
                            for kt in range(8):
                                b = next_bank(0, 4)
                                psb = PS[b][:].bitcast(BF16)
                                for gl in range(8):
                                    g = kt * 8 + gl
                                    P.op("pe", lambda e, psb=psb, gl=gl, g=g, jt=jt: e.transpose(
                                        out=psb[:, gl * 128:(gl + 1) * 128], in_=XSGm[:, jt, g, :], identity=identb[:]),
                                         reads=[("xs", jt * 8 + t) for t in range(8)] + ["identb"], writes=[("ps", b)])
                                P.op("act", lambda e, psb=psb, jt=jt, kt=kt: e.activation(
                                    out=Uv[:, kt, :, jt * 128:(jt + 1) * 128], in_=psb.rearrange("p (g n) -> p g n", g=8), func=AF.Copy),
                                     reads=[("ps", b)], writes=[("featT", kt)])
                        for kt in range(8):
                            dv(lambda e, kt=kt: e.tensor_tensor(
                                out=xrep[:].rearrange("p g (s c) -> p g s c", s=8),
                                in0=XSS[:, kt * 128:(kt + 1) * 128].rearrange("p (g c) -> p g c", c=16).unsqueeze(2).to_broadcast([128, 8, 8, 16]),
                                in1=maskrep[:].rearrange("p (s c) -> p s c", s=8).unsqueeze(1).to_broadcast([128, 8, 8, 16]), op=ALU.mult),
                               [("xs", 16), "maskrep"], ["xrep"])
                            b = next_bank(0, 4)
                            for gl in range(8):
                                pe(lambda e, b=b, gl=gl: e.matmul(PS[b][:, gl * 16:(gl + 1) * 16], lhsT=xrep[:, gl, :], rhs=selb[:],
                                                                   start=True, stop=True), ["xrep", "selb"], [("ps", b)])
                            ac(lambda e, b=b, kt=kt: e.activation(out=Uv[:, kt, :, 256:272], in_=PS[b][:, 0:128].rearrange("p (g n) -> p g n", g=8),
                                                                  func=AF.Copy), [("ps", b)], [("featT", kt)])
                        P.barrier()
                    for (cn, cnk, cc, cck) in ((cn_re, "cn_re", cre, "cre"), (cn_im, "cn_im", cim, "cim")):
                        allk = [(cnk, pl, g2) for pl in range(4) for g2 in range(2)]
                        for gb in range(8):
                            b = next_bank(0, 4)
                            pe(lambda e, b=b, cn=cn, gb=gb: e.transpose(out=PS[b][:, 0:64], in_=cn[:, gb, :], identity=identf[0:64, 0:64]),
                               allk + ["identf"], [("ps", b)])
                            dv(lambda e, b=b, cc=cc, gb=gb: e.tensor_copy(out=cc[:, gb * 4:(gb + 1) * 4, :].rearrange("p g c -> p (g c)"),
                                                                          in_=PS[b][:, 0:64]), [("ps", b)], [cck])
                    P.barrier()
                dv(lambda e: e.tensor_tensor(out=gdU[:], in0=gdU[:], in1=dU[:], op=ALU.mult), [("gdU", s_) for s_ in range(8)] + [("dU", s_) for s_ in range(8)], ["gdU"])
                ac(lambda e: e.activation(out=tq1[:], in_=ldt[:], func=AF.Exp), LDTK, ["tq1"])
                dv(lambda e: e.tensor_tensor(out=adt[:], in0=are[:], in1=tq1[:], op=ALU.mult), AREK + ["tq1"], ["adt"])
                dv(lambda e: e.tensor_tensor(out=thr[:], in0=aim[:], in1=tq1[:], op=ALU.mult), AIMK + ["tq1"], ["thr"])
                wrap(thr[:], "thr", ki0[:], kf0[:], "w0")
                sincos(lim[:], lre[:], thr[:], "thr", "lim", "lre")
                ac(lambda e: e.activation(out=tq2[:], in_=adt[:], func=AF.Exp), ["adt"], ["tq2"])
                dv(lambda e: e.tensor_tensor(out=lre[:], in0=lre[:], in1=tq2[:], op=ALU.mult), ["lre", "tq2"], ["lre"])
                dv(lambda e: e.tensor_tensor(out=lim[:], in0=lim[:], in1=tq2[:], op=ALU.mult), ["lim", "tq2"], ["lim"])
                dv(lambda e: e.tensor_scalar(out=Th8[:], in0=thr[:], scalar1=8.0, scalar2=None, op0=ALU.mult), ["thr"], ["Th8"])
                wrap(Th8[:], "Th8", ki0[:], kf0[:], "w0")
                ac(lambda e: e.activation(out=Rg[:], in_=adt[:], func=AF.Exp, scale=8.0), ["adt"], ["Rg"])
                sincos(Lim[:], Lre[:], Th8[:], "Th8", "Lim", "Lre")
                dv(lambda e: e.tensor_tensor(out=Lre[:], in0=Lre[:], in1=Rg[:], op=ALU.mult), ["Lre", "Rg"], ["Lre"])
                dv(lambda e: e.tensor_tensor(out=Lim[:], in0=Lim[:], in1=Rg[:], op=ALU.mult), ["Lim", "Rg"], ["Lim"])
                dv(lambda e: e.tensor_scalar(out=nLim[:], in0=Lim[:], scalar1=-1.0, scalar2=None, op0=ALU.mult), ["Lim"], ["nLim"])
                dv(lambda e: e.tensor_scalar(out=Th128[:], in0=Th8[:], scalar1=16.0, scalar2=None, op0=ALU.mult), ["Th8"], ["Th128"])
                wrap(Th128[:], "Th128", ki0[:], kf0[:], "w0")
                dv(lambda e: e.tensor_tensor(out=den[:], in0=are[:], in1=are[:], op=ALU.mult), AREK, ["den"])
                dv(lambda e: e.tensor_tensor(out=tq1[:], in0=aim[:], in1=aim[:], op=ALU.mult), AIMK, ["tq1"])
                dv(lambda e: e.tensor_tensor(out=den[:], in0=den[:], in1=tq1[:], op=ALU.add), ["den", "tq1"], ["den"])
                dv(lambda e: e.reciprocal(out=den[:], in_=den[:]), ["den"], ["den"])
                dv(lambda e: e.tensor_scalar(out=tq2[:], in0=lre[:], scalar1=-1.0, scalar2=None, op0=ALU.add), ["lre"], ["tq2"])
                dv(lambda e: e.tensor_tensor(out=fre[:], in0=tq2[:], in1=are[:], op=ALU.mult), ["tq2"] + AREK, ["fre"])
                dv(lambda e: e.tensor_tensor(out=tq1[:], in0=lim[:], in1=aim[:], op=ALU.mult), ["lim"] + AIMK, ["tq1"])
                dv(lambda e: e.tensor_tensor(out=fre[:], in0=fre[:], in1=tq1[:], op=ALU.add), ["fre", "tq1"], ["fre"])
                dv(lambda e: e.tensor_tensor(out=fre[:], in0=fre[:], in1=den[:], op=ALU.mult), ["fre", "den"], ["fre"])
                dv(lambda e: e.tensor_tensor(out=fim[:], in0=lim[:], in1=are[:], op=ALU.mult), ["lim"] + AREK, ["fim"])
                dv(lambda e: e.tensor_tensor(out=tq1[:], in0=tq2[:], in1=aim[:], op=ALU.mult), ["tq2"] + AIMK, ["tq1"])
                dv(lambda e: e.tensor_tensor(out=fim[:], in0=fim[:], in1=tq1[:], op=ALU.subtract), ["fim", "tq1"], ["fim"])
                dv(lambda e: e.tensor_tensor(out=fim[:], in0=fim[:], in1=den[:], op=ALU.mult), ["fim", "den"], ["fim"])
                freb = fre[:].unsqueeze(2).to_broadcast([128, 32, 16])
                fimb = fim[:].unsqueeze(2).to_broadcast([128, 32, 16])
                dv(lambda e: e.tensor_tensor(out=bbre[:], in0=bre[:], in1=freb, op=ALU.mult), BREK + ["fre"], ["bbre"])
                dv(lambda e: e.tensor_tensor(out=bbim[:], in0=bim[:], in1=fimb, op=ALU.mult), BIMK + ["fim"], ["bbim"])
                dv(lambda e: e.tensor_tensor(out=bbre[:], in0=bbre[:], in1=bbim[:], op=ALU.subtract), ["bbre", "bbim"], ["bbre"])
                dv(lambda e: e.tensor_tensor(out=bbim[:], in0=bim[:], in1=freb, op=ALU.mult), BIMK + ["fre"], ["bbim"])
                dv(lambda e: e.tensor_tensor(out=bim[:], in0=bre[:], in1=fimb, op=ALU.mult), BREK + BIMK + ["fim"], BIMK)
                dv(lambda e: e.tensor_tensor(out=bbim[:], in0=bbim[:], in1=bim[:], op=ALU.add), ["bbim"] + BIMK, ["bbim"])
                dv(lambda e: e.tensor_tensor(out=bbre[:], in0=bbre[:], in1=gainPG[:], op=ALU.mult), ["bbre"] + GPGK, ["bbre"])
                dv(lambda e: e.tensor_tensor(out=bbim[:], in0=bbim[:], in1=gainPG[:], op=ALU.mult), ["bbim"] + GPGK, ["bbim"])
                P.barrier()
                phb.close()

                LKre = sb("LKre", [128, 2, 25], F32, ph)
                LKim = sb("LKim", [128, 2, 25], F32, ph)
                lkph = sb("lkph", [128, 2, 25], F32, ph)
                lkkf = sb("lkkf", [128, 2, 25], F32, ph)
                lkki = sb("lkki", [128, 2, 25], I32, ph)
                lkmg = sb("lkmg", [128, 2, 25], F32, ph)
                eph = sb("eph", [128, 2, 32], F32, ph)
                ekf = sb("ekf", [128, 2, 32], F32, ph)
                eki = sb("eki", [128, 2, 32], I32, ph)
                Ere = sb("Ere", [128, 2, 32], F32, ph)
                Eim = sb("Eim", [128, 2, 32], F32, ph)
                WTre = sb("WTre", [128, 2, 128], F32, ph)
                WTim = sb("WTim", [128, 2, 128], F32, ph)
                ATre = sb("ATre", [128, 2, 128], F32, ph)
                ATim = sb("ATim", [128, 2, 128], F32, ph)
                wtmp = sb("wtmp", [128, 2, 144], F32, ph)
                ptmp = sb("ptmp", [128, 2, 128], F32, ph)
                Bfre = sb("Bfre", [128, 2, 144], F32, ph)
                Bfim = sb("Bfim", [128, 2, 144], F32, ph)
                Mtmp = sb("Mtmp", [128, 2, 2, 128], F32, ph)
                Mw2 = [sb("Mw%d" % i, [128, 4, 128], BF16, ph) for i in range(2)]
                W2re2 = [sb("W2re%d" % i, [128, 2, 128], BF16, ph) for i in range(2)]
                W2im2 = [sb("W2im%d" % i, [128, 2, 128], BF16, ph) for i in range(2)]
                W3re2 = [sb("W3re%d" % i, [128, 2, 128], BF16, ph) for i in range(2)]
                W3im2 = [sb("W3im%d" % i, [128, 2, 128], BF16, ph) for i in range(2)]
                Hbre = sb("Hbre", [128, 2, 256], BF16, ph)
                Hbim = sb("Hbim", [128, 2, 256], BF16, ph)
                Sre = sb("Sre", [16, 4, 64], F32, ph)
                Sim = sb("Sim", [16, 4, 64], F32, ph)
                hsre2 = [sb("hsre%d" % i, [128, 2, 16], F32, ph) for i in range(2)]
                hsim2 = [sb("hsim%d" % i, [128, 2, 16], F32, ph) for i in range(2)]
                hbre2 = [sb("hbre%d" % i, [128, 2, 16], BF16, ph) for i in range(2)]
                hbim2 = [sb("hbim%d" % i, [128, 2, 16], BF16, ph) for i in range(2)]
                SNre = sb("SNre", [128, 2, 16], F32, ph)
                SNim = sb("SNim", [128, 2, 16], F32, ph)
                Sore = sb("Sore", [16, 4, 64], F32, ph)
                Soim = sb("Soim", [16, 4, 64], F32, ph)
                zJ = [sb("zJ%d" % i, [128, 8, 128], BF16, ph) for i in range(2)]
                zS = sb("zS", [16, 8, 128], BF16, ph)
                tSs = [sb("tS%d" % i, [128, 256], F32, ph) for i in range(4)]
                tCs = [sb("tC%d" % i, [128, 256], F32, ph) for i in range(4)]
                gtm = [sb("gtm%d" % i, [128, 256], F32, ph) for i in range(2)]
                q1 = sb("q1", [128, 256], F32, ph)
                q2 = sb("q2", [128, 256], F32, ph)
                q3 = sb("q3", [128, 256], F32, ph)
                q4 = sb("q4", [128, 256], F32, ph)
                rre = sb("rre", [128, 256], F32, ph)
                rim = sb("rim", [128, 256], F32, ph)
                Gre = sb("Gre", [128, 256], F32, ph)
                Gim = sb("Gim", [128, 256], F32, ph)
                Po = sb("Po", [32, 256], F32, ph)
                dv(lambda e: e.memset(Hbre[:], 0.0), [], ["Hbre"])
                dv(lambda e: e.memset(Hbim[:], 0.0), [], ["Hbim"])
                kv = cst[:, 0:25]
                jv16 = cst[:, 25:41]

                def prep1(q):
                    r0 = 2 * q
                    dv(lambda e: e.tensor_tensor(out=lkph[:], in0=thr[:, r0:r0 + 2].unsqueeze(2).to_broadcast([128, 2, 25]),
                                                 in1=kv.unsqueeze(1).to_broadcast([128, 2, 25]), op=ALU.mult), ["thr", "cst"], ["lkph"])
                    wrap(lkph[:], "lkph", lkki[:], lkkf[:], "wl")
                    dv(lambda e: e.tensor_tensor(out=lkmg[:], in0=adt[:, r0:r0 + 2].unsqueeze(2).to_broadcast([128, 2, 25]),
                                                 in1=kv.unsqueeze(1).to_broadcast([128, 2, 25]), op=ALU.mult), ["adt", "cst"], ["lkmg"])
                    dv(lambda e: e.tensor_tensor(out=eph[:, :, 0:16], in0=Th8[:, r0:r0 + 2].unsqueeze(2).to_broadcast([128, 2, 16]),
                                                 in1=jv16.unsqueeze(1).to_broadcast([128, 2, 16]), op=ALU.mult), ["Th8", "cst"], ["eph"])
                    dv(lambda e: e.tensor_tensor(out=eph[:, :, 16:32], in0=Th128[:, r0:r0 + 2].unsqueeze(2).to_broadcast([128, 2, 16]),
                                                 in1=jv16.unsqueeze(1).to_broadcast([128, 2, 16]), op=ALU.mult), ["Th128", "cst", "eph"], ["eph"])
                    wrap(eph[:], "eph", eki[:], ekf[:], "we")
                    ac(lambda e: e.activation(out=lkmg[:], in_=lkmg[:], func=AF.Exp), ["lkmg"], ["lkmg"])
                    sincos(LKim[:], LKre[:], lkph[:], "lkph", "LKim", "LKre")
                    sincos(Eim[:], Ere[:], eph[:], "eph", "Eim", "Ere")
                    dv(lambda e: e.tensor_tensor(out=LKre[:], in0=LKre[:], in1=lkmg[:], op=ALU.mult), ["LKre", "lkmg"], ["LKre"])
                    dv(lambda e: e.tensor_tensor(out=LKim[:], in0=LKim[:], in1=lkmg[:], op=ALU.mult), ["LKim", "lkmg"], ["LKim"])

                    def lkb(t, k0, n):
                        return t[:, :, k0:k0 + n].unsqueeze(3).to_broadcast([128, 2, n, 16])

                    def bbb(t, n):
                        return t[:, r0:r0 + 2, :].unsqueeze(2).to_broadcast([128, 2, n, 16])

                    def cplx(opf, tmp, tmpk, ore, oim, k0, n, xre, xim, xk, neg_im=False):
                        o4 = lambda t: t[:, :, 0:n * 16].rearrange("p g (s c) -> p g s c", c=16)
                        Or, Oi, Ot = o4(ore), o4(oim), o4(tmp)
                        lr, li = lkb(LKre, k0, n), lkb(LKim, k0, n)
                        xr, xi = bbb(xre, n), bbb(xim, n)
                        rn, inn = NM[ore.name], NM[oim.name]
                        opf(lambda e: e.tensor_tensor(out=Or, in0=lr, in1=xr, op=ALU.mult), ["LKre"] + xk, [rn])
                        opf(lambda e: e.tensor_tensor(out=Ot, in0=li, in1=xi, op=ALU.mult), ["LKim"] + xk, [tmpk])
                        opf(lambda e: e.tensor_tensor(out=Or, in0=Or, in1=Ot, op=ALU.subtract), [rn, tmpk], [rn])
                        opf(lambda e: e.tensor_tensor(out=Oi, in0=lr, in1=xi, op=ALU.mult), ["LKre"] + xk, [inn])
                        opf(lambda e: e.tensor_tensor(out=Ot, in0=li, in1=xr, op=ALU.mult), ["LKim"] + xk, [tmpk])
                        opf(lambda e: e.tensor_tensor(out=Oi, in0=Oi, in1=Ot, op=ALU.add), [inn, tmpk], [inn])
                        if neg_im:
                            opf(lambda e: e.tensor_scalar(out=Oi, in0=Oi, scalar1=-1.0, scalar2=0.0, op0=ALU.mult, op1=ALU.add), [inn], [inn])

                    cplx(gp, ptmp, "ptmp", WTre, WTim, 0, 8, bbre, bbim, ["bbre", "bbim"])
                    cplx(gp, ptmp, "ptmp", ATre, ATim, 8, 8, bbre, bbim, ["bbre", "bbim"])
                    cplx(gp, wtmp, "wtmp", Bfre, Bfim, 16, 9, cre, cim, ["cre", "cim"], neg_im=True)

                def prep2(q):
                    g0 = q * 4
                    qp = q % 2
                    Mw, W2re, W2im, W3re, W3im = Mw2[qp], W2re2[qp], W2im2[qp], W3re2[qp], W3im2[qp]
                    hsre, hsim, hbre, hbim = hsre2[qp], hsim2[qp], hbre2[qp], hbim2[qp]
                    W = lambda n: (n, qp)
                    for (wt, w2, w2k) in ((WTre, W2re, W("W2re")), (WTim, W2im, W("W2im"))):
                        b = next_bank(0, 4)
                        for pl in range(2):
                            pe(lambda e, b=b, pl=pl, wt=wt: e.transpose(out=PS[b][:, pl * 128:(pl + 1) * 128], in_=wt[:, pl, :], identity=identf[:]),
                               [NM[wt.name], "identf"], [("ps", b)])
                        ac(lambda e, b=b, w2=w2: e.activation(out=w2[:].rearrange("p g n -> p (g n)"), in_=PS[b][:, 0:256], func=AF.Copy),
                           [("ps", b)], [w2k])
                    bm = [next_bank(0, 4), next_bank(0, 4)]
                    for pl in range(2):
                        for g2 in range(2):
                            hs_ = slice(g2 * 64, (g2 + 1) * 64)
                            b = bm[g2]
                            pe(lambda e, b=b, pl=pl, hs_=hs_: e.matmul(PS[b][:, pl * 128:(pl + 1) * 128], lhsT=ATre[hs_, pl, :], rhs=Bfre[hs_, pl, 0:128],
                                                                       start=True, stop=False), ["ATre", "Bfre"], [("ps", b)])
                            pe(lambda e, b=b, pl=pl, hs_=hs_: e.matmul(PS[b][:, pl * 128:(pl + 1) * 128], lhsT=ATim[hs_, pl, :], rhs=Bfim[hs_, pl, 0:128],
                                                                       start=False, stop=True), ["ATim", "Bfim"], [("ps", b)])
                    for g2 in range(2):
                        dv(lambda e, g2=g2: e.tensor_tensor(out=Mtmp[:, g2, :, :], in0=PS[bm[g2]][:, 0:256].rearrange("p (g n) -> p g n", g=2),
                                                            in1=maskM[:].unsqueeze(1).to_broadcast([128, 2, 128]), op=ALU.mult), [("ps", bm[g2]), "maskM"], [("Mtmp", g2)])
                    for pl in range(2):
                        for g2 in range(2):
                            gl = 2 * pl + g2
                            dv(lambda e, gl=gl, pl=pl, g2=g2: e.scalar_tensor_tensor(out=Mw[:, gl, :], in0=identf[:], scalar=gdU[:, g0 + gl:g0 + gl + 1],
                                                                                    in1=Mtmp[:, g2, pl, :], op0=ALU.mult, op1=ALU.add),
                               [("Mtmp", g2), "gdU", "identf"], [W("Mw")])
                    ac(lambda e: e.activation(out=W3re[:], in_=Bfre[:, :, 16:144], func=AF.Copy), ["Bfre"], [W("W3re")])
                    ac(lambda e: e.activation(out=W3im[:], in_=Bfim[:, :, 16:144], func=AF.Copy), ["Bfim"], [W("W3im")])
                    P.op("sp", lambda e: e.dma_start(out=Sre[:], in_=st_re_d[jl, :, g0:g0 + 4, :]), writes=["Sre"], dma=True)
                    P.op("sp", lambda e: e.dma_start(out=Sim[:], in_=st_im_d[jl, :, g0:g0 + 4, :]), writes=["Sim"], dma=True)
                    bs = next_bank(0, 4)
                    for pl in range(2):
                        pe(lambda e, pl=pl, bs=bs: e.transpose(out=PS[bs][:, pl * 16:(pl + 1) * 16], in_=Sre[0:16, 2 * pl:2 * pl + 2, :].rearrange("s g p -> s (g p)"),
                                                               identity=identf[0:16, 0:16]), ["Sre", "identf"], [("ps", bs)])
                        pe(lambda e, pl=pl, bs=bs: e.transpose(out=PS[bs][:, 32 + pl * 16:32 + (pl + 1) * 16], in_=Sim[0:16, 2 * pl:2 * pl + 2, :].rearrange("s g p -> s (g p)"),
                                                               identity=identf[0:16, 0:16]), ["Sim", "identf"], [("ps", bs)])
                    ac(lambda e, bs=bs: e.activation(out=hsre[:].rearrange("p g n -> p (g n)"), in_=PS[bs][:, 0:32], func=AF.Copy), [("ps", bs)], [W("hsre")])
                    ac(lambda e, bs=bs: e.activation(out=hsim[:].rearrange("p g n -> p (g n)"), in_=PS[bs][:, 32:64], func=AF.Copy), [("ps", bs)], [W("hsim")])
                    ac(lambda e: e.activation(out=hbre[:], in_=hsre[:], func=AF.Copy), [W("hsre")], [W("hbre")])
                    ac(lambda e: e.activation(out=hbim[:], in_=hsim[:], func=AF.Copy), [W("hsim")], [W("hbim")])
                    for pl in range(2):
                        ar = Ere[:, pl, 16:32].unsqueeze(2).to_broadcast([128, 16, 16])
                        ai = Eim[:, pl, 16:32].unsqueeze(2).to_broadcast([128, 16, 16])
                        br = Ere[:, pl, 0:16].unsqueeze(1).to_broadcast([128, 16, 16])
                        bi = Eim[:, pl, 0:16].unsqueeze(1).to_broadcast([128, 16, 16])
                        ti = 2 * qp + pl
                        C3 = tCs[ti][:].rearrange("p (a b) -> p a b", b=16)
                        S3 = tSs[ti][:].rearrange("p (a b) -> p a b", b=16)
                        T0 = gtm[0][:].rearrange("p (a b) -> p a b", b=16)
                        T1 = gtm[1][:].rearrange("p (a b) -> p a b", b=16)
                        gp(lambda e, C3=C3, ar=ar, br=br: e.tensor_tensor(out=C3, in0=ar, in1=br, op=ALU.mult), ["Ere"], [("tC", ti)])
                        gp(lambda e, T0=T0, ai=ai, bi=bi: e.tensor_tensor(out=T0, in0=ai, in1=bi, op=ALU.mult), ["Eim"], ["gtm0"])
                        gp(lambda e, C3=C3, T0=T0: e.tensor_tensor(out=C3, in0=C3, in1=T0, op=ALU.subtract), [("tC", ti), "gtm0"], [("tC", ti)])
                        gp(lambda e, S3=S3, ar=ar, bi=bi: e.tensor_tensor(out=S3, in0=ar, in1=bi, op=ALU.mult), ["Ere", "Eim"], [("tS", ti)])
                        gp(lambda e, T1=T1, ai=ai, br=br: e.tensor_tensor(out=T1, in0=ai, in1=br, op=ALU.mult), ["Ere", "Eim"], ["gtm1"])
                        gp(lambda e, S3=S3, T1=T1: e.tensor_tensor(out=S3, in0=S3, in1=T1, op=ALU.add), [("tS", ti), "gtm1"], [("tS", ti)])

                def main(q):
                    g0 = q * 4
                    kt = q // 2
                    qh = q % 2
                    qp = q % 2
                    Mw, W2re, W2im, W3re, W3im = Mw2[qp], W2re2[qp], W2im2[qp], W3re2[qp], W3im2[qp]
                    hsre, hsim, hbre, hbim = hsre2[qp], hsim2[qp], hbre2[qp], hbim2[qp]
                    W = lambda n: (n, qp)
                    for pl in range(2):
                        pr = 2 * q + pl
                        ti = 2 * qp + pl
                        tS, tC = tSs[ti], tCs[ti]
                        tSk, tCk = ("tS", ti), ("tC", ti)
                        bi_re = 4 + next_bank(0, 4)
                        bi_im = 4 + next_bank(0, 4)
                        for g2 in range(2):
                            hs_ = slice(g2 * 64, (g2 + 1) * 64)
                            Ug = Uv[:, kt, qh * 4 + 2 * pl + g2, :]
                            pe(lambda e, pl=pl, Ug=Ug, b=bi_re, hs_=hs_: e.matmul(PS[b][hs_, 0:272], lhsT=W2re[:, pl, hs_], rhs=Ug, start=True, stop=True),
                               [W("W2re"), ("featT", kt)], [("ps", bi_re)])
                            pe(lambda e, pl=pl, Ug=Ug, b=bi_im, hs_=hs_: e.matmul(PS[b][hs_, 0:272], lhsT=W2im[:, pl, hs_], rhs=Ug, start=True, stop=True),
                               [W("W2im"), ("featT", kt)], [("ps", bi_im)])
                        ire = PS[bi_re][:, 0:256]
                        iim = PS[bi_im][:, 0:256]
                        dv(lambda e, ire=ire, tC=tC: e.tensor_tensor(out=q1[:], in0=tC[:], in1=ire, op=ALU.mult), [tCk, ("ps", bi_re)], ["q1"])
                        dv(lambda e, iim=iim, tS=tS: e.tensor_tensor(out=q2[:], in0=tS[:], in1=iim, op=ALU.mult), [tSk, ("ps", bi_im)], ["q2"])
                        dv(lambda e: e.tensor_tensor(out=rre[:], in0=q1[:], in1=q2[:], op=ALU.add), ["q1", "q2"], ["rre"])
                        dv(lambda e, iim=iim, tC=tC: e.tensor_tensor(out=q3[:], in0=tC[:], in1=iim, op=ALU.mult), [tCk, ("ps", bi_im)], ["q3"])
                        dv(lambda e, ire=ire, tS=tS: e.tensor_tensor(out=q4[:], in0=tS[:], in1=ire, op=ALU.mult), [tSk, ("ps", bi_re)], ["q4"])
                        dv(lambda e: e.tensor_tensor(out=rim[:], in0=q3[:], in1=q4[:], op=ALU.subtract), ["q3", "q4"], ["rim"])
                        Rb = Rg[:, pr:pr + 1].to_broadcast([128, 256])
                        dv(lambda e, Rb=Rb: e.tensor_tensor_scan(out=Gre[:], data0=Rb, data1=rre[:], initial=0.0, op0=ALU.mult, op1=ALU.add), ["Rg", "rre"], ["Gre"])
                        dv(lambda e, Rb=Rb: e.tensor_tensor_scan(out=Gim[:], data0=Rb, data1=rim[:], initial=0.0, op0=ALU.mult, op1=ALU.add), ["Rg", "rim"], ["Gim"])
                        dv(lambda e, tC=tC: e.tensor_tensor(out=q1[:], in0=tC[:], in1=Gre[:], op=ALU.mult), [tCk, "Gre"], ["q1"])
                        dv(lambda e, tS=tS: e.tensor_tensor(out=q2[:], in0=tS[:], in1=Gim[:], op=ALU.mult), [tSk, "Gim"], ["q2"])
                        dv(lambda e, pl=pl: e.tensor_tensor(out=Hbre[:, pl, 1:256], in0=q1[:, 0:255], in1=q2[:, 0:255], op=ALU.subtract), ["q1", "q2"], ["Hbre"])
                        dv(lambda e, pr=pr: e.tensor_tensor(out=Pre[:, pr:pr + 1], in0=q1[:, 255:256], in1=q2[:, 255:256], op=ALU.subtract), ["q1", "q2"], ["Pre"])
                        dv(lambda e, tC=tC: e.tensor_tensor(out=q3[:], in0=tC[:], in1=Gim[:], op=ALU.mult), [tCk, "Gim"], ["q3"])
                        dv(lambda e, tS=tS: e.tensor_tensor(out=q4[:], in0=tS[:], in1=Gre[:], op=ALU.mult), [tSk, "Gre"], ["q4"])
                        dv(lambda e, pl=pl: e.tensor_tensor(out=Hbim[:, pl, 1:256], in0=q3[:, 0:255], in1=q4[:, 0:255], op=ALU.add), ["q3", "q4"], ["Hbim"])
                        dv(lambda e, pr=pr: e.tensor_tensor(out=Pim[:, pr:pr + 1], in0=q3[:, 255:256], in1=q4[:, 255:256], op=ALU.add), ["q3", "q4"], ["Pim"])
                        sre = PS[bi_re][:, 256:272]
                        sim = PS[bi_im][:, 256:272]
                        dv(lambda e, pl=pl, pr=pr, sre=sre: e.scalar_tensor_tensor(out=SNre[:, pl, :], in0=hsre[:, pl, :], scalar=Lre[:, pr:pr + 1], in1=sre,
                                                                                   op0=ALU.mult, op1=ALU.add), [W("hsre"), "Lre", ("ps", bi_re)], ["SNre"])
                        dv(lambda e, pl=pl, pr=pr: e.scalar_tensor_tensor(out=SNre[:, pl, :], in0=hsim[:, pl, :], scalar=nLim[:, pr:pr + 1], in1=SNre[:, pl, :],
                                                                          op0=ALU.mult, op1=ALU.add), [W("hsim"), "nLim", "SNre"], ["SNre"])
                        dv(lambda e, pl=pl, pr=pr, sim=sim: e.scalar_tensor_tensor(out=SNim[:, pl, :], in0=hsim[:, pl, :], scalar=Lre[:, pr:pr + 1], in1=sim,
                                                                                   op0=ALU.mult, op1=ALU.add), [W("hsim"), "Lre", ("ps", bi_im)], ["SNim"])
                        dv(lambda e, pl=pl, pr=pr: e.scalar_tensor_tensor(out=SNim[:, pl, :], in0=hsre[:, pl, :], scalar=Lim[:, pr:pr + 1], in1=SNim[:, pl, :],
                                                                          op0=ALU.mult, op1=ALU.add), [W("hsre"), "Lim", "SNim"], ["SNim"])
                    yield None
                    bo = next_bank(0, 4)
                    for pl in range(2):
                        pe(lambda e, pl=pl, bo=bo: e.transpose(out=PS[bo][0:16, pl * 128:(pl + 1) * 128], in_=SNre[:, pl, :], identity=identf[:]),
                           ["SNre", "identf"], [("ps", bo)])
                        pe(lambda e, pl=pl, bo=bo: e.transpose(out=PS[bo][0:16, 256 + pl * 128:256 + (pl + 1) * 128], in_=SNim[:, pl, :], identity=identf[:]),
                           ["SNim", "identf"], [("ps", bo)])
                    ac(lambda e, bo=bo: e.activation(out=Sore[:].rearrange("p g n -> p (g n)"), in_=PS[bo][0:16, 0:256], func=AF.Copy), [("ps", bo)], ["Sore"])
                    ac(lambda e, bo=bo: e.activation(out=Soim[:].rearrange("p g n -> p (g n)"), in_=PS[bo][0:16, 256:512], func=AF.Copy), [("ps", bo)], ["Soim"])
                    P.op("sp", lambda e: e.dma_start(out=sst_re_d[jl, :, g0:g0 + 4, :], in_=Sore[:]), reads=["Sore"], writes=["sst_re_o"], dma=True)
                    P.op("sp", lambda e: e.dma_start(out=sst_im_d[jl, :, g0:g0 + 4, :], in_=Soim[:]), reads=["Soim"], writes=["sst_im_o"], dma=True)
                    for jt in range(2):
                        by = [4 + next_bank(0, 4), 4 + next_bank(0, 4)]
                        for pl in range(2):
                            for g2 in range(2):
                                hs_ = slice(g2 * 64, (g2 + 1) * 64)
                                gl = 2 * pl + g2
                                b = by[g2]
                                o = PS[b][:, pl * 128:(pl + 1) * 128]
                                pe(lambda e, o=o, gl=gl, jt=jt: e.matmul(o, lhsT=Uv[:, kt, qh * 4 + gl, jt * 128:(jt + 1) * 128], rhs=Mw[:, gl, :], start=True, stop=False),
                                   [("featT", kt), W("Mw")], [("ps", b)])
                                pe(lambda e, o=o, pl=pl, jt=jt, hs_=hs_: e.matmul(o, lhsT=Hbre[hs_, pl, jt * 128:(jt + 1) * 128], rhs=W3re[hs_, pl, :], start=False, stop=False),
                                   ["Hbre", W("W3re")], [("ps", b)])
                                pe(lambda e, o=o, pl=pl, jt=jt, hs_=hs_: e.matmul(o, lhsT=Hbim[hs_, pl, jt * 128:(jt + 1) * 128], rhs=W3im[hs_, pl, :], start=False, stop=True),
                                   ["Hbim", W("W3im")], [("ps", b)])
                        for g2 in range(2):
                            ac(lambda e, b=by[g2], jt=jt, g2=g2: e.activation(
                                out=zJ[jt][:, :, qh * 64:(qh + 1) * 64].rearrange("p t (l g c) -> p g l t c", g=2, c=16)[:, g2],
                                in_=PS[b][:, 0:256].rearrange("p (l t c) -> p l t c", l=2, t=8), func=AF.Gelu), [("ps", by[g2])], [("zJ", jt)])
                    bys = [4 + next_bank(0, 4), 4 + next_bank(0, 4)]
                    for pl in range(2):
                        for g2 in range(2):
                            hs_ = slice(g2 * 64, (g2 + 1) * 64)
                            gl = 2 * pl + g2
                            b = bys[g2]
                            o = PS[b][0:16, pl * 128:(pl + 1) * 128]
                            pe(lambda e, o=o, gl=gl: e.matmul(o, lhsT=Uv[:, kt, qh * 4 + gl, 256:272], rhs=Mw[:, gl, :], start=True, stop=False),
                               [("featT", kt), W("Mw")], [("ps", b)])
                            pe(lambda e, o=o, pl=pl, hs_=hs_: e.matmul(o, lhsT=hbre[hs_, pl, :], rhs=W3re[hs_, pl, :], start=False, stop=False), [W("hbre"), W("W3re")], [("ps", b)])
                            pe(lambda e, o=o, pl=pl, hs_=hs_: e.matmul(o, lhsT=hbim[hs_, pl, :], rhs=W3im[hs_, pl, :], start=False, stop=True), [W("hbim"), W("W3im")], [("ps", b)])
                    for g2 in range(2):
                        ac(lambda e, b=bys[g2], g2=g2: e.activation(
                            out=zS[:, :, qh * 64:(qh + 1) * 64].rearrange("p t (l g c) -> p g l t c", g=2, c=16)[:, g2],
                            in_=PS[b][0:16, 0:256].rearrange("p (l t c) -> p l t c", l=2, t=8), func=AF.Gelu), [("ps", bys[g2])], ["zS"])

                    def finish():
                        for jt in range(2):
                            b = next_bank(0, 4)
                            psb = PS[b][:].bitcast(BF16)
                            for t in range(8):
                                pe(lambda e, psb=psb, t=t, jt=jt: e.transpose(out=psb[:, t * 128:(t + 1) * 128], in_=zJ[jt][:, t, :], identity=identb[:]),
                                   [("zJ", jt), "identb"], [("ps", b)])
                            dv(lambda e, psb=psb, jt=jt: e.tensor_copy(
                                out=featT[:, kt, jt * 1024:(jt + 1) * 1024].rearrange("p (j t) -> p j t", t=8),
                                in_=psb.rearrange("p (t j) -> p j t", t=8)), [("ps", b)], [("featT", kt)])
                        b = next_bank(0, 4)
                        psb = PS[b][:].bitcast(BF16)
                        for t in range(8):
                            pe(lambda e, psb=psb, t=t: e.transpose(out=psb[:, t * 16:(t + 1) * 16], in_=zS[0:16, t, :], identity=identb[0:16, 0:16]),
                               ["zS", "identb"], [("ps", b)])
                        dv(lambda e, psb=psb: e.tensor_copy(out=featT[:, kt, 2048:2176].rearrange("p (s t) -> p s t", t=8),
                                                            in_=psb[:, 0:128].rearrange("p (t s) -> p s t", t=8)), [("ps", b)], [("featT", kt)])

                    yield (finish if qh == 1 else None)

                prep1(0)
                prep2(0)
                prep1(1)
                pending = None
                for q in range(16):
                    if pending is not None:
                        pending()
                        pending = None
                    mg = main(q)
                    next(mg)
                    if q + 1 < 16:
                        prep2(q + 1)
                    if q + 2 < 16:
                        prep1(q + 2)
                    fin = next(mg)
                    if fin is not None:
                        pending = fin
                if pending is not None:
                    pending()
                b = next_bank(0, 4)
                pe(lambda e, b=b: e.transpose(out=PS[b][0:32, 0:128], in_=Pre[:], identity=identf[:]), ["Pre", "identf"], [("ps", b)])
                pe(lambda e, b=b: e.transpose(out=PS[b][0:32, 128:256], in_=Pim[:], identity=identf[:]), ["Pim", "identf"], [("ps", b)])
                dv(lambda e, b=b: e.tensor_copy(out=Po[:], in_=PS[b][0:32, 0:256]), [("ps", b)], ["Po"])
                P.op("sp", lambda e: e.dma_start(out=pst_re_d[jl].rearrange("(r t) p -> r (t p)", t=2), in_=Po[:, 0:128]), reads=["Po"], writes=["pst_re_o"], dma=True)
                P.op("sp", lambda e: e.dma_start(out=pst_im_d[jl].rearrange("(r t) p -> r (t p)", t=2), in_=Po[:, 128:256]), reads=["Po"], writes=["pst_im_o"], dma=True)
                P.barrier()
            with ExitStack() as ph:
                wv = sb("wv", [128, 8, 512], BF16, ph)
                wgt = sb("wgt", [128, 8, 512], BF16, ph)
                gt = [sb("gt%d" % i, [128, 512], F32, ph) for i in range(2)]
                it = 0
                for half in range(2):
                    P.op("pool", lambda e, half=half: e.dma_start(
                        out=wv[:], in_=wglu_d[jl, :, half * 512:(half + 1) * 512].rearrange("(k p) c -> p k c", p=128)), writes=["wv"], dma=True)
                    P.op("pool", lambda e, half=half: e.dma_start(
                        out=wgt[:], in_=wglu_d[jl, :, 1024 + half * 512:1024 + (half + 1) * 512].rearrange("(k p) c -> p k c", p=128)), writes=["wgt"], dma=True)
                    for rg in range(17):
                        bv = 2 * (it % 4)
                        bg = bv + 1
                        tp = gt[it % 2]
                        tk = ("gt", it % 2)
                        it += 1
                        for k in range(8):
                            pe(lambda e, k=k, bv=bv, rg=rg: e.matmul(PS[bv][:, :], lhsT=feat_cols(featT, k, rg), rhs=wv[:, k, :], start=(k == 0), stop=(k == 7)),
                               [("featT", k), "wv"], [("ps", bv)])
                        for k in range(8):
                            pe(lambda e, k=k, bg=bg, rg=rg: e.matmul(PS[bg][:, :], lhsT=feat_cols(featT, k, rg), rhs=wgt[:, k, :], start=(k == 0), stop=(k == 7)),
                               [("featT", k), "wgt"], [("ps", bg)])
                        ac(lambda e, bg=bg, tp=tp: e.activation(out=tp[:], in_=PS[bg][:, :], func=AF.Sigmoid), [("ps", bg)], [tk])
                        dv(lambda e, bv=bv, tp=tp: e.tensor_tensor(out=tp[:], in0=tp[:], in1=PS[bv][:, :], op=ALU.mult), [tk, ("ps", bv)], [tk])
                        xk = "XP" if rg < 16 else "XS"
                        xr = Xrow(rg)[:, half * 512:(half + 1) * 512]
                        dv(lambda e, xr=xr, tp=tp: e.tensor_tensor(out=xr, in0=xr, in1=tp[:], op=ALU.add), [tk, xk], [xk])
                P.barrier()

        adbg = cfg.get("adbg", {})

        def attn_layer(L):
            P.handoff = 0.0
            P.cscale = dict(CS_ATT)
            jl = L // 2
            with ExitStack() as ph0:
                alloc_scr(ph0)
                norm(L)
                P.barrier()
            with ExitStack() as ph:
                wqkv = sb("wqkv", [128, 8, 1536], BF16, ph)
                for c in range(3):
                    P.op("pool", lambda e, c=c: e.dma_start(out=wqkv[:, :, c * 512:(c + 1) * 512],
                                                            in_=wqkv_d[jl, :, c * 512:(c + 1) * 512].rearrange("(k p) c -> p k c", p=128)),
                         writes=[("wqkv", c)], dma=True)
                qkvf2 = [sb("qkvf%d" % i, [128, 1536], F32, ph) for i in range(2)]
                gq = sb("gq", [128, 2, 64], F32, ph)
                sq = sb("sq", [128, 20, 64], F32, ph)
                rs2 = [sb("rs%d" % i, [128, 20], F32, ph) for i in range(2)]
                rt = [sb("rt%d" % i, [128, 20, 8], F32, ph) for i in range(4)]
                qkb2 = [sb("qkb%d" % i, [128, 20, 64], BF16, ph) for i in range(2)]
                kdup2 = [sb("kdup%d" % i, [128, 4, 2, 64], BF16, ph) for i in range(2)]
                kdups = [sb("kdups%d" % i, [128, 4, 2, 64], BF16, ph) for i in range(2)]
                qT3 = [sb("qT%d" % i, [128, 8, 128], BF16, ph) for i in range(3)]
                kTd = [sb("kTd%d" % i, [128, 4, 128], BF16, ph) for i in range(3)]
                vaug = [sb("vaug%d" % i, [128, 4, 65], BF16, ph) for i in range(3)]
                kTs = [sb("kTs%d" % i, [128, 4, 128], BF16, ph) for i in range(2)]
                vas = [sb("vas%d" % i, [128, 4, 65], BF16, ph) for i in range(2)]
                ckb = [sb("ckb%d" % i, [128, 4, 64], BF16, ph) for i in range(2)]
                pT = [sb("pT%d" % i, [128, 16, 128], BF16, ph) for i in range(2)]
                SC = PSALL[:, 0:2048]
                PVr = PSALL[:, 2048:3584]
                oacc3 = [sb("oacc%d" % i, [128, 16, 65], F32, ph) for i in range(3)]
                ob = sb("ob", [128, 16, 64], BF16, ph)
                esink = sb("esink", [128, 16], F32, ph)
                den = sb("den", [128, 16], F32, ph)
                P.op("sp", lambda e: e.dma_start(out=gq[:, 0, :], in_=qnorm_d[jl:jl + 1, :].to_broadcast([128, 64])), writes=[("gq", 0)], dma=True)
                P.op("sp", lambda e: e.dma_start(out=gq[:, 1, :], in_=knorm_d[jl:jl + 1, :].to_broadcast([128, 64])), writes=[("gq", 1)], dma=True)
                gqk = [("gq", 0), ("gq", 1)]
                P.op("sp", lambda e: e.dma_start(out=esink[:], in_=sinks_d[jl:jl + 1, :].to_broadcast([128, 16])), writes=["esink"], dma=True)
                ac(lambda e: e.activation(out=esink[:], in_=esink[:], func=AF.Exp), ["esink"], ["esink"])
                for i in range(3):
                    dv(lambda e, i=i: e.memset(vaug[i][:], 1.0), [], [("vaug", i)])
                for i in range(2):
                    dv(lambda e, i=i: e.memset(vas[i][:], 1.0), [], [("vas", i)])
                P.op("sp", lambda e: e.dma_start(out=sck_d[jl, :, 0:120, :], in_=ck_d[jl, :, 8:128, :]), writes=["sck_a"], dma=True)
                P.op("sp", lambda e: e.dma_start(out=scv_d[jl, :, 0:120, :], in_=cv_d[jl, :, 8:128, :]), writes=["scv_a"], dma=True)
                itc = [0]
                sck = [("ps", 0), ("ps", 1), ("ps", 2), ("ps", 3)]
                pvk = [("ps", 4), ("ps", 5), ("ps", 6)]

                def key_block(st_, kT, kTk, va, vak, mask):
                    qT, oacc, qTk, oak = st_["qT"], st_["oacc"], st_["qTk"], st_["oak"]
                    i = itc[0] % 2
                    itc[0] += 1
                    for kv in range(4):
                        for h2 in range(2):
                            off = (h2 * 2 + kv // 2) * 512 + (kv % 2) * 256
                            pe(lambda e, off=off, h2=h2, kv=kv: e.matmul(SC[:, off:off + 256], lhsT=kT[h2 * 64:(h2 + 1) * 64, kv, :],
                                                                         rhs=qT[h2 * 64:(h2 + 1) * 64, 2 * kv:2 * kv + 2, :], start=True, stop=True),
                               [kTk, qTk], sck)
                    ac(lambda e, i=i: e.activation(out=pT[i][:].rearrange("p h q -> p (h q)"), in_=SC, func=AF.Exp, scale=0.125), sck, [("pT", i)])
                    dv(lambda e, i=i: e.tensor_tensor(out=pT[i][:], in0=pT[i][:], in1=mask.unsqueeze(1).to_broadcast([128, 16, 128]), op=ALU.mult),
                       [("pT", i), "masks"], [("pT", i)])
                    for h in range(16):
                        kv, r = h // 4, h % 4
                        slot = (r % 2) * 8 + kv * 2 + r // 2
                        off = (h // 7) * 512 + (h % 7) * 65
                        pe(lambda e, off=off, slot=slot, kv=kv, i=i: e.matmul(PVr[:, off:off + 65], lhsT=pT[i][:, slot, :], rhs=va[:, kv, :], start=True, stop=True),
                           [("pT", i), vak], pvk)
                    pv14 = PVr[:, 0:1024].rearrange("p (b n) -> p b n", b=2)[:, :, 0:455].rearrange("p b (h d) -> p b h d", d=65)
                    pv2 = PVr[:, 1024:1154].rearrange("p (h d) -> p h d", d=65)
                    o14 = oacc[:, 0:14, :].rearrange("p (b h) d -> p b h d", b=2)
                    o2 = oacc[:, 14:16, :]
                    if st_["first"]:
                        st_["first"] = False
                        dv(lambda e: e.tensor_copy(out=o14, in_=pv14), pvk, [oak])
                        dv(lambda e: e.tensor_copy(out=o2, in_=pv2), pvk, [oak])
                    else:
                        dv(lambda e: e.tensor_tensor(out=o14, in0=o14, in1=pv14, op=ALU.add), pvk + [oak], [oak])
                        dv(lambda e: e.tensor_tensor(out=o2, in0=o2, in1=pv2, op=ALU.add), pvk + [oak], [oak])

                def block(blk):
                    par = blk % 2
                    p3 = blk % 3
                    q3 = (blk - 1) % 3
                    cols = slice(blk * 128, (blk + 1) * 128)
                    qkvf, rs, qkb, kdup = qkvf2[par], rs2[par], qkb2[par], kdup2[par]
                    qT, oacc = qT3[p3], oacc3[p3]
                    K_ = lambda n: (n, par)
                    st_ = dict(qT=qT, oacc=oacc, qTk=("qT", p3), oak=("oacc", p3), first=True)
                    fbk = ("fb", blk)
                    qf = [("qkvf", c, par) for c in range(3)]
                    for c in range(3):
                        for k in range(8):
                            pe(lambda e, c=c, k=k: e.matmul(PS[7][:, :], lhsT=featT[:, k, cols], rhs=wqkv[:, k, c * 512:(c + 1) * 512], start=(k == 0), stop=(k == 7)),
                               [fbk, ("wqkv", c)], [("ps", 7)])
                        ac(lambda e, c=c: e.activation(out=qkvf[:, c * 512:(c + 1) * 512], in_=PS[7][:, :], func=AF.Copy), [("ps", 7)], [("qkvf", c, par)])
                    yield
                    qk3 = qkvf[:, 0:1280].rearrange("p (h d) -> p h d", d=64)
                    ac(lambda e: e.activation(out=sq[:], in_=qk3, func=AF.Square), qf, ["sq"])
                    dv(lambda e: e.tensor_reduce(out=rs[:], in_=sq[:], axis=AX.X, op=ALU.add), ["sq"], [K_("rs")])
                    dv(lambda e: e.tensor_scalar(out=rs[:], in0=rs[:], scalar1=1.0 / 64, scalar2=EPS, op0=ALU.mult, op1=ALU.add), [K_("rs")], [K_("rs")])
                    ac(lambda e: e.activation(out=rs[:], in_=rs[:], func=AF.Ln), [K_("rs")], [K_("rs")])
                    ac(lambda e: e.activation(out=rs[:], in_=rs[:], func=AF.Exp, scale=-0.5), [K_("rs")], [K_("rs")])
                    dv(lambda e: e.tensor_tensor(out=qk3, in0=qk3, in1=rs[:].unsqueeze(2).to_broadcast([128, 20, 64]), op=ALU.mult), qf + [K_("rs")], qf)
                    dv(lambda e: e.tensor_tensor(out=qk3[:, 0:16, :], in0=qk3[:, 0:16, :], in1=gq[:, 0:1, :].to_broadcast([128, 16, 64]), op=ALU.mult),
                       qf + gqk, qf)
                    dv(lambda e: e.tensor_tensor(out=qk3[:, 16:20, :], in0=qk3[:, 16:20, :], in1=gq[:, 1:2, :].to_broadcast([128, 4, 64]), op=ALU.mult),
                       qf + gqk, qf)
                    cb = rope[:, blk, 0:8].unsqueeze(1).to_broadcast([128, 20, 8])
                    sbb = rope[:, blk, 8:16].unsqueeze(1).to_broadcast([128, 20, 8])
                    x1 = qk3[:, :, 0:8]
                    x2 = qk3[:, :, 8:16]
                    gp(lambda e: e.tensor_tensor(out=rt[0][:], in0=x1, in1=cb, op=ALU.mult), qf + ["rope"], ["rt0"])
                    gp(lambda e: e.tensor_tensor(out=rt[1][:], in0=x2, in1=sbb, op=ALU.mult), qf + ["rope"], ["rt1"])
                    gp(lambda e: e.tensor_tensor(out=rt[2][:], in0=x2, in1=cb, op=ALU.mult), qf + ["rope"], ["rt2"])
                    gp(lambda e: e.tensor_tensor(out=rt[3][:], in0=x1, in1=sbb, op=ALU.mult), qf + ["rope"], ["rt3"])
                    gp(lambda e: e.tensor_tensor(out=x1, in0=rt[0][:], in1=rt[1][:], op=ALU.subtract), ["rt0", "rt1"] + qf, qf)
                    gp(lambda e: e.tensor_tensor(out=x2, in0=rt[2][:], in1=rt[3][:], op=ALU.add), ["rt2", "rt3"] + qf, qf)
                    ac(lambda e: e.activation(out=qkb[:], in_=qk3, func=AF.Copy), qf, [K_("qkb")])
                    gp(lambda e: e.tensor_copy(out=kdup[:], in_=qkb[:, 16:20, :].unsqueeze(2).to_broadcast([128, 4, 2, 64])), [K_("qkb")], [K_("kdup")])
                    if blk == 15:
                        P.op("sp", lambda e: e.dma_start(out=pck_d[jl], in_=qkvf[:, 1024:1280]), reads=qf, writes=["pck_o"], dma=True)
                        P.op("sp", lambda e: e.dma_start(out=pcv_d[jl], in_=qkvf[:, 1280:1536]), reads=qf, writes=["pcv_o"], dma=True)
                    if blk == 16:
                        for s_ in range(16):
                            P.op("sp", lambda e, s_=s_: e.dma_start(out=sck_d[jl, s_, 120:128, :], in_=qkvf[s_ * 8:(s_ + 1) * 8, 1024:1280]),
                                 reads=qf, writes=[("sck_b", s_ % 4)], dma=True, dkey="sck_b_dk")
                            P.op("sp", lambda e, s_=s_: e.dma_start(out=scv_d[jl, s_, 120:128, :], in_=qkvf[s_ * 8:(s_ + 1) * 8, 1280:1536]),
                                 reads=qf, writes=[("scv_b", s_ % 4)], dma=True, dkey="scv_b_dk")
                    yield
                    dv(lambda e: e.tensor_copy(out=vaug[p3][:, :, 0:64], in_=qkvf[:, 1280:1536].rearrange("p (h d) -> p h d", d=64)), qf, [("vaug", p3)])
                    psb = PS[7][:].bitcast(BF16)
                    for hp in range(8):
                        pe(lambda e, hp=hp: e.transpose(out=psb[:, hp * 128:(hp + 1) * 128], in_=qkb[:, 2 * hp:2 * hp + 2, :].rearrange("p h d -> p (h d)"),
                                                        identity=identb[:]), [K_("qkb"), "identb"], [("ps", 7)])
                    ac(lambda e: e.activation(out=qT[:].rearrange("p h q -> p (h q)"), in_=psb, func=AF.Copy), [("ps", 7)], [("qT", p3)])
                    for kv in range(4):
                        pe(lambda e, kv=kv: e.transpose(out=psb[:, kv * 128:(kv + 1) * 128], in_=kdup[:, kv, :, :].rearrange("p a d -> p (a d)"),
                                                        identity=identb[:]), [K_("kdup"), "identb"], [("ps", 7)])
                    ac(lambda e: e.activation(out=kTd[p3][:].rearrange("p h q -> p (h q)"), in_=psb[:, 0:512], func=AF.Copy), [("ps", 7)], [("kTd", p3)])
                    yield
                    if blk < 16:
                        if blk > 0:
                            key_block(st_, kTd[q3], ("kTd", q3), vaug[q3], ("vaug", q3), msk[:, 128:256])
                    else:
                        for s_ in range(16):
                            i = s_ % 2
                            P.op("pool", lambda e, s_=s_, i=i: e.dma_start(out=ckb[i][:].rearrange("p h d -> p (h d)"), in_=ck_d[jl, s_]), writes=[("ckb", i)], dma=True)
                            P.op("pool", lambda e, s_=s_, i=i: e.dma_start(out=vas[i][:, :, 0:64], in_=cv_d[jl, s_].rearrange("p (h d) -> p h d", d=64)),
                                 writes=[("vas", i)], dma=True)
                            gp(lambda e, i=i: e.tensor_copy(out=kdups[i][:], in_=ckb[i][:].unsqueeze(2).to_broadcast([128, 4, 2, 64])), [("ckb", i)], [("kdups", i)])
                            for kv in range(4):
                                pe(lambda e, kv=kv, i=i: e.transpose(out=psb[:, kv * 128:(kv + 1) * 128], in_=kdups[i][:, kv, :, :].rearrange("p a d -> p (a d)"),
                                                                     identity=identb[:]), [("kdups", i), "identb"], [("ps", 7)])
                            ac(lambda e, i=i: e.activation(out=kTs[i][:].rearrange("p h q -> p (h q)"), in_=psb[:, 0:512], func=AF.Copy),
                               [("ps", 7)], [("kTs", i)])
                            key_block(st_, kTs[i], ("kTs", i), vas[i], ("vas", i), mskS[:, s_, :])
                    yield
                    key_block(st_, kTd[p3], ("kTd", p3), vaug[p3], ("vaug", p3), msk[:, 0:128] if blk < 16 else msk[:, 256:384])
                    yield
                    oak = ("oacc", p3)
                    dv(lambda e: e.tensor_tensor(out=den[:], in0=oacc[:, :, 64], in1=esink[:], op=ALU.add), [oak, "esink"], ["den"])
                    dv(lambda e: e.reciprocal(out=den[:], in_=den[:]), ["den"], ["den"])
                    dv(lambda e: e.tensor_tensor(out=ob[:], in0=oacc[:, :, 0:64], in1=den[:].unsqueeze(2).to_broadcast([128, 16, 64]), op=ALU.mult),
                       [oak, "den"], ["ob"])
                    for k in range(8):
                        pe(lambda e, k=k: e.transpose(out=psb[:, k * 128:(k + 1) * 128], in_=ob[:, 2 * k:2 * k + 2, :].rearrange("p h d -> p (h d)"),
                                                      identity=identb[:]), ["ob", "identb"], [("ps", 7)])
                    dv(lambda e: e.tensor_copy(out=featT[:, :, cols], in_=psb.rearrange("p (k n) -> p k n", k=8)), [("ps", 7)], [fbk])

                nb = 17
                gens = [block(b_) for b_ in range(nb)]
                NST = 6
                for it_ in range(nb + NST - 1):
                    for s_ in range(NST - 1, -1, -1):
                        b_ = it_ - s_
                        if 0 <= b_ < nb:
                            try:
                                next(gens[b_])
                            except StopIteration:
                                pass
                P.barrier()
            with ExitStack() as ph:
                wo = sb("wo", [128, 8, D], BF16, ph)
                for half in range(2):
                    P.op("pool", lambda e, half=half: e.dma_start(out=wo[:, :, half * 512:(half + 1) * 512],
                                                                  in_=wo_d[jl, :, half * 512:(half + 1) * 512].rearrange("(k p) c -> p k c", p=128)),
                         writes=[("wo", half)], dma=True)
                for rg in range(17):
                    for half in range(2):
                        b = 4 + next_bank(0, 4)
                        for k in range(8):
                            pe(lambda e, k=k, b=b, rg=rg, half=half: e.matmul(PS[b][:, :], lhsT=feat_cols(featT, k, rg), rhs=wo[:, k, half * 512:(half + 1) * 512],
                                                                              start=(k == 0), stop=(k == 7)), [("featT", k), ("wo", half)], [("ps", b)])
                        xk = "XP" if rg < 16 else "XS"
                        xr = Xrow(rg)[:, half * 512:(half + 1) * 512]
                        dv(lambda e, xr=xr, b=b: e.tensor_tensor(out=xr, in0=xr, in1=PS[b][:, :], op=ALU.add), [("ps", b), xk], [xk])
                P.barrier()

        mix_layers = cfg.get("mix_layers", list(range(nlayers)) if do_mix else [])
        ffn_layers = cfg.get("ffn_layers", list(range(nlayers)) if do_ffn else [])
        for L in range(DEPTH):
            if L in mix_layers and L % 2 == 0:
                s5_layer(L)
            if L in mix_layers and L % 2 == 1:
                attn_layer(L)
            if L in ffn_layers:
                ffn(L)

        P.op("sp", lambda e: e.dma_start(out=yp_d.rearrange("(a j t) d -> j a (t d)", a=2, t=8),
                                         in_=XP[:].rearrange("p a t d -> p a (t d)")),
             reads=["XP"], writes=["yp_out"], dma=True)
        P.op("sp", lambda e: e.dma_start(out=ys_d, in_=XS[:]), reads=["XS"], writes=["ys_out"], dma=True)
        P.barrier()
        P.emit()
    return nc


def make_consts():
    c = np.zeros((128, 554), np.float32)
    kvec = np.array([7 - s for s in range(8)] + [-s for s in range(8)] + list(range(9)), np.float32)
    c[:, 0:25] = kvec[None, :]
    c[:, 25:281] = np.arange(256, dtype=np.float32)[None, :]
    p = np.arange(128)
    c[:, 281:409] = ((p[None, :] // 16) >= (p[:, None] // 16)).astype(np.float32)
    c[:, 409] = 1.0
    c[:, 410:538] = ((p[:, None] % 8) == (p[None, :] // 16)).astype(np.float32)
    c[:, 538:554] = ((p[:, None] // 8) == np.arange(16)[None, :]).astype(np.float32)
    return c


SHARED = ["attn_w_qkv", "attn_q_norm", "attn_k_norm", "attn_sinks", "attn_w_o", "norm_mix", "norm_ffn", "ffn_w_gate_up", "ffn_w_down", "ssm_a_re", "ssm_a_im", "ssm_log_dt", "ssm_b_re", "ssm_b_im",
          "ssm_c_re", "ssm_c_im", "ssm_d", "ssm_w_glu"]
_CONST = {}


def core_inputs(inputs, c, shared=None):
    if shared is None:
        shared = {k: np.ascontiguousarray(np.asarray(inputs[k], dtype=np.float32)) for k in SHARED}
    if "ident" not in _CONST:
        _CONST["ident"] = np.eye(128, dtype=np.float32)
        _CONST["cst"] = make_consts()
        p = np.arange(128)
        pos = np.concatenate([np.arange(2048, dtype=np.float32).reshape(16, 128), (8192 + (p % 8)).astype(np.float32)[None, :]], 0)
        inv = (np.float32(500000.0) ** (-np.arange(8, dtype=np.float32) * np.float32(2.0) / np.float32(16))).astype(np.float32)
        ang = (pos[:, :, None] * inv[None, None, :]).astype(np.float32)
        rp = np.concatenate([np.cos(ang), np.sin(ang)], -1).astype(np.float32)
        _CONST["rope"] = np.ascontiguousarray(rp.transpose(1, 0, 2))
        mk = np.zeros((128, 384 + 2048), np.float32)
        mk[:, 0:128] = (p[:, None] <= p[None, :])
        mk[:, 128:256] = (p[:, None] > p[None, :])
        mk[:, 256:384] = ((p[:, None] // 8) == (p[None, :] // 8)) & ((p[:, None] % 8) <= (p[None, :] % 8))
        for s_ in range(16):
            mk[:, 384 + s_ * 128:384 + (s_ + 1) * 128] = ((p[None, :] // 8) == s_) & (p[:, None] > (p[None, :] % 8))
        _CONST["msk"] = mk
    m = dict(shared)
    m.update(_CONST)
    m["xp"] = np.ascontiguousarray(np.asarray(inputs["x_prompt"][c], dtype=np.float32))
    m["xs"] = np.ascontiguousarray(np.asarray(inputs["x_sample"][16 * c:16 * (c + 1)], dtype=np.float32).reshape(128, D))
    m["st_re"] = np.ascontiguousarray(np.asarray(inputs["state_ssm_re"][:, 16 * c:16 * (c + 1)], dtype=np.float32))
    m["st_im"] = np.ascontiguousarray(np.asarray(inputs["state_ssm_im"][:, 16 * c:16 * (c + 1)], dtype=np.float32))
    m["ck"] = np.ascontiguousarray(np.asarray(inputs["cache_swa_k"][:, 16 * c:16 * (c + 1)], dtype=np.float32).reshape(2, 16, 128, 256))
    m["cv"] = np.ascontiguousarray(np.asarray(inputs["cache_swa_v"][:, 16 * c:16 * (c + 1)], dtype=np.float32).reshape(2, 16, 128, 256))
    return m


def kernel(**inputs):
    ncores = 8
    nc = build()
    shared = {k: np.ascontiguousarray(np.asarray(inputs[k], dtype=np.float32)) for k in SHARED}
    in_maps = [core_inputs(inputs, c, shared) for c in range(ncores)]
    res = run_bass_kernel_spmd(nc, in_maps, core_ids=list(range(ncores)))
    R = res.results
    yp = np.stack([R[c]["yp"] for c in range(ncores)], 0)
    ys = np.concatenate([R[c]["ys"].reshape(16, 8, D) for c in range(ncores)], 0)
    p_re = np.stack([R[c]["pst_re"] for c in range(ncores)], 1)
    p_im = np.stack([R[c]["pst_im"] for c in range(ncores)], 1)
    s_re = np.concatenate([R[c]["sst_re"] for c in range(ncores)], 1)
    s_im = np.concatenate([R[c]["sst_im"] for c in range(ncores)], 1)
    p_k = np.stack([R[c]["pck"].reshape(2, 128, 4, 64) for c in range(ncores)], 1)
    p_v = np.stack([R[c]["pcv"].reshape(2, 128, 4, 64) for c in range(ncores)], 1)
    s_k = np.concatenate([R[c]["sck"].reshape(2, 16, 128, 4, 64) for c in range(ncores)], 1)
    s_v = np.concatenate([R[c]["scv"].reshape(2, 16, 128, 4, 64) for c in range(ncores)], 1)
    return yp, ys, p_re, p_im, p_k, p_v, s_re, s_im, s_k, s_v
```
